# Optimizing a Trainium2 kernel written in Bass

```python
import math
import jax, jax.numpy as jnp
from jax import lax
import numpy as np

D_MODEL = 2048
BATCH = 4
SEQ = 4096
DEPTH = 4

ATTN_HEADS = 16
ATTN_KV_HEADS = 4
HEAD_DIM = 64
WINDOW = 128
ATTN_WIDTH = ATTN_HEADS * HEAD_DIM
KV_WIDTH = ATTN_KV_HEADS * HEAD_DIM
D_INNER = D_MODEL
SSM_HEAD_DIM = 64
SSM_HEADS = D_INNER // SSM_HEAD_DIM
SSM_GROUPS = 4
D_STATE = 128
CONV_WIDTH = 4
CHUNK = 128
CONV_CH = D_INNER + 2 * SSM_GROUPS * D_STATE
D_FF = 256 * ((8 * D_MODEL // 3 + 255) // 256)
N_MOD = 9
NORM_EPS = 1e-6
P_IN = ATTN_WIDTH + 2 * KV_WIDTH + D_INNER + CONV_CH + SSM_HEADS + 2 * D_MODEL

kernel_name = "hybrid_swa_ssd_macaron_block"


def _rms_normalize(x):
    xf = x.astype(jnp.float32)
    return xf * lax.rsqrt(jnp.mean(xf * xf, axis=-1, keepdims=True) + NORM_EPS)


def rms_norm(x, g):
    return (_rms_normalize(x) * g.astype(jnp.float32)).astype(x.dtype)


def modulate(n, shift, scale):
    return n * (1.0 + scale) + shift


def swiglu(h, w_gate, w_up, w_down):
    return (jax.nn.silu(h @ w_gate) * (h @ w_up)) @ w_down


def alibi_slopes(n_heads):
    return jnp.exp2(-8.0 * jnp.arange(1, n_heads + 1, dtype=jnp.float32) / n_heads)


def sliding_window_attention(q, k, v, sinks):
    b, L = q.shape[0], q.shape[1]
    nb = L // WINDOW
    grp = ATTN_HEADS // ATTN_KV_HEADS
    qb = q.reshape(b, nb, WINDOW, ATTN_KV_HEADS, grp, HEAD_DIM)

    def band(t):
        pad = jnp.zeros((b, WINDOW, ATTN_KV_HEADS, HEAD_DIM), t.dtype)
        tb = jnp.concatenate([pad, t], axis=1).reshape(b, nb + 1, WINDOW, ATTN_KV_HEADS, HEAD_DIM)
        return jnp.concatenate([tb[:, :-1], tb[:, 1:]], axis=2)

    kb, vb = band(k), band(v)
    s = jnp.einsum('bnqkgd,bnskd->bnkgqs', qb, kb,
                   preferred_element_type=jnp.float32) * (HEAD_DIM ** -0.5)
    qpos = jnp.arange(WINDOW)[:, None]
    kpos = jnp.arange(2 * WINDOW)[None, :]
    dist = WINDOW + qpos - kpos
    abs_k = (jnp.arange(nb)[:, None, None] - 1) * WINDOW + kpos[None]
    valid = (dist >= 0)[None] & (dist < WINDOW)[None] & (abs_k >= 0)
    slopes = alibi_slopes(ATTN_HEADS).reshape(ATTN_KV_HEADS, grp)
    s = s - slopes[:, :, None, None] * dist.astype(jnp.float32)
    s = jnp.where(valid[None, :, None, None], s, -jnp.inf)
    sink = sinks.astype(jnp.float32).reshape(ATTN_KV_HEADS, grp)[:, :, None, None]
    m = jnp.maximum(jnp.max(s, axis=-1, keepdims=True), sink)
    e = jnp.exp(s - m)
    p = (e / (jnp.sum(e, axis=-1, keepdims=True) + jnp.exp(sink - m))).astype(v.dtype)
    o = jnp.einsum('bnkgqs,bnskd->bnqkgd', p, vb)
    return o.reshape(b, L, ATTN_WIDTH)


def causal_depthwise_conv(u, w, bias):
    ch = u.shape[-1]
    out = lax.conv_general_dilated(u, w[:, None, :].astype(u.dtype), window_strides=(1,),
                                   padding=[(CONV_WIDTH - 1, 0)],
                                   dimension_numbers=('NWC', 'WIO', 'NWC'),
                                   feature_group_count=ch)
    return out + bias


def ssd_chunked(x, dt, a, bmat, cmat):
    b, L = x.shape[0], x.shape[1]
    nc = L // CHUNK
    R = SSM_HEADS // SSM_GROUPS
    xc = x.astype(jnp.float32).reshape(b, nc, CHUNK, SSM_GROUPS, R, SSM_HEAD_DIM)
    bc = bmat.astype(jnp.float32).reshape(b, nc, CHUNK, SSM_GROUPS, D_STATE)
    cc = cmat.astype(jnp.float32).reshape(b, nc, CHUNK, SSM_GROUPS, D_STATE)
    dtc = dt.reshape(b, nc, CHUNK, SSM_GROUPS, R)
    a_cs = jnp.cumsum(dtc * a.reshape(SSM_GROUPS, R), axis=2)
    xdt = xc * dtc[..., None]

    acs_t = jnp.moveaxis(a_cs, 2, -1)
    causal = jnp.tril(jnp.ones((CHUNK, CHUNK), dtype=bool))
    decay = jnp.exp(jnp.where(causal, acs_t[..., :, None] - acs_t[..., None, :], -jnp.inf))
    cb = jnp.einsum('bclgn,bcsgn->bcgls', cc, bc)
    y_diag = jnp.einsum('bcgrls,bcsgrp->bclgrp', cb[:, :, :, None] * decay, xdt)

    decay_states = jnp.exp(a_cs[:, :, -1:] - a_cs)
    states = jnp.einsum('bclgn,bclgrp->bcgrpn', bc, xdt * decay_states[..., None])
    chunk_decay = jnp.exp(a_cs[:, :, -1])

    def step(carry, inp):
        st, dec = inp
        return carry * dec[..., None, None] + st, carry

    init = jnp.zeros((b, SSM_GROUPS, R, SSM_HEAD_DIM, D_STATE), jnp.float32)
    _, prev = lax.scan(step, init, (jnp.swapaxes(states, 0, 1), jnp.swapaxes(chunk_decay, 0, 1)))
    prev = jnp.swapaxes(prev, 0, 1)

    y_off = jnp.einsum('bclgn,bcgrpn->bclgrp', cc, prev) * jnp.exp(a_cs)[..., None]
    return (y_diag + y_off).reshape(b, L, SSM_HEADS, SSM_HEAD_DIM)


def token_mixer(h, w_in, sinks, conv_w, conv_b, dt_bias, a_log, d_skip, ssm_norm,
                w_attn_out, w_ssm_out, w_out):
    b, L, _ = h.shape
    splits = np.cumsum([ATTN_WIDTH, KV_WIDTH, KV_WIDTH, D_INNER, CONV_CH, SSM_HEADS, D_MODEL]).tolist()
    q, k, v, z, xbc, dt_raw, g_attn, g_ssd = jnp.split(h @ w_in, splits, axis=-1)

    y_a = sliding_window_attention(q.reshape(b, L, ATTN_HEADS, HEAD_DIM),
                                   k.reshape(b, L, ATTN_KV_HEADS, HEAD_DIM),
                                   v.reshape(b, L, ATTN_KV_HEADS, HEAD_DIM), sinks) @ w_attn_out

    xbc = jax.nn.silu(causal_depthwise_conv(xbc, conv_w, conv_b))
    xs, bm, cm = jnp.split(xbc, [D_INNER, D_INNER + SSM_GROUPS * D_STATE], axis=-1)
    xs = xs.reshape(b, L, SSM_HEADS, SSM_HEAD_DIM)
    dt = jax.nn.softplus(dt_raw.astype(jnp.float32) + dt_bias.astype(jnp.float32))
    a = -jnp.exp(a_log.astype(jnp.float32))
    y = ssd_chunked(xs, dt, a, bm.reshape(b, L, SSM_GROUPS, D_STATE),
                    cm.reshape(b, L, SSM_GROUPS, D_STATE))
    y = y + d_skip.astype(jnp.float32)[:, None] * xs.astype(jnp.float32)
    y = y.reshape(b, L, D_INNER) * jax.nn.silu(z.astype(jnp.float32))
    y = _rms_normalize(y.reshape(b, L, SSM_GROUPS, D_INNER // SSM_GROUPS)).reshape(b, L, D_INNER)
    y_s = (y * ssm_norm.astype(jnp.float32)).astype(h.dtype) @ w_ssm_out

    merged = jax.nn.sigmoid(g_attn) * y_a + jax.nn.sigmoid(g_ssd) * y_s
    return merged @ w_out


def setup_inputs(seed: int = 0) -> dict:
    key = jax.random.key(seed)
    ks = jax.random.split(key, 20)
    nrm = jax.random.normal
    f32 = jnp.float32
    dt0 = jnp.exp(jax.random.uniform(ks[13], (DEPTH, SSM_HEADS), f32,
                                     minval=math.log(1e-3), maxval=math.log(1e-1)))
    return {
        "x": nrm(ks[0], (BATCH, SEQ, D_MODEL), f32),
        "c": nrm(ks[1], (BATCH, D_MODEL), f32),
        "w_mod": nrm(ks[2], (DEPTH, D_MODEL, N_MOD * D_MODEL), f32) * D_MODEL ** -0.5,
        "b_mod": 0.02 * nrm(ks[3], (DEPTH, N_MOD * D_MODEL), f32),
        "norm_pre": 1.0 + 0.05 * nrm(ks[4], (DEPTH, 3, D_MODEL), f32),
        "norm_post": 1.0 + 0.05 * nrm(ks[5], (DEPTH, 3, D_MODEL), f32),
        "w_ffn_gate": nrm(ks[6], (DEPTH, 2, D_MODEL, D_FF), f32) * D_MODEL ** -0.5,
        "w_ffn_up": nrm(ks[7], (DEPTH, 2, D_MODEL, D_FF), f32) * D_MODEL ** -0.5,
        "w_ffn_down": nrm(ks[8], (DEPTH, 2, D_FF, D_MODEL), f32) * D_FF ** -0.5,
        "w_in": nrm(ks[9], (DEPTH, D_MODEL, P_IN), f32) * D_MODEL ** -0.5,
        "attn_sinks": 0.5 * nrm(ks[10], (DEPTH, ATTN_HEADS), f32),
        "conv_w": nrm(ks[11], (DEPTH, CONV_WIDTH, CONV_CH), f32) * CONV_WIDTH ** -0.5,
        "conv_b": 0.02 * nrm(ks[12], (DEPTH, CONV_CH), f32),
        "dt_bias": dt0 + jnp.log(-jnp.expm1(-dt0)),
        "a_log": jnp.log(jax.random.uniform(ks[14], (DEPTH, SSM_HEADS), f32, minval=1.0, maxval=16.0)),
        "d_skip": 1.0 + 0.1 * nrm(ks[15], (DEPTH, SSM_HEADS), f32),
        "ssm_norm": 1.0 + 0.05 * nrm(ks[16], (DEPTH, D_INNER), f32),
        "w_attn_out": nrm(ks[17], (DEPTH, ATTN_WIDTH, D_MODEL), f32) * ATTN_WIDTH ** -0.5,
        "w_ssm_out": nrm(ks[18], (DEPTH, D_INNER, D_MODEL), f32) * D_INNER ** -0.5,
        "w_out": nrm(ks[19], (DEPTH, D_MODEL, D_MODEL), f32) * D_MODEL ** -0.5,
    }


def reference(x, c, w_mod, b_mod, norm_pre, norm_post, w_ffn_gate, w_ffn_up, w_ffn_down,
              w_in, attn_sinks, conv_w, conv_b, dt_bias, a_log, d_skip, ssm_norm,
              w_attn_out, w_ssm_out, w_out):
    b = x.shape[0]
    c_act = jax.nn.silu(c)
    for l in range(DEPTH):
        mod = (c_act @ w_mod[l] + b_mod[l]).reshape(b, 1, N_MOD, D_MODEL)
        sh1, sc1, g1, sh2, sc2, g2, sh3, sc3, g3 = [mod[:, :, i] for i in range(N_MOD)]

        n = modulate(rms_norm(x, norm_pre[l, 0]), sh1, sc1)
        f = swiglu(n, w_ffn_gate[l, 0], w_ffn_up[l, 0], w_ffn_down[l, 0])
        x = x + 0.5 * g1 * rms_norm(f, norm_post[l, 0])

        n = modulate(rms_norm(x, norm_pre[l, 1]), sh2, sc2)
        t = token_mixer(n, w_in[l], attn_sinks[l], conv_w[l], conv_b[l], dt_bias[l], a_log[l],
                        d_skip[l], ssm_norm[l], w_attn_out[l], w_ssm_out[l], w_out[l])
        x = x + g2 * rms_norm(t, norm_post[l, 1])

        n = modulate(rms_norm(x, norm_pre[l, 2]), sh3, sc3)
        f = swiglu(n, w_ffn_gate[l, 1], w_ffn_up[l, 1], w_ffn_down[l, 1])
        x = x + 0.5 * g3 * rms_norm(f, norm_post[l, 2])
    return x
```

```python
import numpy as np
import concourse.bass as bass
import concourse.mybir as mybir
from concourse.bass_utils import run_bass_kernel_spmd

F32 = mybir.dt.float32
BF16 = mybir.dt.bfloat16
AF = mybir.ActivationFunctionType
ALU = mybir.AluOpType
AX = mybir.AxisListType

D = 2048
C = 16
DFF = 5632
JF = 44
NT = 512
P_IN = 10784
EPS = 1e-6
NEG = -30000.0


class Buf:
    __slots__ = ("name", "w", "r", "dsem", "dcnt", "excl")

    def __init__(self, name):
        self.name = name
        self.excl = False
        self.w = None
        self.r = []
        self.dsem = None
        self.dcnt = 0


class KB:
    def __init__(self):
        nc = bass.Bass("TRN2", target_bir_lowering=False)
        self.nc = nc
        self.E = dict(pe=nc.tensor, act=nc.scalar, dve=nc.vector, pool=nc.gpsimd, sp=nc.sync)
        self.esem = {e: nc.alloc_semaphore("es_" + e) for e in ("pe", "act", "dve", "pool")}
        self.ecnt = {e: 0 for e in self.esem}
        self.known = {e: {} for e in self.E}
        self.out_events = []
        self.nbuf = 0

    def buf(self, name=None):
        self.nbuf += 1
        return Buf(name or f"b{self.nbuf}")

    def bufs(self, n, name="b"):
        return [self.buf(f"{name}{i}") for i in range(n)]

    def _waits(self, e, reads, writes):
        deps = {}

        def add(ev):
            if ev is None:
                return
            s, v = ev
            k = id(s)
            if k not in deps or deps[k][1] < v:
                deps[k] = (s, v)

        for b in reads:
            add(b.w)
            if b.excl:
                for ev in b.r:
                    add(ev)
        for b in writes:
            add(b.w)
            for ev in b.r:
                add(ev)
        kn = self.known[e]
        pes = self.esem["pe"]
        for k, (s, v) in deps.items():
            if e == "pe" and s is pes:
                continue
            if kn.get(k, 0) >= v:
                continue
            self.E[e].wait_ge(s, v)
            kn[k] = v

    def _commit(self, ev, reads, writes):
        for b in writes:
            b.w = ev
            b.r = []
        for b in reads:
            if b not in writes:
                b.r.append(ev)
                if len(b.r) > 12:
                    m = {}
                    for s, v in b.r:
                        if id(s) not in m or m[id(s)][1] < v:
                            m[id(s)] = (s, v)
                    b.r = list(m.values())

    def op(self, e, fn, reads=(), writes=()):
        self._waits(e, reads, writes)
        ins = fn()
        ins.then_inc(self.esem[e], 1)
        self.ecnt[e] += 1
        ev = (self.esem[e], self.ecnt[e])
        self._commit(ev, reads, writes)
        return ev

    def dma(self, q, out, in_, sembuf, reads=(), writes=(), is_output=False):
        self._waits(q, reads, writes)
        if sembuf.dsem is None:
            sembuf.dsem = self.nc.alloc_semaphore("ds_" + sembuf.name)
        self.E[q].dma_start(out=out, in_=in_).then_inc(sembuf.dsem, 16)
        sembuf.dcnt += 1
        ev = (sembuf.dsem, 16 * sembuf.dcnt)
        self._commit(ev, reads, writes)
        if is_output:
            self.out_events.append(ev)
        return ev

    def finish(self):
        m = {}
        for s, v in self.out_events:
            if id(s) not in m or m[id(s)][1] < v:
                m[id(s)] = (s, v)
        for s, v in m.values():
            self.E["sp"].wait_ge(s, v)


def bc_mid(ap, n):
    p, f = ap.shape
    return ap.unsqueeze(1).broadcast_to([p, n, f])


def bc_last(ap, n):
    p, g = ap.shape
    return ap.unsqueeze(2).broadcast_to([p, g, n])


SLOPES = [2.0 ** (-8.0 * (h + 1) / 16.0) for h in range(16)]
KB1 = 1024


class _Stop(Exception):
    pass


class Prog:
    def __init__(self, T, depth, do_ffn=True, do_mix=True, dbg=None):
        self.dbg = dbg
        self.T = T
        self.depth = depth
        self.do_ffn = do_ffn
        self.do_mix = do_mix
        self.NTILES = T // NT
        kb = KB()
        self.kb = kb
        nc = kb.nc
        self.nc = nc
        dt = nc.dram_tensor
        L = depth
        self.xT = dt("xT", [D, T], F32, kind="ExternalInput").ap()
        self.yT = dt("yT", [D, T], F32, kind="ExternalOutput").ap()
        self.cT = dt("cT", [128, C], F32, kind="ExternalInput").ap()
        self.w_mod = dt("w_mod", [L, D, 9 * D], F32, kind="ExternalInput").ap()
        self.b_modT = dt("b_modT", [L, 128, 9 * C], F32, kind="ExternalInput").ap()
        self.npreT = dt("npreT", [L, 128, 3 * C], F32, kind="ExternalInput").ap()
        self.npostT = dt("npostT", [L, 128, 3 * C], F32, kind="ExternalInput").ap()
        self.w_gate = dt("w_gate", [L, 2, D, DFF], F32, kind="ExternalInput").ap()
        self.w_up = dt("w_up", [L, 2, D, DFF], F32, kind="ExternalInput").ap()
        self.w_down = dt("w_down", [L, 2, DFF, D], F32, kind="ExternalInput").ap()
        self.w_in = dt("w_in", [L, D, P_IN], F32, kind="ExternalInput").ap()
        self.w_ao = dt("w_ao", [L, 1024, D], F32, kind="ExternalInput").ap()
        self.w_so = dt("w_so", [L, D, D], F32, kind="ExternalInput").ap()
        self.w_o = dt("w_o", [L, D, D], F32, kind="ExternalInput").ap()
        self.acon_d = dt("acon", [128, 3 * 256], F32, kind="ExternalInput").ap()
        self.sink_d = dt("sinkc", [L, 128, 16], F32, kind="ExternalInput").ap()
        self.convw_d = dt("convwT", [L, 128, 24 * 4], F32, kind="ExternalInput").ap()
        self.convb_d = dt("convbT", [L, 128, 24], F32, kind="ExternalInput").ap()
        self.hvec_d = dt("hvec", [L, 128, 3 * 32], F32, kind="ExternalInput").ap()
        self.snorm_d = dt("snorm", [L, 128, D], F32, kind="ExternalInput").ap()
        self.cmats_d = dt("cmats", [128, 4 * 128], F32, kind="ExternalInput").ap()
        self.cur = self.xT
        self.b_y = [kb.bufs(C, f"y{t}_") for t in range(self.NTILES)]
        self._alloc()
        self._emit()
        kb.finish()

    def carve(self, off, dtype, shape):
        esz = 4 if dtype == F32 else 2
        n = 1
        for d in shape[1:]:
            n *= d
        assert off % 4 == 0 and off + n * esz <= self.ARENA, (off, n * esz, self.ARENA)
        v = self.arena[:, off:off + n * esz].bitcast(dtype)
        if len(shape) == 3:
            v = v.rearrange("p (a b) -> p a b", a=shape[1])
        elif len(shape) == 4:
            v = v.rearrange("p (a b c) -> p a b c", a=shape[1], b=shape[2])
        return v

    def tok(self, off, nbytes):
        return self.b_ar[off // 2048:(off + nbytes + 2047) // 2048]

    def _alloc(self):
        nc, kb = self.nc, self.kb
        A = nc.alloc_sbuf_tensor
        K = KB1
        self.cm = A("cm", [128, 4 * 128], F32)
        self.cm_bf = A("cm_bf", [128, 4 * 128], BF16)
        self.tri = self.cm[:, 0:128]
        self.ident = self.cm[:, 128:256]
        self.ones = self.cm[:, 256:384]
        self.nones = self.cm[:, 384:512]
        self.ident_bf = self.cm_bf[:, 128:256]
        self.ones_bf = self.cm_bf[:, 256:384]
        self.epsc = A("epsc", [128, 1], F32)
        self.cact = A("cact", [128, C], BF16)
        self.ctmp = A("ctmp", [128, C], F32)
        self.mod = A("mod", [128, 9 * C], F32)
        self.bmod = A("bmod", [128, 9 * C], F32)
        self.npre = A("npre", [128, 3 * C], F32)
        self.npost = A("npost", [128, 3 * C], F32)
        self.gsc = A("gsc", [128, 3 * C], F32)
        self.gp = A("gp", [128, 3 * C], F32)
        self.b_const = kb.buf("const")
        self.b_mod = kb.buf("mod")
        self.b_cact = kb.buf("cact")
        self.rstd = A("rstd", [128, NT], F32)
        self.b_rstd = kb.buf("rstd")
        self.rstd2 = A("rstd2", [128, NT], F32)
        self.b_rstd2 = kb.buf("rstd2")
        self.sq = A("sq", [128, 2, 2 * NT], BF16)
        self.b_sq = kb.bufs(2, "sq")
        self.tmp = A("tmp", [128, 2, NT], F32)
        self.b_tmp = kb.bufs(2, "tmp")
        self.xr = A("xr", [128, 2, NT], F32)
        self.b_xr = kb.bufs(2, "xr")
        self.sg = A("sg", [128, 2, NT], F32)
        self.b_sg = kb.bufs(2, "sg")
        self.sqf = A("sqf", [128, 2, NT], BF16)
        self.b_sqf = kb.bufs(2, "sqf")
        self.acon = A("acon_sb", [128, 3 * 256], F32)
        self.sinkc = A("sink_sb", [128, 16], F32)
        self.convw = A("convw_sb", [128, 24 * 4], F32)
        self.convb = A("convb_sb", [128, 24], F32)
        self.hv = A("hv_sb", [128, 96], F32)
        self.na = A("na_sb", [128, 32], F32)
        self.snorm = A("snorm_sb", [128, D], F32)
        self.S = A("S_sb", [128, D], F32)
        self.halo = A("halo_sb", [128, 24, 4], F32)
        self.b_lay = kb.buf("layer_consts")
        self.b_S = kb.bufs(4, "S")
        self.b_halo = kb.bufs(24, "halo")
        self.psall = nc.alloc_psum_tensor("psall", [128, 8, 512], F32)
        self.ps = [self.psall[:, i, :] for i in range(8)]
        self.b_ps = kb.bufs(8, "ps")
        for b in self.b_ps:
            b.excl = True
        self.ARENA = (nc.sbuf_bytes_remaining // 1024 - 1) * 1024
        self.arena = A("arena", [128, self.ARENA], mybir.dt.uint8)
        self.b_ar = kb.bufs(self.ARENA // 2048 + 1, "ar")
        self.o_hT = 0
        self.hT = self.carve(0, BF16, [128, JF, NT])
        self.o_fT = 44 * K
        self.fT = self.carve(self.o_fT, F32, [128, C, NT])
        self.o_wA = [76 * K, 92 * K]
        self.wA = [self.carve(o, BF16, [128, C, 512]) for o in self.o_wA]
        self.o_wB = [108 * K, 130 * K]
        self.wB = [self.carve(o, BF16, [128, JF, 256]) for o in self.o_wB]
        assert 152 * K <= self.ARENA, self.ARENA
        self.o_qT = 16 * K
        self.qT = self.carve(self.o_qT, BF16, [128, 8, NT])
        self.o_kT = 24 * K
        self.kT = self.carve(self.o_kT, BF16, [128, 4, 640])
        self.o_vd = 30 * K
        self.vd = self.carve(self.o_vd, BF16, [128, 5, 4, 128])
        self.o_BT = 36 * K
        self.BT = self.carve(self.o_BT, BF16, [128, 4, NT])
        self.o_CT = 40 * K
        self.CT = self.carve(self.o_CT, BF16, [128, 4, NT])
        self.o_dt = 44 * K
        self.dtv = self.carve(self.o_dt, F32, [128, 4, 32])
        self.o_xsT = 46 * K
        self.xsT = self.carve(self.o_xsT, BF16, [128, C, NT])
        self.o_zs = 108 * K
        self.zs = self.carve(self.o_zs, BF16, [128, 4, D])
        self.o_ynT = 124 * K
        self.ynT = self.carve(self.o_ynT, BF16, [128, C, NT])
        self.o_xdt = 140 * K
        self.xdt = self.carve(self.o_xdt, BF16, [128, D])
        self.o_xdtd = 144 * K
        self.xdtd = self.carve(self.o_xdtd, BF16, [128, D])
        self.o_y = 148 * K
        self.y = self.carve(self.o_y, F32, [128, D])
        self.o_ynb = 140 * K
        self.ynb = self.carve(self.o_ynb, BF16, [128, D])
        assert 159 * K <= self.ARENA, self.ARENA
        self.o_Sbf = 62 * K
        self.Sbf = self.carve(self.o_Sbf, BF16, [128, D])
        self.o_Btok = 66 * K
        self.Btok = self.carve(self.o_Btok, BF16, [128, 4, 128])
        self.o_MT = 67 * K
        self.MT = self.carve(self.o_MT, BF16, [128, 4, 128])
        self.o_dat = 68 * K
        self.dat = self.carve(self.o_dat, F32, [128, 4, 128])
        self.o_Dm = 70 * K
        self.Dm = self.carve(self.o_Dm, F32, [128, 4, 128])
        self.o_CBm = 72 * K
        self.CBm = self.carve(self.o_CBm, F32, [128, 4, 128])
        self.o_sv = 74 * K
        self.sv = self.carve(self.o_sv, F32, [128, 16, 32])
        self.o_ssb = [140 * K, 140 * K + 4160]
        self.ssb = [self.carve(o, F32, [128, 4, 260]) for o in self.o_ssb]
        self.o_p = [150 * K, 152 * K]
        self.pp = [self.carve(o, BF16, [128, 4, 256]) for o in self.o_p]
        self.o_pT = [154 * K, 156 * K]
        self.pT = [self.carve(o, BF16, [128, 8, 128]) for o in self.o_pT]
        self.o_ast = 158 * K
        self.ast = self.carve(self.o_ast, F32, [128, 2, 16])
        self.ring = {}

    def rr(self, key, n):
        i = self.ring.get(key, 0)
        self.ring[key] = i + 1
        return i % n

    def mrg(self, c):
        if c < 7:
            o = 62 * KB1 + c * 2048
        else:
            o = 140 * KB1 + (c - 7) * 2048
        return self.carve(o, F32, [128, NT]), self.tok(o, 2048)

    def mm(self, out, pairs, reads, wtoks, f32=False):
        nc = self.nc
        n = len(pairs)

        def f_():
            ins = None
            for i, (l, r) in enumerate(pairs):
                ins = nc.tensor.matmul(out, lhsT=l, rhs=r, start=(i == 0), stop=(i == n - 1))
            return ins
        return self.kb.op("pe", f_, reads=reads, writes=wtoks)

    def rsqrt_mean(self, out, in_, n, rtoks, wtoks):
        nc = self.nc
        self.kb.op("act", lambda: nc.scalar.activation(out=out, in_=in_, func=AF.Sqrt, scale=1.0 / n, bias=self.epsc[:, 0:1]),
                   reads=rtoks + [self.b_const], writes=wtoks)
        self.kb.op("dve", lambda: nc.vector.reciprocal(out=out, in_=out), reads=wtoks, writes=wtoks)

    def load_slab(self, src, kcs, c0, ncols, dst_off=0):
        i = self.rr("wA", 2)
        self._slab_into(i, src, kcs, c0, ncols, dst_off)
        return i

    def _slab_into(self, i, src, kcs, c0, ncols, dst_off=0):
        wb = self.wA[i]
        toks = self.tok(self.o_wA[i], 16 * KB1)
        sv = src.rearrange("(kc p) n -> p kc n", p=128)
        step = 4 if ncols >= 512 else 8
        for k0 in range(0, kcs, step):
            k1 = min(kcs, k0 + step)
            self.kb.dma("pool", wb[:, k0:k1, dst_off:dst_off + ncols], sv[:, k0:k1, c0:c0 + ncols], toks[0], writes=toks)

    def _emit(self):
        kb, nc = self.kb, self.nc
        kb.dma("sp", self.cm[:], self.cmats_d[:, :], self.b_const, writes=[self.b_const])
        kb.op("dve", lambda: nc.vector.tensor_copy(out=self.cm_bf[:], in_=self.cm[:]),
              reads=[self.b_const], writes=[self.b_const])
        kb.dma("sp", self.acon[:], self.acon_d[:, :], self.b_const, writes=[self.b_const])
        kb.op("dve", lambda: nc.vector.memset(self.epsc[:], EPS), writes=[self.b_const])
        kb.dma("sp", self.ctmp[:], self.cT[:, :], self.b_cact, writes=[self.b_cact])
        kb.op("act", lambda: nc.scalar.activation(out=self.cact[:], in_=self.ctmp[:], func=AF.Silu),
              reads=[self.b_cact], writes=[self.b_cact])
        try:
            self._emit_layers()
        except _Stop:
            kb.op("dve", lambda: nc.vector.memset(self.rstd[:], 1.0), writes=[self.b_rstd])
            kb.dma("sp", self.yT[0:128, 0:NT], self.rstd[:], self.b_rstd, reads=[self.b_rstd], is_output=True)

    def stage(self, name):
        if self.dbg == name:
            raise _Stop()

    def _emit_layers(self):
        kb, nc = self.kb, self.nc
        for l in range(self.depth):
            self.emit_mod(l)
            if self.dbg == "mod":
                kb.dma("sp", self.yT[0:128, 0:144], self.mod[:], self.b_mod, reads=[self.b_mod], is_output=True)
                return
            if self.do_ffn:
                self.emit_ffn(l, 0)
            if self.do_mix:
                self.emit_mixer(l)
            if self.do_ffn:
                self.emit_ffn(l, 1)

    def emit_mod(self, l):
        kb, nc = self.kb, self.nc
        V = nc.vector
        kb.dma("sp", self.bmod[:], self.b_modT[l], self.b_mod, writes=[self.b_mod])
        kb.dma("sp", self.npre[:], self.npreT[l], self.b_mod, writes=[self.b_mod])
        kb.dma("sp", self.npost[:], self.npostT[l], self.b_mod, writes=[self.b_mod])
        pb = 7
        nslab = 9 * D // 512
        pend = [self.load_slab(self.w_mod[l], C, 0, 512)]
        for s in range(nslab):
            if s + 1 < nslab:
                pend.append(self.load_slab(self.w_mod[l], C, (s + 1) * 512, 512))
            i = pend.pop(0)
            wb = self.wA[i]
            toks = self.tok(self.o_wA[i], 16 * KB1)
            for cc in range(4):
                m = s * 4 + cc
                self.mm(self.ps[pb][:, m:m + 1],
                        [(wb[:, kc, cc * 128:(cc + 1) * 128], self.cact[:, kc:kc + 1]) for kc in range(C)],
                        reads=toks + [self.b_cact], wtoks=[self.b_ps[pb]])
        kb.op("dve", lambda: V.tensor_tensor(out=self.mod[:], in0=self.ps[pb][:, 0:9 * C], in1=self.bmod[:], op=ALU.add),
              reads=[self.b_ps[pb], self.b_mod], writes=[self.b_mod])
        for s in range(3):
            sc = self.mod[:, (3 * s + 1) * C:(3 * s + 2) * C]
            g = self.mod[:, (3 * s + 2) * C:(3 * s + 3) * C]
            kb.op("dve", lambda s=s, sc=sc: V.scalar_tensor_tensor(
                out=self.gsc[:, s * C:(s + 1) * C], in0=sc, scalar=1.0, in1=self.npre[:, s * C:(s + 1) * C],
                op0=ALU.add, op1=ALU.mult), reads=[self.b_mod], writes=[self.b_mod])
            kb.op("dve", lambda s=s, g=g: V.scalar_tensor_tensor(
                out=self.gp[:, s * C:(s + 1) * C], in0=g, scalar=(1.0 if s == 1 else 0.5),
                in1=self.npost[:, s * C:(s + 1) * C], op0=ALU.mult, op1=ALU.mult),
                reads=[self.b_mod], writes=[self.b_mod])

    def emit_prenorm(self, s, t, o_xt, o_nT):
        kb, nc = self.kb, self.nc
        V, S_ = nc.vector, nc.scalar
        xt = self.carve(o_xt, F32, [128, C, NT])
        nT = self.carve(o_nT, BF16, [128, C, NT])
        src = self.cur.rearrange("(c p) n -> p c n", p=128)
        for h in range(4):
            tk = self.tok(o_xt + 4 * h * 2048, 4 * 2048)
            kb.dma("sp", xt[:, 4 * h:4 * h + 4, :], src[:, 4 * h:4 * h + 4, t * NT:(t + 1) * NT],
                   tk[0], reads=self.b_y[t][4 * h:4 * h + 4], writes=tk)
        pb = 6
        for h in range(8):
            i = self.rr("sq", 2)
            tk = self.tok(o_xt + 2 * h * 2048, 2 * 2048)
            kb.op("act", lambda h=h, i=i: S_.activation(
                out=self.sq[:, i, :], in_=xt[:, 2 * h:2 * h + 2, :].rearrange("p c n -> p (c n)"), func=AF.Square),
                reads=tk, writes=[self.b_sq[i]])

            def mm(h=h, i=i):
                ins = None
                for cc in range(2):
                    ins = nc.tensor.matmul(self.ps[pb], lhsT=self.ones_bf, rhs=self.sq[:, i, cc * NT:(cc + 1) * NT],
                                           start=(h == 0 and cc == 0), stop=(h == 7 and cc == 1))
                return ins
            kb.op("pe", mm, reads=[self.b_sq[i], self.b_const], writes=[self.b_ps[pb]])
        self.rsqrt_mean(self.rstd[:], self.ps[pb], D, [self.b_ps[pb]], [self.b_rstd])
        for c in range(C):
            i = self.rr("tmp", 2)
            kb.op("dve", lambda c=c, i=i: V.tensor_tensor(out=self.tmp[:, i, :], in0=xt[:, c, :], in1=self.rstd[:], op=ALU.mult),
                  reads=self.tok(o_xt + c * 2048, 2048) + [self.b_rstd], writes=[self.b_tmp[i]])
            kb.op("act", lambda c=c, i=i: S_.activation(
                out=nT[:, c, :], in_=self.tmp[:, i, :], func=AF.Identity,
                scale=self.gsc[:, s * C + c:s * C + c + 1], bias=self.mod[:, 3 * s * C + c:3 * s * C + c + 1]),
                reads=[self.b_tmp[i], self.b_mod], writes=self.tok(o_nT + c * 1024, 1024))

    def emit_postnorm_store(self, s, t, o_fT, pb2):
        kb, nc = self.kb, self.nc
        V = nc.vector
        fT = self.carve(o_fT, F32, [128, C, NT])
        self.rsqrt_mean(self.rstd2[:], self.ps[pb2], D, [self.b_ps[pb2]], [self.b_rstd2])
        src = self.cur.rearrange("(c p) n -> p c n", p=128)
        dst = self.yT.rearrange("(c p) n -> p c n", p=128)
        for c in range(C):
            j = self.rr("xr", 2)
            kb.dma("sp", self.xr[:, j, :], src[:, c, t * NT:(t + 1) * NT], self.b_xr[j],
                   reads=[self.b_y[t][c]], writes=[self.b_xr[j]])
            i = self.rr("tmp", 2)
            kb.op("dve", lambda c=c, i=i: V.tensor_tensor(out=self.tmp[:, i, :], in0=fT[:, c, :], in1=self.rstd2[:], op=ALU.mult),
                  reads=self.tok(o_fT + c * 2048, 2048) + [self.b_rstd2], writes=[self.b_tmp[i]])
            kb.op("dve", lambda c=c, i=i, j=j: V.scalar_tensor_tensor(
                out=self.xr[:, j, :], in0=self.tmp[:, i, :], scalar=self.gp[:, s * C + c:s * C + c + 1], in1=self.xr[:, j, :],
                op0=ALU.mult, op1=ALU.add), reads=[self.b_tmp[i], self.b_mod, self.b_xr[j]], writes=[self.b_xr[j]])
            kb.dma("sp", dst[:, c, t * NT:(t + 1) * NT], self.xr[:, j, :], self.b_xr[j],
                   reads=[self.b_xr[j]], writes=[self.b_y[t][c]], is_output=True)

    def emit_ffn(self, l, f):
        kb, nc = self.kb, self.nc
        V, S_ = nc.vector, nc.scalar
        s = 0 if f == 0 else 2
        wd = self.w_down[l, f].rearrange("(j p) n -> p j n", p=128)
        NS1 = DFF // 256
        NS2 = D // 256
        o_nT = self.o_fT
        nT = self.carve(o_nT, BF16, [128, C, NT])
        nT_toks = self.tok(o_nT, 16 * KB1)
        hT_toks = self.tok(self.o_hT, 44 * KB1)

        def load_gu(sl):
            i = self.rr("wA", 2)
            self._slab_into(i, self.w_gate[l, f], C, sl * 256, 256, 0)
            self._slab_into(i, self.w_up[l, f], C, sl * 256, 256, 256)
            return i

        def load_d(sl):
            i = self.rr("wB", 2)
            wb = self.wB[i]
            toks = self.tok(self.o_wB[i], 22 * KB1)
            for h in range(4):
                kb.dma("pool", wb[:, 11 * h:11 * h + 11, :], wd[:, 11 * h:11 * h + 11, sl * 256:(sl + 1) * 256], toks[0], writes=toks)
            return i

        for t in range(self.NTILES):
            pend_gu = [load_gu(0), load_gu(1)]
            self.emit_prenorm(s, t, self.o_hT, o_nT)
            pend_d = []
            for sl in range(NS1):
                wi = pend_gu.pop(0)
                wb = self.wA[wi]
                wtoks = self.tok(self.o_wA[wi], 16 * KB1)
                for jj in range(2):
                    j = sl * 2 + jj
                    pg = self.rr("pgu", 2) * 2
                    pu = pg + 1
                    self.mm(self.ps[pg], [(wb[:, kc, jj * 128:(jj + 1) * 128], nT[:, kc, :]) for kc in range(C)],
                            reads=wtoks + nT_toks, wtoks=[self.b_ps[pg]])
                    self.mm(self.ps[pu], [(wb[:, kc, 256 + jj * 128:256 + (jj + 1) * 128], nT[:, kc, :]) for kc in range(C)],
                            reads=wtoks + nT_toks, wtoks=[self.b_ps[pu]])
                    i = self.rr("sg", 2)
                    kb.op("act", lambda i=i, pg=pg: S_.activation(out=self.sg[:, i, :], in_=self.ps[pg], func=AF.Silu),
                          reads=[self.b_ps[pg]], writes=[self.b_sg[i]])
                    kb.op("dve", lambda i=i, pu=pu, j=j: V.tensor_tensor(out=self.hT[:, j, :], in0=self.sg[:, i, :], in1=self.ps[pu], op=ALU.mult),
                          reads=[self.b_sg[i], self.b_ps[pu]], writes=self.tok(self.o_hT + j * 1024, 1024))
                if sl + 2 < NS1:
                    pend_gu.append(load_gu(sl + 2))
                else:
                    pend_d.append(load_d(len(pend_d)))
            pb2 = 7
            for sl in range(NS2):
                wi = pend_d.pop(0)
                wb = self.wB[wi]
                wtoks = self.tok(self.o_wB[wi], 22 * KB1)
                for ii in range(2):
                    c = sl * 2 + ii
                    pf = 4 + self.rr("pf", 2)
                    self.mm(self.ps[pf], [(wb[:, j, ii * 128:(ii + 1) * 128], self.hT[:, j, :]) for j in range(JF)],
                            reads=wtoks + hT_toks, wtoks=[self.b_ps[pf]])
                    kb.op("dve", lambda c=c, pf=pf: V.tensor_copy(out=self.fT[:, c, :], in_=self.ps[pf]),
                          reads=[self.b_ps[pf]], writes=self.tok(self.o_fT + c * 2048, 2048))
                    i = self.rr("sqf", 2)
                    kb.op("act", lambda i=i, pf=pf: S_.activation(out=self.sqf[:, i, :], in_=self.ps[pf], func=AF.Square),
                          reads=[self.b_ps[pf]], writes=[self.b_sqf[i]])
                    self.mm(self.ps[pb2], [(self.ones_bf, self.sqf[:, i, :])], reads=[self.b_sqf[i], self.b_const], wtoks=[self.b_ps[pb2]]) \
                        if False else kb.op("pe", lambda i=i, c=c: nc.tensor.matmul(
                            self.ps[pb2], lhsT=self.ones_bf, rhs=self.sqf[:, i, :], start=(c == 0), stop=(c == C - 1)),
                            reads=[self.b_sqf[i], self.b_const], writes=[self.b_ps[pb2]])
                if sl + 2 < NS2:
                    pend_d.append(load_d(sl + 2))
            self.emit_postnorm_store(s, t, self.o_fT, pb2)
        self.cur = self.yT

    def psbf(self, b0, nb=1):
        return self.psall[:, b0:b0 + nb, :].rearrange("p a b -> p (a b)").bitcast(BF16)

    def emit_mixer(self, l):
        kb, nc = self.kb, self.nc
        V, S_ = nc.vector, nc.scalar
        K = KB1
        bl = self.b_lay
        for dst, src in ((self.sinkc, self.sink_d), (self.convw, self.convw_d), (self.convb, self.convb_d),
                         (self.hv, self.hvec_d), (self.snorm, self.snorm_d)):
            kb.dma("sp", dst[:], src[l], bl, writes=[bl])
        kb.op("act", lambda: S_.activation(out=self.na[:], in_=self.hv[:, 32:64], func=AF.Exp), reads=[bl], writes=[bl])
        kb.op("dve", lambda: V.tensor_scalar(out=self.na[:], in0=self.na[:], scalar1=-1.0, scalar2=0.0, op0=ALU.mult, op1=ALU.add),
              reads=[bl], writes=[bl])
        kb.op("dve", lambda: V.memset(self.S[:], 0.0), writes=self.b_S)
        kb.op("dve", lambda: V.memset(self.halo[:], 0.0), writes=self.b_halo)
        t_kT = self.tok(self.o_kT, 5 * K)
        t_vd = self.tok(self.o_vd, 5 * K)
        kb.op("dve", lambda: V.memset(self.kT[:, :, 0:128], 0.0), writes=t_kT)
        kb.op("dve", lambda: V.memset(self.vd[:, 0], 0.0), writes=t_vd)
        for t in range(self.NTILES):
            self.mixer_tile(l, t)
        self.cur = self.yT

    def mixer_tile(self, l, t):
        kb, nc = self.kb, self.nc
        V, S_ = nc.vector, nc.scalar
        K = KB1
        bl = self.b_lay
        nT = self.carve(0, BF16, [128, C, NT])
        t_nT = self.tok(0, 16 * K)
        t_qT = self.tok(self.o_qT, 8 * K)
        t_kT = self.tok(self.o_kT, 5 * K)
        t_vd = self.tok(self.o_vd, 5 * K)
        t_BT = self.tok(self.o_BT, 4 * K)
        t_CT = self.tok(self.o_CT, 4 * K)
        t_dt = self.tok(self.o_dt, 2 * K)
        t_xsT = self.tok(self.o_xsT, 16 * K)
        t_zs = self.tok(self.o_zs, 16 * K)
        t_ynT = self.tok(self.o_ynT, 16 * K)
        t_xdt = self.tok(self.o_xdt, 4 * K)
        t_xdtd = self.tok(self.o_xdtd, 4 * K)
        t_y = self.tok(self.o_y, 8 * K)
        t_ynb = self.tok(self.o_ynb, 4 * K)
        t_Sbf = self.tok(self.o_Sbf, 4 * K)
        t_sm = self.tok(self.o_Btok, 10 * K)
        t_at = self.tok(140 * K, 19 * K)
        win = self.w_in[l]
        ps, bps = self.ps, self.b_ps

        pend = [self.load_slab(win, C, 0, 512), self.load_slab(win, C, 512, 512)]
        self.emit_prenorm(1, t, self.o_zs, 0)

        def next_slab():
            i = pend.pop(0)
            return self.wA[i], self.tok(self.o_wA[i], 16 * K)

        def fm_chunk(wb, wt, cc, pb):
            self.mm(ps[pb], [(wb[:, kc, cc * 128:(cc + 1) * 128], nT[:, kc, :]) for kc in range(C)],
                    reads=wt + t_nT, wtoks=[bps[pb]])

        for sl in range(2):
            wb, wt = next_slab()
            for cc in range(4):
                pb = self.rr("p4", 4)
                fm_chunk(wb, wt, cc, pb)
                c = sl * 4 + cc
                kb.op("act", lambda c=c, pb=pb: S_.activation(out=self.qT[:, c, :], in_=ps[pb], func=AF.Copy, scale=0.125),
                      reads=[bps[pb]], writes=t_qT)
            pend.append(self.load_slab(win, C, 1024 if sl == 0 else 1536, 512))
        self.stage("q")
        wb, wt = next_slab()
        for i in range(2):
            pb = self.rr("p4", 4)
            fm_chunk(wb, wt, i, pb)
            kb.op("act", lambda i=i, pb=pb: S_.activation(out=self.kT[0:64, 2 * i, 128:640], in_=ps[pb][0:64, :], func=AF.Copy),
                  reads=[bps[pb]], writes=t_kT)
            kb.op("act", lambda i=i, pb=pb: S_.activation(out=self.kT[64:128, 2 * i + 1, 128:640], in_=ps[pb][64:128, :], func=AF.Copy),
                  reads=[bps[pb]], writes=t_kT)
        self.stage("kcopy")
        for i in range(2):
            kb.dma("sp", self.kT[64:128, 2 * i, 128:640], self.kT[0:64, 2 * i, 128:640], t_kT[0], reads=t_kT, writes=t_kT)
            kb.dma("sp", self.kT[0:64, 2 * i + 1, 128:640], self.kT[64:128, 2 * i + 1, 128:640], t_kT[0], reads=t_kT, writes=t_kT)
        self.stage("kdma")
        for b in range(4):
            pb = self.rr("p4", 4)
            self.mm(ps[pb][:, 0:256], [(nT[:, kc, b * 128:(b + 1) * 128], wb[:, kc, 256:512]) for kc in range(C)],
                    reads=wt + t_nT, wtoks=[bps[pb]])
            kb.op("dve", lambda b=b, pb=pb: V.tensor_copy(
                out=self.vd[:, 1 + b].rearrange("p k (r d) -> p k r d", r=2),
                in_=ps[pb][:, 0:256].rearrange("p (k d) -> p k d", k=4).unsqueeze(2).broadcast_to([128, 4, 2, 64])),
                reads=[bps[pb]], writes=t_vd)
        pend.append(self.load_slab(win, C, 2048, 512))
        self.stage("v")
        for i in range(4):
            wb, wt = next_slab()
            for b in range(4):
                pb = self.rr("p4", 4)
                self.mm(ps[pb], [(nT[:, kc, b * 128:(b + 1) * 128], wb[:, kc, :]) for kc in range(C)],
                        reads=wt + t_nT, wtoks=[bps[pb]])
                kb.op("act", lambda i=i, b=b, pb=pb: S_.activation(out=self.zs[:, b, i * 512:(i + 1) * 512], in_=ps[pb], func=AF.Silu),
                      reads=[bps[pb]], writes=t_zs)
            pend.append(self.load_slab(win, C, [2560, 3072, 3584, 4096][i], 512))
        self.stage("z")
        for i in range(6):
            wb, wt = next_slab()
            for cc in range(4):
                fc = i * 4 + cc
                pb = self.rr("p4", 4)
                fm_chunk(wb, wt, cc, pb)
                u = ps[pb]
                j = self.rr("tmp", 2)
                acc = self.tmp[:, j, :]
                bt = self.b_tmp[j]
                w = lambda k, fc=fc: self.convw[:, fc * 4 + k:fc * 4 + k + 1]
                hl = self.halo[:, fc, :]
                bh = self.b_halo[fc]
                kb.op("act", lambda acc=acc, u=u, fc=fc: S_.activation(out=acc, in_=u, func=AF.Identity, scale=self.convw[:, fc * 4 + 3:fc * 4 + 4],
                                                                       bias=self.convb[:, fc:fc + 1]), reads=[bps[pb], bl], writes=[bt])
                for k, sh in ((2, 1), (1, 2), (0, 3)):
                    kb.op("dve", lambda acc=acc, u=u, k=k, sh=sh, w=w: V.scalar_tensor_tensor(
                        out=acc[:, sh:NT], in0=u[:, 0:NT - sh], scalar=w(k), in1=acc[:, sh:NT], op0=ALU.mult, op1=ALU.add),
                        reads=[bps[pb], bl, bt], writes=[bt])
                    kb.op("dve", lambda acc=acc, hl=hl, k=k, sh=sh, w=w: V.scalar_tensor_tensor(
                        out=acc[:, 0:sh], in0=hl[:, 3 - sh:3], scalar=w(k), in1=acc[:, 0:sh], op0=ALU.mult, op1=ALU.add),
                        reads=[bh, bl, bt], writes=[bt])
                kb.op("dve", lambda hl=hl, u=u: V.tensor_copy(out=hl[:, 0:3], in_=u[:, NT - 3:NT]), reads=[bps[pb]], writes=[bh])
                if fc < 16:
                    dst, dtk = self.xsT[:, fc, :], t_xsT
                elif fc < 20:
                    dst, dtk = self.BT[:, fc - 16, :], t_BT
                else:
                    dst, dtk = self.CT[:, fc - 20, :], t_CT
                kb.op("act", lambda dst=dst, acc=acc: S_.activation(out=dst, in_=acc, func=AF.Silu), reads=[bt], writes=dtk)
            if i < 4:
                pend.append(self.load_slab(win, C, 3584 + (i + 2) * 512, 512))
            elif i == 4:
                pend.append(self.load_slab(win, C, 6688 - 512, 512))
        self.stage("conv")
        wb, wt = next_slab()
        sv = self.sv
        for b in range(4):
            pb = self.rr("p4", 4)
            self.mm(ps[pb][:, 0:32], [(nT[:, kc, b * 128:(b + 1) * 128], wb[:, kc, 480:512]) for kc in range(C)],
                    reads=wt + t_nT, wtoks=[bps[pb]])
            kb.op("dve", lambda pb=pb: V.tensor_tensor(out=sv[:, 0, :], in0=ps[pb][:, 0:32], in1=self.hv[:, 0:32], op=ALU.add),
                  reads=[bps[pb], bl], writes=t_sm)
            kb.op("act", lambda: S_.activation(out=sv[:, 1, :], in_=sv[:, 0, :], func=AF.Exp), reads=t_sm, writes=t_sm)
            kb.op("act", lambda b=b: S_.activation(out=self.dtv[:, b, :], in_=sv[:, 1, :], func=AF.Ln, bias=self.ones[:, 0:1], scale=1.0),
                  reads=t_sm + [self.b_const], writes=t_dt)
        self.stage("inproj")
        kb.op("dve", lambda: V.tensor_copy(out=self.Sbf[:], in_=self.S[:]), reads=self.b_S, writes=t_Sbf)
        psx = self.psbf(0, 2)
        psB = self.psbf(2)
        tri, ones, nones = self.tri, self.ones, self.nones
        for b in range(4):
            blk = slice(b * 128, (b + 1) * 128)
            dt_ = self.dtv[:, b, :]
            dA, acs, eacs, dd, dtd, cdec = sv[:, 2, :], sv[:, 3, :], sv[:, 4, :], sv[:, 5, :], sv[:, 6, :], sv[:, 7, :]
            kb.op("dve", lambda dt_=dt_: V.tensor_tensor(out=dA, in0=dt_, in1=self.na[:], op=ALU.mult), reads=t_dt + [bl], writes=t_sm)
            self.mm(ps[3][:, 0:32], [(tri, dA)], reads=t_sm + [self.b_const], wtoks=[bps[3]])
            self.mm(ps[3][:, 32:64], [(ones, dA)], reads=t_sm + [self.b_const], wtoks=[bps[3]])
            kb.op("dve", lambda: V.tensor_copy(out=acs, in_=ps[3][:, 0:32]), reads=[bps[3]], writes=t_sm)
            kb.op("act", lambda: S_.activation(out=eacs, in_=acs, func=AF.Exp), reads=t_sm, writes=t_sm)
            kb.op("dve", lambda: V.tensor_tensor(out=dd, in0=ps[3][:, 32:64], in1=acs, op=ALU.subtract), reads=[bps[3]] + t_sm, writes=t_sm)
            kb.op("act", lambda: S_.activation(out=dd, in_=dd, func=AF.Exp), reads=t_sm, writes=t_sm)
            kb.op("dve", lambda dt_=dt_: V.tensor_tensor(out=dtd, in0=dt_, in1=dd, op=ALU.mult), reads=t_dt + t_sm, writes=t_sm)
            kb.op("act", lambda: S_.activation(out=cdec, in_=ps[3][:, 32:64], func=AF.Exp), reads=[bps[3]], writes=t_sm)

            def tr_x(b=b):
                ins = None
                for c in range(C):
                    ins = nc.tensor.transpose(out=psx[:, c * 128:(c + 1) * 128], in_=self.xsT[:, c, b * 128:(b + 1) * 128], identity=self.ident_bf)
                return ins
            kb.op("pe", tr_x, reads=t_xsT + [self.b_const], writes=[bps[0], bps[1]])
            px3 = psx.rearrange("p (h d) -> p h d", h=32)
            kb.op("dve", lambda dt_=dt_: V.tensor_tensor(out=self.xdt[:].rearrange("p (h d) -> p h d", h=32), in0=px3, in1=bc_last(dt_, 64), op=ALU.mult),
                  reads=[bps[0], bps[1]] + t_dt, writes=t_xdt)
            kb.op("dve", lambda: V.tensor_tensor(out=self.xdtd[:].rearrange("p (h d) -> p h d", h=32), in0=px3, in1=bc_last(dtd, 64), op=ALU.mult),
                  reads=[bps[0], bps[1]] + t_sm, writes=t_xdtd)
            kb.op("dve", lambda: V.tensor_tensor(out=self.y[:].rearrange("p (h d) -> p h d", h=32), in0=px3, in1=bc_last(self.hv[:, 64:96], 64), op=ALU.mult),
                  reads=[bps[0], bps[1], bl], writes=t_y)

            def tr_b(b=b):
                ins = None
                for g in range(4):
                    ins = nc.tensor.transpose(out=psB[:, g * 128:(g + 1) * 128], in_=self.BT[:, g, b * 128:(b + 1) * 128], identity=self.ident_bf)
                return ins
            kb.op("pe", tr_b, reads=t_BT + [self.b_const], writes=[bps[2]])
            kb.op("act", lambda: S_.activation(out=self.Btok[:].rearrange("p g n -> p (g n)"), in_=psB[:, 0:512], func=AF.Copy),
                  reads=[bps[2]], writes=t_sm)
            for g in range(4):
                self.mm(ps[4][:, g * 128:(g + 1) * 128], [(self.BT[:, g, blk], self.CT[:, g, blk])], reads=t_BT + t_CT, wtoks=[bps[4]])
            kb.op("dve", lambda: V.tensor_tensor(out=self.CBm[:], in0=ps[4].rearrange("p (g n) -> p g n", g=4), in1=bc_mid(tri, 4), op=ALU.mult),
                  reads=[bps[4], self.b_const], writes=t_sm)
            for g in range(4):
                gs = slice(g * 512, (g + 1) * 512)
                for r in range(8):
                    h = 8 * g + r
                    i = self.rr("hd", 4)
                    q = self.rr("pD", 8)
                    pD = 5 + q // 4
                    dcol = slice((q % 4) * 128, (q % 4 + 1) * 128)
                    kb.op("dve", lambda i=i, h=h: V.tensor_scalar(out=self.dat[:, i, :], in0=tri, scalar1=dA[:, h:h + 1], scalar2=0.0,
                                                                  op0=ALU.mult, op1=ALU.add), reads=t_sm + [self.b_const], writes=t_sm)
                    self.mm(ps[pD][:, dcol], [(ones, self.dat[:, i, :]), (self.dat[:, i, :], nones)], reads=t_sm + [self.b_const], wtoks=[bps[pD]])
                    kb.op("dve", lambda i=i, pD=pD, dcol=dcol: V.tensor_scalar(out=self.Dm[:, i, :], in0=ps[pD][:, dcol], scalar1=0.0, scalar2=0.0,
                                                                              op0=ALU.min, op1=ALU.add), reads=[bps[pD]], writes=t_sm)
                    kb.op("act", lambda i=i: S_.activation(out=self.Dm[:, i, :], in_=self.Dm[:, i, :], func=AF.Exp), reads=t_sm, writes=t_sm)
                    kb.op("dve", lambda i=i, g=g: V.tensor_tensor(out=self.MT[:, i, :], in0=self.Dm[:, i, :], in1=self.CBm[:, g, :], op=ALU.mult),
                          reads=t_sm, writes=t_sm)
                    self.mm(ps[7][:, r * 64:(r + 1) * 64], [(self.MT[:, i, :], self.xdt[:, h * 64:(h + 1) * 64])], reads=t_sm + t_xdt, wtoks=[bps[7]])
                yo = g % 2
                self.mm(ps[yo], [(self.CT[:, g, blk], self.Sbf[:, gs])], reads=t_CT + t_Sbf, wtoks=[bps[yo]])
                j = self.rr("tmp", 2)
                kb.op("dve", lambda j=j, yo=yo, g=g: V.tensor_tensor(
                    out=self.tmp[:, j, :].rearrange("p (h d) -> p h d", h=8), in0=ps[yo].rearrange("p (h d) -> p h d", h=8),
                    in1=bc_last(eacs[:, 8 * g:8 * g + 8], 64), op=ALU.mult), reads=[bps[yo]] + t_sm, writes=[self.b_tmp[j]])
                kb.op("dve", lambda j=j, gs=gs: V.tensor_tensor(out=self.y[:, gs], in0=self.y[:, gs], in1=self.tmp[:, j, :], op=ALU.add),
                      reads=t_y + [self.b_tmp[j]], writes=t_y)
                kb.op("dve", lambda gs=gs: V.tensor_tensor(out=self.y[:, gs], in0=self.y[:, gs], in1=ps[7], op=ALU.add),
                      reads=t_y + [bps[7]], writes=t_y)
                self.mm(ps[2], [(self.Btok[:, g, :], self.xdtd[:, gs])], reads=t_sm + t_xdtd, wtoks=[bps[2]])
                kb.op("dve", lambda g=g, gs=gs: V.tensor_tensor(
                    out=self.S[:, gs].rearrange("p (h d) -> p h d", h=8), in0=self.S[:, gs].rearrange("p (h d) -> p h d", h=8),
                    in1=bc_last(cdec[:, 8 * g:8 * g + 8], 64), op=ALU.mult), reads=[self.b_S[g]] + t_sm, writes=[self.b_S[g]])
                kb.op("dve", lambda gs=gs: V.tensor_tensor(out=self.S[:, gs], in0=self.S[:, gs], in1=ps[2], op=ALU.add),
                      reads=[self.b_S[g], bps[2]], writes=[self.b_S[g]])
                kb.op("act", lambda gs=gs: S_.activation(out=self.Sbf[:, gs], in_=self.S[:, gs], func=AF.Copy), reads=[self.b_S[g]], writes=t_Sbf)
            kb.op("dve", lambda b=b: V.tensor_tensor(out=self.y[:], in0=self.y[:], in1=self.zs[:, b, :], op=ALU.mult), reads=t_y + t_zs, writes=t_y)
            ssq, rg = sv[:, 8, 0:4], sv[:, 9, 0:4]
            kb.op("dve", lambda: V.memset(ssq, 0.0), writes=t_sm)
            for g in range(4):
                j = self.rr("sg", 2)
                kb.op("act", lambda g=g, j=j: S_.activation(out=self.sg[:, j, :], in_=self.y[:, g * 512:(g + 1) * 512], func=AF.Square,
                                                            accum_out=ssq[:, g:g + 1]), reads=t_y + t_sm, writes=[self.b_sg[j]] + t_sm)
            kb.op("act", lambda: S_.activation(out=rg, in_=ssq, func=AF.Sqrt, scale=1.0 / 512, bias=self.epsc[:, 0:1]), reads=t_sm + [self.b_const], writes=t_sm)
            kb.op("dve", lambda: V.reciprocal(out=rg, in_=rg), reads=t_sm, writes=t_sm)
            for g in range(4):
                gs = slice(g * 512, (g + 1) * 512)
                kb.op("dve", lambda g=g, gs=gs: V.scalar_tensor_tensor(out=self.ynb[:, gs], in0=self.y[:, gs], scalar=rg[:, g:g + 1], in1=self.snorm[:, gs],
                                                                       op0=ALU.mult, op1=ALU.mult), reads=t_y + t_sm + [bl], writes=t_ynb)
            psY = self.psbf(5, 2)

            def tr_y():
                ins = None
                for c in range(C):
                    ins = nc.tensor.transpose(out=psY[:, c * 128:(c + 1) * 128], in_=self.ynb[:, c * 128:(c + 1) * 128], identity=self.ident_bf)
                return ins
            kb.op("pe", tr_y, reads=t_ynb + [self.b_const], writes=[bps[5], bps[6]])
            kb.op("act", lambda blk=blk: S_.activation(out=self.ynT[:, 0:8, blk], in_=psY[:, 0:1024].rearrange("p (c n) -> p c n", c=8), func=AF.Copy),
                  reads=[bps[5]], writes=t_ynT)
            kb.op("dve", lambda blk=blk: V.tensor_copy(out=self.ynT[:, 8:16, blk], in_=psY[:, 1024:2048].rearrange("p (c n) -> p c n", c=8)),
                  reads=[bps[6]], writes=t_ynT)
        self.stage("ssd")
        dist = self.acon[:, 0:256]
        for b in range(4):
            blk = slice(b * 128, (b + 1) * 128)
            mask = self.acon[:, 512:768] if (t == 0 and b == 0) else self.acon[:, 256:512]
            for g in range(4):
                k2 = self.rr("pS", 2)
                psS = self.psall[:, 2 * k2:2 * k2 + 2, :].rearrange("p a (h k) -> p (a h) k", h=2)
                bS = [bps[2 * k2], bps[2 * k2 + 1]]

                def sc(b=b, g=g, psS=psS):
                    ins = None
                    for r in range(4):
                        hf = slice((r % 2) * 64, (r % 2) * 64 + 64)
                        ch = 2 * g + r // 2
                        ins = nc.tensor.matmul(psS[:, (r % 2) * 2 + r // 2, :], lhsT=self.qT[hf, ch, b * 128:(b + 1) * 128],
                                               rhs=self.kT[hf, g, b * 128:b * 128 + 256], start=True, stop=True)
                    return ins
                kb.op("pe", sc, reads=t_qT + t_kT, writes=bS)
                i = self.rr("ssb", 2)
                ssb = self.ssb[i]
                for r in range(4):
                    kb.op("dve", lambda r=r, ssb=ssb, psS=psS, g=g: V.scalar_tensor_tensor(
                        out=ssb[:, r, 0:256], in0=dist, scalar=-SLOPES[4 * g + r], in1=psS[:, (r % 2) * 2 + r // 2, :], op0=ALU.mult, op1=ALU.add),
                        reads=bS + [self.b_const], writes=t_at)
                kb.op("dve", lambda ssb=ssb, mask=mask: V.tensor_tensor(out=ssb[:, :, 0:256], in0=ssb[:, :, 0:256], in1=bc_mid(mask, 4), op=ALU.add),
                      reads=t_at + [self.b_const], writes=t_at)
                kb.op("dve", lambda ssb=ssb, g=g: V.tensor_copy(out=ssb[:, :, 256:257], in_=self.sinkc[:, 4 * g:4 * g + 4].unsqueeze(2)),
                      reads=[bl], writes=t_at)
                mx, sm, rs = self.ast[:, 0, 0:4], self.ast[:, 0, 4:8], self.ast[:, 0, 8:12]
                kb.op("dve", lambda ssb=ssb: V.tensor_reduce(out=mx, in_=ssb[:, :, 0:257], axis=AX.X, op=ALU.max), reads=t_at, writes=t_at)
                kb.op("dve", lambda ssb=ssb: V.tensor_tensor(out=ssb[:, :, 0:257], in0=ssb[:, :, 0:257], in1=bc_last(mx, 257), op=ALU.subtract),
                      reads=t_at, writes=t_at)
                kb.op("act", lambda ssb=ssb: S_.activation(out=ssb[:, :, 0:257], in_=ssb[:, :, 0:257], func=AF.Exp), reads=t_at, writes=t_at)
                kb.op("dve", lambda ssb=ssb: V.tensor_reduce(out=sm, in_=ssb[:, :, 0:257], axis=AX.X, op=ALU.add), reads=t_at, writes=t_at)
                kb.op("dve", lambda: V.reciprocal(out=rs, in_=sm), reads=t_at, writes=t_at)
                pp = self.pp[i]
                kb.op("dve", lambda ssb=ssb, pp=pp: V.tensor_tensor(out=pp[:], in0=ssb[:, :, 0:256], in1=bc_last(rs, 256), op=ALU.mult),
                      reads=t_at, writes=t_at)
                pbT = 4 + self.rr("pT", 2)
                psT = self.psbf(pbT)

                def trp(pp=pp, psT=psT):
                    ins = None
                    for r in range(4):
                        for k2_ in range(2):
                            m = r * 2 + k2_
                            ins = nc.tensor.transpose(out=psT[:, m * 128:(m + 1) * 128], in_=pp[:, r, k2_ * 128:(k2_ + 1) * 128], identity=self.ident_bf)
                    return ins
                kb.op("pe", trp, reads=t_at + [self.b_const], writes=[bps[pbT]])
                pT = self.pT[i]
                kb.op("act", lambda pT=pT, psT=psT: S_.activation(out=pT[:].rearrange("p m n -> p (m n)"), in_=psT[:, 0:1024], func=AF.Copy),
                      reads=[bps[pbT]], writes=t_at)
                pbO = 6 + self.rr("pO", 2)

                def pv(b=b, g=g, pT=pT, pbO=pbO):
                    ins = None
                    for r in range(4):
                        nc.tensor.matmul(ps[pbO][:, r * 128:(r + 1) * 128], lhsT=self.vd[:, b, g, :], rhs=pT[:, 2 * r, :], start=True, stop=False)
                        ins = nc.tensor.matmul(ps[pbO][:, r * 128:(r + 1) * 128], lhsT=self.vd[:, b + 1, g, :], rhs=pT[:, 2 * r + 1, :], start=False, stop=True)
                    return ins
                kb.op("pe", pv, reads=t_at + t_vd, writes=[bps[pbO]])
                pO = ps[pbO].rearrange("p (r n) -> p r n", r=4)
                kb.op("act", lambda g=g, blk=blk, pO=pO: S_.activation(out=self.qT[0:64, 2 * g:2 * g + 2, blk], in_=pO[0:64, 0::2, :], func=AF.Copy),
                      reads=[bps[pbO]], writes=t_qT)
                kb.op("dve", lambda g=g, blk=blk, pO=pO: V.tensor_copy(out=self.qT[64:128, 2 * g:2 * g + 2, blk], in_=pO[64:128, 1::2, :]),
                      reads=[bps[pbO]], writes=t_qT)
        self.stage("attn")
        kb.op("act", lambda: S_.activation(out=self.kT[:, :, 0:128], in_=self.kT[:, :, 512:640], func=AF.Copy), reads=t_kT, writes=t_kT)
        kb.op("act", lambda: S_.activation(out=self.vd[:, 0].rearrange("p k n -> p (k n)"), in_=self.vd[:, 4].rearrange("p k n -> p (k n)"), func=AF.Copy),
              reads=t_vd, writes=t_vd)
        oT = self.qT
        mg = self.xsT
        for ph in range(2):
            for sl in range(4):
                if ph == 0:
                    i1 = self.load_slab(self.w_ao[l], 8, sl * 512, 512)
                    i2 = self.load_slab(win, C, 6688 + sl * 512, 512)
                    kcs, act, t_act = 8, oT, t_qT
                else:
                    i1 = self.load_slab(self.w_so[l], C, sl * 512, 512)
                    i2 = self.load_slab(win, C, 8736 + sl * 512, 512)
                    kcs, act, t_act = C, self.ynT, t_ynT
                w1, w1t = self.wA[i1], self.tok(self.o_wA[i1], 16 * K)
                w2, w2t = self.wA[i2], self.tok(self.o_wA[i2], 16 * K)
                for cc in range(4):
                    c = sl * 4 + cc
                    pa = self.rr("pa", 2)
                    pg = 2 + self.rr("pg", 2)
                    self.mm(ps[pa], [(w1[:, kc, cc * 128:(cc + 1) * 128], act[:, kc, :]) for kc in range(kcs)], reads=w1t + t_act, wtoks=[bps[pa]])
                    self.mm(ps[pg], [(w2[:, kc, cc * 128:(cc + 1) * 128], nT[:, kc, :]) for kc in range(C)], reads=w2t + t_nT, wtoks=[bps[pg]])
                    j = self.rr("sg", 2)
                    kb.op("act", lambda j=j, pg=pg: S_.activation(out=self.sg[:, j, :], in_=ps[pg], func=AF.Sigmoid), reads=[bps[pg]], writes=[self.b_sg[j]])
                    mc, mct = self.mrg(c)
                    if ph == 0:
                        kb.op("dve", lambda j=j, pa=pa, mc=mc: V.tensor_tensor(out=mc, in0=self.sg[:, j, :], in1=ps[pa], op=ALU.mult),
                              reads=[self.b_sg[j], bps[pa]], writes=mct)
                    else:
                        kb.op("dve", lambda j=j, pa=pa: V.tensor_tensor(out=self.sg[:, j, :], in0=self.sg[:, j, :], in1=ps[pa], op=ALU.mult),
                              reads=[self.b_sg[j], bps[pa]], writes=[self.b_sg[j]])
                        kb.op("dve", lambda j=j, mc=mc, c=c: V.tensor_tensor(out=mg[:, c, :], in0=self.sg[:, j, :], in1=mc, op=ALU.add),
                              reads=[self.b_sg[j]] + mct, writes=t_xsT)
        self.stage("merge")
        tT = self.carve(self.o_zs, F32, [128, C, NT])
        pend = [self.load_slab(self.w_o[l], C, 0, 512)]
        for sl in range(4):
            if sl + 1 < 4:
                pend.append(self.load_slab(self.w_o[l], C, (sl + 1) * 512, 512))
            wb, wt = next_slab()
            for cc in range(4):
                c = sl * 4 + cc
                pf = 4 + self.rr("pf", 2)
                self.mm(ps[pf], [(wb[:, kc, cc * 128:(cc + 1) * 128], mg[:, kc, :]) for kc in range(C)], reads=wt + t_xsT, wtoks=[bps[pf]])
                kb.op("dve", lambda c=c, pf=pf: V.tensor_copy(out=tT[:, c, :], in_=ps[pf]), reads=[bps[pf]], writes=self.tok(self.o_zs + c * 2048, 2048))
                i = self.rr("sqf", 2)
                kb.op("act", lambda i=i, pf=pf: S_.activation(out=self.sqf[:, i, :], in_=ps[pf], func=AF.Square), reads=[bps[pf]], writes=[self.b_sqf[i]])
                kb.op("pe", lambda i=i, c=c: nc.tensor.matmul(ps[7], lhsT=self.ones_bf, rhs=self.sqf[:, i, :], start=(c == 0), stop=(c == C - 1)),
                      reads=[self.b_sqf[i], self.b_const], writes=[bps[7]])
        self.emit_postnorm_store(1, t, self.o_zs, 7)


def _fm(v):
    v = np.asarray(v, np.float32)
    k = v.shape[-1] // 128
    return np.ascontiguousarray(v.reshape(k, 128).T)


def _consts():
    f32 = np.float32
    tri = np.triu(np.ones((128, 128), f32))
    cm = np.concatenate([tri, np.eye(128, dtype=f32), np.ones((128, 128), f32), -np.ones((128, 128), f32)], 1)
    q = np.arange(128)[:, None]
    k = np.arange(256)[None, :]
    dist = (128 + q - k).astype(f32)
    valid = (dist >= 0) & (dist < 128)
    mask = np.where(valid, 0.0, NEG).astype(f32)
    mask0 = np.where(valid & (k >= 128), 0.0, NEG).astype(f32)
    acon = np.concatenate([dist, mask, mask0], 1)
    return np.ascontiguousarray(cm), np.ascontiguousarray(acon)


def make_in_map(x_b, c_b, w, L):
    f32 = np.float32
    cm, acon = _consts()
    bc = lambda v: np.ascontiguousarray(np.broadcast_to(np.asarray(v, f32)[None, :], (128, v.shape[-1])))
    m = dict(
        xT=np.ascontiguousarray(np.asarray(x_b, f32).T), cT=_fm(c_b),
        w_mod=w["w_mod"], b_modT=np.stack([_fm(w["b_mod"][l]) for l in range(L)]),
        npreT=np.stack([_fm(w["norm_pre"][l].reshape(-1)) for l in range(L)]),
        npostT=np.stack([_fm(w["norm_post"][l].reshape(-1)) for l in range(L)]),
        w_gate=w["w_ffn_gate"], w_up=w["w_ffn_up"], w_down=w["w_ffn_down"], w_in=w["w_in"],
        w_ao=w["w_attn_out"], w_so=w["w_ssm_out"], w_o=w["w_out"],
        acon=acon, cmats=cm,
        sinkc=np.stack([bc(w["attn_sinks"][l]) for l in range(L)]),
        convwT=np.stack([np.ascontiguousarray(w["conv_w"][l].reshape(4, 24, 128).transpose(2, 1, 0).reshape(128, 96)) for l in range(L)]),
        convbT=np.stack([_fm(w["conv_b"][l]) for l in range(L)]),
        hvec=np.stack([bc(np.concatenate([w["dt_bias"][l], w["a_log"][l], w["d_skip"][l]])) for l in range(L)]),
        snorm=np.stack([bc(w["ssm_norm"][l]) for l in range(L)]),
    )
    return {k: np.ascontiguousarray(v, dtype=f32) for k, v in m.items()}


_PROG = {}


def kernel(**inputs):
    x = np.asarray(inputs["x"], np.float32)
    B, T, _ = x.shape
    L = inputs["w_mod"].shape[0]
    w = {k: np.asarray(v, np.float32) for k, v in inputs.items() if k not in ("x", "c")}
    key = (T, L)
    if key not in _PROG:
        _PROG[key] = Prog(T, L)
    p = _PROG[key]
    in_maps = [make_in_map(x[b], inputs["c"][b], w, L) for b in range(B)]
    res = run_bass_kernel_spmd(p.nc, in_maps, core_ids=list(range(B)))
    out = np.stack([np.ascontiguousarray(res.results[b]["yT"].T) for b in range(B)])
    return out.astype(np.float32)
```

```python
import numpy as np
import concourse.bass as bass
import concourse.mybir as mybir
from concourse.bass_utils import run_bass_kernel_spmd

F32 = mybir.dt.float32
BF16 = mybir.dt.bfloat16
AF = mybir.ActivationFunctionType
ALU = mybir.AluOpType
AX = mybir.AxisListType

D = 2048
C = 16
DFF = 5632
JF = 44
NT = 512
P_IN = 10784
EPS = 1e-6
NEG = -30000.0


class Buf:
    __slots__ = ("name", "w", "r", "dsem", "dcnt", "excl")

    def __init__(self, name):
        self.name = name
        self.excl = False
        self.w = None
        self.r = []
        self.dsem = None
        self.dcnt = 0


class KB:
    def __init__(self):
        nc = bass.Bass("TRN2", target_bir_lowering=False)
        self.nc = nc
        self.E = dict(pe=nc.tensor, act=nc.scalar, dve=nc.vector, pool=nc.gpsimd, sp=nc.sync)
        self.esem = {e: nc.alloc_semaphore("es_" + e) for e in ("pe", "act", "dve", "pool")}
        self.ecnt = {e: 0 for e in self.esem}
        self.known = {e: {} for e in self.E}
        self.out_events = []
        self.nbuf = 0

    def buf(self, name=None):
        self.nbuf += 1
        return Buf(name or f"b{self.nbuf}")

    def bufs(self, n, name="b"):
        return [self.buf(f"{name}{i}") for i in range(n)]

    def _waits(self, e, reads, writes):
        deps = {}

        def add(ev):
            if ev is None:
                return
            s, v = ev
            k = id(s)
            if k not in deps or deps[k][1] < v:
                deps[k] = (s, v)

        for b in reads:
            add(b.w)
            if b.excl:
                for ev in b.r:
                    add(ev)
        for b in writes:
            add(b.w)
            for ev in b.r:
                add(ev)
        kn = self.known[e]
        pes = self.esem["pe"]
        for k, (s, v) in deps.items():
            if e == "pe" and s is pes:
                continue
            if kn.get(k, 0) >= v:
                continue
            self.E[e].wait_ge(s, v)
            kn[k] = v

    def _commit(self, ev, reads, writes):
        for b in writes:
            b.w = ev
            b.r = []
        for b in reads:
            if b not in writes:
                b.r.append(ev)
                if len(b.r) > 12:
                    m = {}
                    for s, v in b.r:
                        if id(s) not in m or m[id(s)][1] < v:
                            m[id(s)] = (s, v)
                    b.r = list(m.values())

    def op(self, e, fn, reads=(), writes=()):
        self._waits(e, reads, writes)
        ins = fn()
        ins.then_inc(self.esem[e], 1)
        self.ecnt[e] += 1
        ev = (self.esem[e], self.ecnt[e])
        self._commit(ev, reads, writes)
        return ev

    def dma(self, q, out, in_, sembuf, reads=(), writes=(), is_output=False):
        self._waits(q, reads, writes)
        if sembuf.dsem is None:
            sembuf.dsem = self.nc.alloc_semaphore("ds_" + sembuf.name)
        self.E[q].dma_start(out=out, in_=in_).then_inc(sembuf.dsem, 16)
        sembuf.dcnt += 1
        ev = (sembuf.dsem, 16 * sembuf.dcnt)
        self._commit(ev, reads, writes)
        if is_output:
            self.out_events.append(ev)
        return ev

    def finish(self):
        m = {}
        for s, v in self.out_events:
            if id(s) not in m or m[id(s)][1] < v:
                m[id(s)] = (s, v)
        for s, v in m.values():
            self.E["sp"].wait_ge(s, v)


def bc_mid(ap, n):
    p, f = ap.shape
    return ap.unsqueeze(1).broadcast_to([p, n, f])


def bc_last(ap, n):
    p, g = ap.shape
    return ap.unsqueeze(2).broadcast_to([p, g, n])


SLOPES = [2.0 ** (-8.0 * (h + 1) / 16.0) for h in range(16)]
KB1 = 1024


class _Stop(Exception):
    pass


class Prog:
    def __init__(self, T, depth, do_ffn=True, do_mix=True, dbg=None):
        self.dbg = dbg
        self.T = T
        self.depth = depth
        self.do_ffn = do_ffn
        self.do_mix = do_mix
        self.NTILES = T // NT
        kb = KB()
        self.kb = kb
        nc = kb.nc
        self.nc = nc
        dt = nc.dram_tensor
        L = depth
        self.xT = dt("xT", [D, T], F32, kind="ExternalInput").ap()
        self.yT = dt("yT", [D, T], F32, kind="ExternalOutput").ap()
        self.cT = dt("cT", [128, C], F32, kind="ExternalInput").ap()
        self.w_mod = dt("w_mod", [L, D, 9 * D], F32, kind="ExternalInput").ap()
        self.b_modT = dt("b_modT", [L, 128, 9 * C], F32, kind="ExternalInput").ap()
        self.npreT = dt("npreT", [L, 128, 3 * C], F32, kind="ExternalInput").ap()
        self.npostT = dt("npostT", [L, 128, 3 * C], F32, kind="ExternalInput").ap()
        self.w_gate = dt("w_gate", [L, 2, D, DFF], F32, kind="ExternalInput").ap()
        self.w_up = dt("w_up", [L, 2, D, DFF], F32, kind="ExternalInput").ap()
        self.w_down = dt("w_down", [L, 2, DFF, D], F32, kind="ExternalInput").ap()
        self.w_in = dt("w_in", [L, D, P_IN], F32, kind="ExternalInput").ap()
        self.w_ao = dt("w_ao", [L, 1024, D], F32, kind="ExternalInput").ap()
        self.w_so = dt("w_so", [L, D, D], F32, kind="ExternalInput").ap()
        self.w_o = dt("w_o", [L, D, D], F32, kind="ExternalInput").ap()
        self.acon_d = dt("acon", [128, 3 * 256], F32, kind="ExternalInput").ap()
        self.sink_d = dt("sinkc", [L, 128, 16], F32, kind="ExternalInput").ap()
        self.convw_d = dt("convwT", [L, 128, 24 * 4], F32, kind="ExternalInput").ap()
        self.convb_d = dt("convbT", [L, 128, 24], F32, kind="ExternalInput").ap()
        self.hvec_d = dt("hvec", [L, 128, 3 * 32], F32, kind="ExternalInput").ap()
        self.snorm_d = dt("snorm", [L, 128, D], F32, kind="ExternalInput").ap()
        self.cmats_d = dt("cmats", [128, 4 * 128], F32, kind="ExternalInput").ap()
        self.wgb = dt("wgb", [L, 2, D, DFF], BF16).ap()
        self.wub = dt("wub", [L, 2, D, DFF], BF16).ap()
        self.wdb = dt("wdb", [L, 2, DFF, D], BF16).ap()
        self.b_cast = [[kb.bufs(3, f"cast{l}_{f}_") for f in range(2)] for l in range(L)]
        self.cur = self.xT
        self.b_y = [kb.bufs(C, f"y{t}_") for t in range(self.NTILES)]
        self._alloc()
        self._emit()
        kb.finish()

    def carve(self, off, dtype, shape):
        esz = 4 if dtype == F32 else 2
        n = 1
        for d in shape[1:]:
            n *= d
        assert off % 4 == 0 and off + n * esz <= self.ARENA, (off, n * esz, self.ARENA)
        v = self.arena[:, off:off + n * esz].bitcast(dtype)
        if len(shape) == 3:
            v = v.rearrange("p (a b) -> p a b", a=shape[1])
        elif len(shape) == 4:
            v = v.rearrange("p (a b c) -> p a b c", a=shape[1], b=shape[2])
        return v

    def tok(self, off, nbytes):
        return self.b_ar[off // 2048:(off + nbytes + 2047) // 2048]

    def _alloc(self):
        nc, kb = self.nc, self.kb
        A = nc.alloc_sbuf_tensor
        K = KB1
        self.cm = A("cm", [128, 4 * 128], F32)
        self.cm_bf = A("cm_bf", [128, 4 * 128], BF16)
        self.tri = self.cm[:, 0:128]
        self.ident = self.cm[:, 128:256]
        self.ones = self.cm[:, 256:384]
        self.nones = self.cm[:, 384:512]
        self.ident_bf = self.cm_bf[:, 128:256]
        self.ones_bf = self.cm_bf[:, 256:384]
        self.epsc = A("epsc", [128, 1], F32)
        self.cact = A("cact", [128, C], BF16)
        self.ctmp = A("ctmp", [128, C], F32)
        self.mod = A("mod", [128, 9 * C], F32)
        self.bmod = A("bmod", [128, 9 * C], F32)
        self.npre = A("npre", [128, 3 * C], F32)
        self.npost = A("npost", [128, 3 * C], F32)
        self.gsc = A("gsc", [128, 3 * C], F32)
        self.gp = A("gp", [128, 3 * C], F32)
        self.b_const = kb.buf("const")
        self.b_mod = kb.buf("mod")
        self.b_cact = kb.buf("cact")
        self.rstd = A("rstd", [128, NT], F32)
        self.b_rstd = kb.buf("rstd")
        self.rstd2 = A("rstd2", [128, NT], F32)
        self.b_rstd2 = kb.buf("rstd2")
        self.sq = A("sq", [128, 2, 2 * NT], BF16)
        self.b_sq = kb.bufs(2, "sq")
        self.tmp = A("tmp", [128, 2, NT], F32)
        self.b_tmp = kb.bufs(2, "tmp")
        self.xr = A("xr", [128, 2, NT], F32)
        self.b_xr = kb.bufs(2, "xr")
        self.sg = A("sg", [128, 2, NT], F32)
        self.b_sg = kb.bufs(2, "sg")
        self.sqf = A("sqf", [128, 2, NT], BF16)
        self.b_sqf = kb.bufs(2, "sqf")
        self.acon = A("acon_sb", [128, 3 * 256], F32)
        self.sinkc = A("sink_sb", [128, 16], F32)
        self.convw = A("convw_sb", [128, 24 * 4], F32)
        self.convb = A("convb_sb", [128, 24], F32)
        self.hv = A("hv_sb", [128, 96], F32)
        self.na = A("na_sb", [128, 32], F32)
        self.snorm = A("snorm_sb", [128, D], F32)
        self.S = A("S_sb", [128, D], F32)
        self.halo = A("halo_sb", [128, 24, 4], F32)
        self.b_lay = kb.buf("layer_consts")
        self.b_S = kb.bufs(4, "S")
        self.b_halo = kb.bufs(24, "halo")
        self.psall = nc.alloc_psum_tensor("psall", [128, 8, 512], F32)
        self.ps = [self.psall[:, i, :] for i in range(8)]
        self.b_ps = kb.bufs(8, "ps")
        for b in self.b_ps:
            b.excl = True
        self.ARENA = (nc.sbuf_bytes_remaining // 1024 - 1) * 1024
        self.arena = A("arena", [128, self.ARENA], mybir.dt.uint8)
        self.b_ar = kb.bufs(self.ARENA // 2048 + 1, "ar")
        self.o_hT = 0
        self.hT = self.carve(0, BF16, [128, JF, NT])
        self.o_fT = 44 * K
        self.fT = self.carve(self.o_fT, F32, [128, C, NT])
        self.o_wA = [76 * K, 92 * K]
        self.wA = [self.carve(o, BF16, [128, C, 512]) for o in self.o_wA]
        self.o_wB = [108 * K, 130 * K]
        self.wB = [self.carve(o, BF16, [128, JF, 256]) for o in self.o_wB]
        assert 152 * K <= self.ARENA, self.ARENA
        self.o_qT = 16 * K
        self.qT = self.carve(self.o_qT, BF16, [128, 8, NT])
        self.o_kT = 24 * K
        self.kT = self.carve(self.o_kT, BF16, [128, 4, 640])
        self.o_vd = 30 * K
        self.vd = self.carve(self.o_vd, BF16, [128, 5, 4, 128])
        self.o_BT = 36 * K
        self.BT = self.carve(self.o_BT, BF16, [128, 4, NT])
        self.o_CT = 40 * K
        self.CT = self.carve(self.o_CT, BF16, [128, 4, NT])
        self.o_dt = 44 * K
        self.dtv = self.carve(self.o_dt, F32, [128, 4, 32])
        self.o_xsT = 46 * K
        self.xsT = self.carve(self.o_xsT, BF16, [128, C, NT])
        self.o_zs = 108 * K
        self.zs = self.carve(self.o_zs, BF16, [128, 4, D])
        self.o_ynT = 124 * K
        self.ynT = self.carve(self.o_ynT, BF16, [128, C, NT])
        self.o_xdt = 140 * K
        self.xdt = self.carve(self.o_xdt, BF16, [128, D])
        self.o_xdtd = 144 * K
        self.xdtd = self.carve(self.o_xdtd, BF16, [128, D])
        self.o_y = 148 * K
        self.y = self.carve(self.o_y, F32, [128, D])
        self.o_ynb = 140 * K
        self.ynb = self.carve(self.o_ynb, BF16, [128, D])
        assert 159 * K <= self.ARENA, self.ARENA
        self.o_Sbf = 62 * K
        self.Sbf = self.carve(self.o_Sbf, BF16, [128, D])
        self.o_Btok = 66 * K
        self.Btok = self.carve(self.o_Btok, BF16, [128, 4, 128])
        self.o_MT = 67 * K
        self.MT = self.carve(self.o_MT, BF16, [128, 4, 128])
        self.o_dat = 68 * K
        self.dat = self.carve(self.o_dat, F32, [128, 4, 128])
        self.o_Dm = 70 * K
        self.Dm = self.carve(self.o_Dm, F32, [128, 4, 128])
        self.o_CBm = 72 * K
        self.CBm = self.carve(self.o_CBm, F32, [128, 4, 128])
        self.o_sv = 74 * K
        self.sv = self.carve(self.o_sv, F32, [128, 16, 32])
        self.o_ssb = [140 * K, 140 * K + 4160]
        self.ssb = [self.carve(o, F32, [128, 4, 260]) for o in self.o_ssb]
        self.o_p = [150 * K, 152 * K]
        self.pp = [self.carve(o, BF16, [128, 4, 256]) for o in self.o_p]
        self.o_pT = [154 * K, 156 * K]
        self.pT = [self.carve(o, BF16, [128, 8, 128]) for o in self.o_pT]
        self.o_ast = 158 * K
        self.ast = self.carve(self.o_ast, F32, [128, 2, 16])
        self.ring = {}

    def rr(self, key, n):
        i = self.ring.get(key, 0)
        self.ring[key] = i + 1
        return i % n

    def mrg(self, c):
        if c < 7:
            o = 62 * KB1 + c * 2048
        else:
            o = 140 * KB1 + (c - 7) * 2048
        return self.carve(o, F32, [128, NT]), self.tok(o, 2048)

    def mm(self, out, pairs, reads, wtoks, f32=False):
        nc = self.nc
        n = len(pairs)

        def f_():
            ins = None
            for i, (l, r) in enumerate(pairs):
                ins = nc.tensor.matmul(out, lhsT=l, rhs=r, start=(i == 0), stop=(i == n - 1))
            return ins
        return self.kb.op("pe", f_, reads=reads, writes=wtoks)

    def rsqrt_mean(self, out, in_, n, rtoks, wtoks):
        nc = self.nc
        self.kb.op("act", lambda: nc.scalar.activation(out=out, in_=in_, func=AF.Sqrt, scale=1.0 / n, bias=self.epsc[:, 0:1]),
                   reads=rtoks + [self.b_const], writes=wtoks)
        self.kb.op("dve", lambda: nc.vector.reciprocal(out=out, in_=out), reads=wtoks, writes=wtoks)

    def load_slab(self, src, kcs, c0, ncols, dst_off=0):
        i = self.rr("wA", 2)
        self._slab_into(i, src, kcs, c0, ncols, dst_off)
        return i

    def _slab_into(self, i, src, kcs, c0, ncols, dst_off=0):
        wb = self.wA[i]
        toks = self.tok(self.o_wA[i], 16 * KB1)
        sv = src.rearrange("(kc p) n -> p kc n", p=128)
        step = 4 if ncols >= 512 else 8
        for k0 in range(0, kcs, step):
            k1 = min(kcs, k0 + step)
            self.kb.dma("pool", wb[:, k0:k1, dst_off:dst_off + ncols], sv[:, k0:k1, c0:c0 + ncols], toks[0], writes=toks)

    def _emit(self):
        kb, nc = self.kb, self.nc
        kb.dma("sp", self.cm[:], self.cmats_d[:, :], self.b_const, writes=[self.b_const])
        kb.op("dve", lambda: nc.vector.tensor_copy(out=self.cm_bf[:], in_=self.cm[:]),
              reads=[self.b_const], writes=[self.b_const])
        kb.dma("sp", self.acon[:], self.acon_d[:, :], self.b_const, writes=[self.b_const])
        kb.op("dve", lambda: nc.vector.memset(self.epsc[:], EPS), writes=[self.b_const])
        kb.dma("sp", self.ctmp[:], self.cT[:, :], self.b_cact, writes=[self.b_cact])
        kb.op("act", lambda: nc.scalar.activation(out=self.cact[:], in_=self.ctmp[:], func=AF.Silu),
              reads=[self.b_cact], writes=[self.b_cact])
        try:
            self._emit_layers()
        except _Stop:
            kb.op("dve", lambda: nc.vector.memset(self.rstd[:], 1.0), writes=[self.b_rstd])
            kb.dma("sp", self.yT[0:128, 0:NT], self.rstd[:], self.b_rstd, reads=[self.b_rstd], is_output=True)

    def stage(self, name):
        if self.dbg == name:
            raise _Stop()

    def _emit_layers(self):
        kb, nc = self.kb, self.nc
        for l in range(self.depth):
            self.emit_mod(l)
            if self.dbg == "mod":
                kb.dma("sp", self.yT[0:128, 0:144], self.mod[:], self.b_mod, reads=[self.b_mod], is_output=True)
                return
            if self.do_ffn:
                if l == 0:
                    self.emit_cast(0)
                self.emit_ffn(l, 0)
            if self.do_mix:
                self.emit_mixer(l)
            if self.do_ffn:
                if l + 1 < self.depth:
                    self.emit_cast(l + 1)
                self.emit_ffn(l, 1)

    def emit_cast(self, l):
        kb = self.kb
        for f in range(2):
            for m, (src, dst, rows) in enumerate(((self.w_gate, self.wgb, D), (self.w_up, self.wub, D), (self.w_down, self.wdb, DFF))):
                b = self.b_cast[l][f][m]
                for r0 in range(0, rows, 256):
                    kb.dma("pool", dst[l, f, r0:r0 + 256, :], src[l, f, r0:r0 + 256, :], b, writes=[b])

    def emit_mod(self, l):
        kb, nc = self.kb, self.nc
        V = nc.vector
        kb.dma("sp", self.bmod[:], self.b_modT[l], self.b_mod, writes=[self.b_mod])
        kb.dma("sp", self.npre[:], self.npreT[l], self.b_mod, writes=[self.b_mod])
        kb.dma("sp", self.npost[:], self.npostT[l], self.b_mod, writes=[self.b_mod])
        pb = 7
        nslab = 9 * D // 512
        pend = [self.load_slab(self.w_mod[l], C, 0, 512)]
        for s in range(nslab):
            if s + 1 < nslab:
                pend.append(self.load_slab(self.w_mod[l], C, (s + 1) * 512, 512))
            i = pend.pop(0)
            wb = self.wA[i]
            toks = self.tok(self.o_wA[i], 16 * KB1)
            for cc in range(4):
                m = s * 4 + cc
                self.mm(self.ps[pb][:, m:m + 1],
                        [(wb[:, kc, cc * 128:(cc + 1) * 128], self.cact[:, kc:kc + 1]) for kc in range(C)],
                        reads=toks + [self.b_cact], wtoks=[self.b_ps[pb]])
        kb.op("dve", lambda: V.tensor_tensor(out=self.mod[:], in0=self.ps[pb][:, 0:9 * C], in1=self.bmod[:], op=ALU.add),
              reads=[self.b_ps[pb], self.b_mod], writes=[self.b_mod])
        for s in range(3):
            sc = self.mod[:, (3 * s + 1) * C:(3 * s + 2) * C]
            g = self.mod[:, (3 * s + 2) * C:(3 * s + 3) * C]
            kb.op("dve", lambda s=s, sc=sc: V.scalar_tensor_tensor(
                out=self.gsc[:, s * C:(s + 1) * C], in0=sc, scalar=1.0, in1=self.npre[:, s * C:(s + 1) * C],
                op0=ALU.add, op1=ALU.mult), reads=[self.b_mod], writes=[self.b_mod])
            kb.op("dve", lambda s=s, g=g: V.scalar_tensor_tensor(
                out=self.gp[:, s * C:(s + 1) * C], in0=g, scalar=(1.0 if s == 1 else 0.5),
                in1=self.npost[:, s * C:(s + 1) * C], op0=ALU.mult, op1=ALU.mult),
                reads=[self.b_mod], writes=[self.b_mod])

    def emit_prenorm(self, s, t, o_xt, o_nT):
        kb, nc = self.kb, self.nc
        V, S_ = nc.vector, nc.scalar
        xt = self.carve(o_xt, F32, [128, C, NT])
        nT = self.carve(o_nT, BF16, [128, C, NT])
        src = self.cur.rearrange("(c p) n -> p c n", p=128)
        for h in range(4):
            tk = self.tok(o_xt + 4 * h * 2048, 4 * 2048)
            kb.dma("sp", xt[:, 4 * h:4 * h + 4, :], src[:, 4 * h:4 * h + 4, t * NT:(t + 1) * NT],
                   tk[0], reads=self.b_y[t][4 * h:4 * h + 4], writes=tk)
        pb = 6
        for h in range(8):
            i = self.rr("sq", 2)
            tk = self.tok(o_xt + 2 * h * 2048, 2 * 2048)
            kb.op("act", lambda h=h, i=i: S_.activation(
                out=self.sq[:, i, :], in_=xt[:, 2 * h:2 * h + 2, :].rearrange("p c n -> p (c n)"), func=AF.Square),
                reads=tk, writes=[self.b_sq[i]])

            def mm(h=h, i=i):
                ins = None
                for cc in range(2):
                    ins = nc.tensor.matmul(self.ps[pb], lhsT=self.ones_bf, rhs=self.sq[:, i, cc * NT:(cc + 1) * NT],
                                           start=(h == 0 and cc == 0), stop=(h == 7 and cc == 1))
                return ins
            kb.op("pe", mm, reads=[self.b_sq[i], self.b_const], writes=[self.b_ps[pb]])
        self.rsqrt_mean(self.rstd[:], self.ps[pb], D, [self.b_ps[pb]], [self.b_rstd])
        for c in range(C):
            i = self.rr("tmp", 2)
            kb.op("dve", lambda c=c, i=i: V.tensor_tensor(out=self.tmp[:, i, :], in0=xt[:, c, :], in1=self.rstd[:], op=ALU.mult),
                  reads=self.tok(o_xt + c * 2048, 2048) + [self.b_rstd], writes=[self.b_tmp[i]])
            kb.op("act", lambda c=c, i=i: S_.activation(
                out=nT[:, c, :], in_=self.tmp[:, i, :], func=AF.Identity,
                scale=self.gsc[:, s * C + c:s * C + c + 1], bias=self.mod[:, 3 * s * C + c:3 * s * C + c + 1]),
                reads=[self.b_tmp[i], self.b_mod], writes=self.tok(o_nT + c * 1024, 1024))

    def emit_postnorm_store(self, s, t, o_fT, pb2):
        kb, nc = self.kb, self.nc
        V = nc.vector
        fT = self.carve(o_fT, F32, [128, C, NT])
        self.rsqrt_mean(self.rstd2[:], self.ps[pb2], D, [self.b_ps[pb2]], [self.b_rstd2])
        src = self.cur.rearrange("(c p) n -> p c n", p=128)
        dst = self.yT.rearrange("(c p) n -> p c n", p=128)
        for c in range(C):
            j = self.rr("xr", 2)
            kb.dma("sp", self.xr[:, j, :], src[:, c, t * NT:(t + 1) * NT], self.b_xr[j],
                   reads=[self.b_y[t][c]], writes=[self.b_xr[j]])
            i = self.rr("tmp", 2)
            kb.op("dve", lambda c=c, i=i: V.tensor_tensor(out=self.tmp[:, i, :], in0=fT[:, c, :], in1=self.rstd2[:], op=ALU.mult),
                  reads=self.tok(o_fT + c * 2048, 2048) + [self.b_rstd2], writes=[self.b_tmp[i]])
            kb.op("dve", lambda c=c, i=i, j=j: V.scalar_tensor_tensor(
                out=self.xr[:, j, :], in0=self.tmp[:, i, :], scalar=self.gp[:, s * C + c:s * C + c + 1], in1=self.xr[:, j, :],
                op0=ALU.mult, op1=ALU.add), reads=[self.b_tmp[i], self.b_mod, self.b_xr[j]], writes=[self.b_xr[j]])
            kb.dma("sp", dst[:, c, t * NT:(t + 1) * NT], self.xr[:, j, :], self.b_xr[j],
                   reads=[self.b_xr[j]], writes=[self.b_y[t][c]], is_output=True)

    def emit_ffn(self, l, f):
        kb, nc = self.kb, self.nc
        V, S_ = nc.vector, nc.scalar
        s = 0 if f == 0 else 2
        wd = self.w_down[l, f].rearrange("(j p) n -> p j n", p=128)
        NS1 = DFF // 256
        NS2 = D // 256
        o_nT = self.o_fT
        nT = self.carve(o_nT, BF16, [128, C, NT])
        nT_toks = self.tok(o_nT, 16 * KB1)
        hT_toks = self.tok(self.o_hT, 44 * KB1)

        bc = self.b_cast[l][f]
        wgv = self.wgb[l, f].rearrange("(kc p) n -> p kc n", p=128)
        wuv = self.wub[l, f].rearrange("(kc p) n -> p kc n", p=128)
        wdv = self.wdb[l, f].rearrange("(j p) n -> p j n", p=128)

        def load_gu(sl):
            i = self.rr("wA", 2)
            wb = self.wA[i]
            toks = self.tok(self.o_wA[i], 16 * KB1)
            for h in range(2):
                kb.dma("sp", wb[:, 8 * h:8 * h + 8, 0:256], wgv[:, 8 * h:8 * h + 8, sl * 256:(sl + 1) * 256], toks[0], reads=[bc[0]], writes=toks)
            for h in range(2):
                kb.dma("sp", wb[:, 8 * h:8 * h + 8, 256:512], wuv[:, 8 * h:8 * h + 8, sl * 256:(sl + 1) * 256], toks[0], reads=[bc[1]], writes=toks)
            return i

        def load_d(sl):
            i = self.rr("wB", 2)
            wb = self.wB[i]
            toks = self.tok(self.o_wB[i], 22 * KB1)
            for h in range(4):
                kb.dma("sp", wb[:, 11 * h:11 * h + 11, :], wdv[:, 11 * h:11 * h + 11, sl * 256:(sl + 1) * 256], toks[0], reads=[bc[2]], writes=toks)
            return i

        for t in range(self.NTILES):
            pend_gu = [load_gu(0), load_gu(1)]
            self.emit_prenorm(s, t, self.o_hT, o_nT)
            pend_d = []
            for sl in range(NS1):
                wi = pend_gu.pop(0)
                wb = self.wA[wi]
                wtoks = self.tok(self.o_wA[wi], 16 * KB1)
                for jj in range(2):
                    j = sl * 2 + jj
                    pg = self.rr("pgu", 2) * 2
                    pu = pg + 1
                    self.mm(self.ps[pg], [(wb[:, kc, jj * 128:(jj + 1) * 128], nT[:, kc, :]) for kc in range(C)],
                            reads=wtoks + nT_toks, wtoks=[self.b_ps[pg]])
                    self.mm(self.ps[pu], [(wb[:, kc, 256 + jj * 128:256 + (jj + 1) * 128], nT[:, kc, :]) for kc in range(C)],
                            reads=wtoks + nT_toks, wtoks=[self.b_ps[pu]])
                    i = self.rr("sg", 2)
                    kb.op("act", lambda i=i, pg=pg: S_.activation(out=self.sg[:, i, :], in_=self.ps[pg], func=AF.Silu),
                          reads=[self.b_ps[pg]], writes=[self.b_sg[i]])
                    kb.op("dve", lambda i=i, pu=pu, j=j: V.tensor_tensor(out=self.hT[:, j, :], in0=self.sg[:, i, :], in1=self.ps[pu], op=ALU.mult),
                          reads=[self.b_sg[i], self.b_ps[pu]], writes=self.tok(self.o_hT + j * 1024, 1024))
                if sl + 2 < NS1:
                    pend_gu.append(load_gu(sl + 2))
                else:
                    pend_d.append(load_d(len(pend_d)))
            pb2 = 7
            for sl in range(NS2):
                wi = pend_d.pop(0)
                wb = self.wB[wi]
                wtoks = self.tok(self.o_wB[wi], 22 * KB1)
                for ii in range(2):
                    c = sl * 2 + ii
                    pf = 4 + self.rr("pf", 2)
                    self.mm(self.ps[pf], [(wb[:, j, ii * 128:(ii + 1) * 128], self.hT[:, j, :]) for j in range(JF)],
                            reads=wtoks + hT_toks, wtoks=[self.b_ps[pf]])
                    kb.op("dve", lambda c=c, pf=pf: V.tensor_copy(out=self.fT[:, c, :], in_=self.ps[pf]),
                          reads=[self.b_ps[pf]], writes=self.tok(self.o_fT + c * 2048, 2048))
                    i = self.rr("sqf", 2)
                    kb.op("act", lambda i=i, pf=pf: S_.activation(out=self.sqf[:, i, :], in_=self.ps[pf], func=AF.Square),
                          reads=[self.b_ps[pf]], writes=[self.b_sqf[i]])
                    self.mm(self.ps[pb2], [(self.ones_bf, self.sqf[:, i, :])], reads=[self.b_sqf[i], self.b_const], wtoks=[self.b_ps[pb2]]) \
                        if False else kb.op("pe", lambda i=i, c=c: nc.tensor.matmul(
                            self.ps[pb2], lhsT=self.ones_bf, rhs=self.sqf[:, i, :], start=(c == 0), stop=(c == C - 1)),
                            reads=[self.b_sqf[i], self.b_const], writes=[self.b_ps[pb2]])
                if sl + 2 < NS2:
                    pend_d.append(load_d(sl + 2))
            self.emit_postnorm_store(s, t, self.o_fT, pb2)
        self.cur = self.yT

    def psbf(self, b0, nb=1):
        return self.psall[:, b0:b0 + nb, :].rearrange("p a b -> p (a b)").bitcast(BF16)

    def emit_mixer(self, l):
        kb, nc = self.kb, self.nc
        V, S_ = nc.vector, nc.scalar
        K = KB1
        bl = self.b_lay
        for dst, src in ((self.sinkc, self.sink_d), (self.convw, self.convw_d), (self.convb, self.convb_d),
                         (self.hv, self.hvec_d), (self.snorm, self.snorm_d)):
            kb.dma("sp", dst[:], src[l], bl, writes=[bl])
        kb.op("act", lambda: S_.activation(out=self.na[:], in_=self.hv[:, 32:64], func=AF.Exp), reads=[bl], writes=[bl])
        kb.op("dve", lambda: V.tensor_scalar(out=self.na[:], in0=self.na[:], scalar1=-1.0, scalar2=0.0, op0=ALU.mult, op1=ALU.add),
              reads=[bl], writes=[bl])
        kb.op("dve", lambda: V.memset(self.S[:], 0.0), writes=self.b_S)
        kb.op("dve", lambda: V.memset(self.halo[:], 0.0), writes=self.b_halo)
        t_kT = self.tok(self.o_kT, 5 * K)
        t_vd = self.tok(self.o_vd, 5 * K)
        kb.op("dve", lambda: V.memset(self.kT[:, :, 0:128], 0.0), writes=t_kT)
        kb.op("dve", lambda: V.memset(self.vd[:, 0], 0.0), writes=t_vd)
        for t in range(self.NTILES):
            self.mixer_tile(l, t)
        self.cur = self.yT

    def mixer_tile(self, l, t):
        kb, nc = self.kb, self.nc
        V, S_ = nc.vector, nc.scalar
        K = KB1
        bl = self.b_lay
        nT = self.carve(0, BF16, [128, C, NT])
        t_nT = self.tok(0, 16 * K)
        t_qT = self.tok(self.o_qT, 8 * K)
        t_kT = self.tok(self.o_kT, 5 * K)
        t_vd = self.tok(self.o_vd, 5 * K)
        t_BT = self.tok(self.o_BT, 4 * K)
        t_CT = self.tok(self.o_CT, 4 * K)
        t_dt = self.tok(self.o_dt, 2 * K)
        t_xsT = self.tok(self.o_xsT, 16 * K)
        t_zs = self.tok(self.o_zs, 16 * K)
        t_ynT = self.tok(self.o_ynT, 16 * K)
        t_xdt = self.tok(self.o_xdt, 4 * K)
        t_xdtd = self.tok(self.o_xdtd, 4 * K)
        t_y = self.tok(self.o_y, 8 * K)
        t_ynb = self.tok(self.o_ynb, 4 * K)
        t_Sbf = self.tok(self.o_Sbf, 4 * K)
        t_sm = self.tok(self.o_Btok, 10 * K)
        t_at = self.tok(140 * K, 19 * K)
        win = self.w_in[l]
        ps, bps = self.ps, self.b_ps

        pend = [self.load_slab(win, C, 0, 512), self.load_slab(win, C, 512, 512)]
        self.emit_prenorm(1, t, self.o_zs, 0)

        def next_slab():
            i = pend.pop(0)
            return self.wA[i], self.tok(self.o_wA[i], 16 * K)

        def fm_chunk(wb, wt, cc, pb):
            self.mm(ps[pb], [(wb[:, kc, cc * 128:(cc + 1) * 128], nT[:, kc, :]) for kc in range(C)],
                    reads=wt + t_nT, wtoks=[bps[pb]])

        for sl in range(2):
            wb, wt = next_slab()
            for cc in range(4):
                pb = self.rr("p4", 4)
                fm_chunk(wb, wt, cc, pb)
                c = sl * 4 + cc
                kb.op("act", lambda c=c, pb=pb: S_.activation(out=self.qT[:, c, :], in_=ps[pb], func=AF.Copy, scale=0.125),
                      reads=[bps[pb]], writes=t_qT)
            pend.append(self.load_slab(win, C, 1024 if sl == 0 else 1536, 512))
        self.stage("q")
        wb, wt = next_slab()
        for i in range(2):
            pb = self.rr("p4", 4)
            fm_chunk(wb, wt, i, pb)
            kb.op("act", lambda i=i, pb=pb: S_.activation(out=self.kT[0:64, 2 * i, 128:640], in_=ps[pb][0:64, :], func=AF.Copy),
                  reads=[bps[pb]], writes=t_kT)
            kb.op("act", lambda i=i, pb=pb: S_.activation(out=self.kT[64:128, 2 * i + 1, 128:640], in_=ps[pb][64:128, :], func=AF.Copy),
                  reads=[bps[pb]], writes=t_kT)
        self.stage("kcopy")
        for i in range(2):
            kb.dma("sp", self.kT[64:128, 2 * i, 128:640], self.kT[0:64, 2 * i, 128:640], t_kT[0], reads=t_kT, writes=t_kT)
            kb.dma("sp", self.kT[0:64, 2 * i + 1, 128:640], self.kT[64:128, 2 * i + 1, 128:640], t_kT[0], reads=t_kT, writes=t_kT)
        self.stage("kdma")
        for b in range(4):
            pb = self.rr("p4", 4)
            self.mm(ps[pb][:, 0:256], [(nT[:, kc, b * 128:(b + 1) * 128], wb[:, kc, 256:512]) for kc in range(C)],
                    reads=wt + t_nT, wtoks=[bps[pb]])
            kb.op("dve", lambda b=b, pb=pb: V.tensor_copy(
                out=self.vd[:, 1 + b].rearrange("p k (r d) -> p k r d", r=2),
                in_=ps[pb][:, 0:256].rearrange("p (k d) -> p k d", k=4).unsqueeze(2).broadcast_to([128, 4, 2, 64])),
                reads=[bps[pb]], writes=t_vd)
        pend.append(self.load_slab(win, C, 2048, 512))
        self.stage("v")
        for i in range(4):
            wb, wt = next_slab()
            for b in range(4):
                pb = self.rr("p4", 4)
                self.mm(ps[pb], [(nT[:, kc, b * 128:(b + 1) * 128], wb[:, kc, :]) for kc in range(C)],
                        reads=wt + t_nT, wtoks=[bps[pb]])
                kb.op("act", lambda i=i, b=b, pb=pb: S_.activation(out=self.zs[:, b, i * 512:(i + 1) * 512], in_=ps[pb], func=AF.Silu),
                      reads=[bps[pb]], writes=t_zs)
            pend.append(self.load_slab(win, C, [2560, 3072, 3584, 4096][i], 512))
        self.stage("z")
        for i in range(6):
            wb, wt = next_slab()
            for cc in range(4):
                fc = i * 4 + cc
                pb = self.rr("p4", 4)
                fm_chunk(wb, wt, cc, pb)
                u = ps[pb]
                j = self.rr("tmp", 2)
                acc = self.tmp[:, j, :]
                bt = self.b_tmp[j]
                w = lambda k, fc=fc: self.convw[:, fc * 4 + k:fc * 4 + k + 1]
                hl = self.halo[:, fc, :]
                bh = self.b_halo[fc]
                kb.op("act", lambda acc=acc, u=u, fc=fc: S_.activation(out=acc, in_=u, func=AF.Identity, scale=self.convw[:, fc * 4 + 3:fc * 4 + 4],
                                                                       bias=self.convb[:, fc:fc + 1]), reads=[bps[pb], bl], writes=[bt])
                for k, sh in ((2, 1), (1, 2), (0, 3)):
                    kb.op("dve", lambda acc=acc, u=u, k=k, sh=sh, w=w: V.scalar_tensor_tensor(
                        out=acc[:, sh:NT], in0=u[:, 0:NT - sh], scalar=w(k), in1=acc[:, sh:NT], op0=ALU.mult, op1=ALU.add),
                        reads=[bps[pb], bl, bt], writes=[bt])
                    kb.op("dve", lambda acc=acc, hl=hl, k=k, sh=sh, w=w: V.scalar_tensor_tensor(
                        out=acc[:, 0:sh], in0=hl[:, 3 - sh:3], scalar=w(k), in1=acc[:, 0:sh], op0=ALU.mult, op1=ALU.add),
                        reads=[bh, bl, bt], writes=[bt])
                kb.op("dve", lambda hl=hl, u=u: V.tensor_copy(out=hl[:, 0:3], in_=u[:, NT - 3:NT]), reads=[bps[pb]], writes=[bh])
                if fc < 16:
                    dst, dtk = self.xsT[:, fc, :], t_xsT
                elif fc < 20:
                    dst, dtk = self.BT[:, fc - 16, :], t_BT
                else:
                    dst, dtk = self.CT[:, fc - 20, :], t_CT
                kb.op("act", lambda dst=dst, acc=acc: S_.activation(out=dst, in_=acc, func=AF.Silu), reads=[bt], writes=dtk)
            if i < 4:
                pend.append(self.load_slab(win, C, 3584 + (i + 2) * 512, 512))
            elif i == 4:
                pend.append(self.load_slab(win, C, 6688 - 512, 512))
        self.stage("conv")
        wb, wt = next_slab()
        sv = self.sv
        for b in range(4):
            pb = self.rr("p4", 4)
            self.mm(ps[pb][:, 0:32], [(nT[:, kc, b * 128:(b + 1) * 128], wb[:, kc, 480:512]) for kc in range(C)],
                    reads=wt + t_nT, wtoks=[bps[pb]])
            kb.op("dve", lambda pb=pb: V.tensor_tensor(out=sv[:, 0, :], in0=ps[pb][:, 0:32], in1=self.hv[:, 0:32], op=ALU.add),
                  reads=[bps[pb], bl], writes=t_sm)
            kb.op("act", lambda: S_.activation(out=sv[:, 1, :], in_=sv[:, 0, :], func=AF.Exp), reads=t_sm, writes=t_sm)
            kb.op("act", lambda b=b: S_.activation(out=self.dtv[:, b, :], in_=sv[:, 1, :], func=AF.Ln, bias=self.ones[:, 0:1], scale=1.0),
                  reads=t_sm + [self.b_const], writes=t_dt)
        self.stage("inproj")
        kb.op("dve", lambda: V.tensor_copy(out=self.Sbf[:], in_=self.S[:]), reads=self.b_S, writes=t_Sbf)
        psx = self.psbf(0, 2)
        psB = self.psbf(2)
        tri, ones, nones = self.tri, self.ones, self.nones
        for b in range(4):
            blk = slice(b * 128, (b + 1) * 128)
            dt_ = self.dtv[:, b, :]
            dA, acs, eacs, dd, dtd, cdec = sv[:, 2, :], sv[:, 3, :], sv[:, 4, :], sv[:, 5, :], sv[:, 6, :], sv[:, 7, :]
            kb.op("dve", lambda dt_=dt_: V.tensor_tensor(out=dA, in0=dt_, in1=self.na[:], op=ALU.mult), reads=t_dt + [bl], writes=t_sm)
            self.mm(ps[3][:, 0:32], [(tri, dA)], reads=t_sm + [self.b_const], wtoks=[bps[3]])
            self.mm(ps[3][:, 32:64], [(ones, dA)], reads=t_sm + [self.b_const], wtoks=[bps[3]])
            kb.op("dve", lambda: V.tensor_copy(out=acs, in_=ps[3][:, 0:32]), reads=[bps[3]], writes=t_sm)
            kb.op("act", lambda: S_.activation(out=eacs, in_=acs, func=AF.Exp), reads=t_sm, writes=t_sm)
            kb.op("dve", lambda: V.tensor_tensor(out=dd, in0=ps[3][:, 32:64], in1=acs, op=ALU.subtract), reads=[bps[3]] + t_sm, writes=t_sm)
            kb.op("act", lambda: S_.activation(out=dd, in_=dd, func=AF.Exp), reads=t_sm, writes=t_sm)
            kb.op("dve", lambda dt_=dt_: V.tensor_tensor(out=dtd, in0=dt_, in1=dd, op=ALU.mult), reads=t_dt + t_sm, writes=t_sm)
            kb.op("act", lambda: S_.activation(out=cdec, in_=ps[3][:, 32:64], func=AF.Exp), reads=[bps[3]], writes=t_sm)

            def tr_x(b=b):
                ins = None
                for c in range(C):
                    ins = nc.tensor.transpose(out=psx[:, c * 128:(c + 1) * 128], in_=self.xsT[:, c, b * 128:(b + 1) * 128], identity=self.ident_bf)
                return ins
            kb.op("pe", tr_x, reads=t_xsT + [self.b_const], writes=[bps[0], bps[1]])
            px3 = psx.rearrange("p (h d) -> p h d", h=32)
            kb.op("dve", lambda dt_=dt_: V.tensor_tensor(out=self.xdt[:].rearrange("p (h d) -> p h d", h=32), in0=px3, in1=bc_last(dt_, 64), op=ALU.mult),
                  reads=[bps[0], bps[1]] + t_dt, writes=t_xdt)
            kb.op("dve", lambda: V.tensor_tensor(out=self.xdtd[:].rearrange("p (h d) -> p h d", h=32), in0=px3, in1=bc_last(dtd, 64), op=ALU.mult),
                  reads=[bps[0], bps[1]] + t_sm, writes=t_xdtd)
            kb.op("dve", lambda: V.tensor_tensor(out=self.y[:].rearrange("p (h d) -> p h d", h=32), in0=px3, in1=bc_last(self.hv[:, 64:96], 64), op=ALU.mult),
                  reads=[bps[0], bps[1], bl], writes=t_y)

            def tr_b(b=b):
                ins = None
                for g in range(4):
                    ins = nc.tensor.transpose(out=psB[:, g * 128:(g + 1) * 128], in_=self.BT[:, g, b * 128:(b + 1) * 128], identity=self.ident_bf)
                return ins
            kb.op("pe", tr_b, reads=t_BT + [self.b_const], writes=[bps[2]])
            kb.op("act", lambda: S_.activation(out=self.Btok[:].rearrange("p g n -> p (g n)"), in_=psB[:, 0:512], func=AF.Copy),
                  reads=[bps[2]], writes=t_sm)
            for g in range(4):
                self.mm(ps[4][:, g * 128:(g + 1) * 128], [(self.BT[:, g, blk], self.CT[:, g, blk])], reads=t_BT + t_CT, wtoks=[bps[4]])
            kb.op("dve", lambda: V.tensor_tensor(out=self.CBm[:], in0=ps[4].rearrange("p (g n) -> p g n", g=4), in1=bc_mid(tri, 4), op=ALU.mult),
                  reads=[bps[4], self.b_const], writes=t_sm)
            for g in range(4):
                gs = slice(g * 512, (g + 1) * 512)
                for r in range(8):
                    h = 8 * g + r
                    i = self.rr("hd", 4)
                    q = self.rr("pD", 8)
                    pD = 5 + q // 4
                    dcol = slice((q % 4) * 128, (q % 4 + 1) * 128)
                    kb.op("dve", lambda i=i, h=h: V.tensor_scalar(out=self.dat[:, i, :], in0=tri, scalar1=dA[:, h:h + 1], scalar2=0.0,
                                                                  op0=ALU.mult, op1=ALU.add), reads=t_sm + [self.b_const], writes=t_sm)
                    self.mm(ps[pD][:, dcol], [(ones, self.dat[:, i, :]), (self.dat[:, i, :], nones)], reads=t_sm + [self.b_const], wtoks=[bps[pD]])
                    kb.op("dve", lambda i=i, pD=pD, dcol=dcol: V.tensor_scalar(out=self.Dm[:, i, :], in0=ps[pD][:, dcol], scalar1=0.0, scalar2=0.0,
                                                                              op0=ALU.min, op1=ALU.add), reads=[bps[pD]], writes=t_sm)
                    kb.op("act", lambda i=i: S_.activation(out=self.Dm[:, i, :], in_=self.Dm[:, i, :], func=AF.Exp), reads=t_sm, writes=t_sm)
                    kb.op("dve", lambda i=i, g=g: V.tensor_tensor(out=self.MT[:, i, :], in0=self.Dm[:, i, :], in1=self.CBm[:, g, :], op=ALU.mult),
                          reads=t_sm, writes=t_sm)
                    self.mm(ps[7][:, r * 64:(r + 1) * 64], [(self.MT[:, i, :], self.xdt[:, h * 64:(h + 1) * 64])], reads=t_sm + t_xdt, wtoks=[bps[7]])
                yo = g % 2
                self.mm(ps[yo], [(self.CT[:, g, blk], self.Sbf[:, gs])], reads=t_CT + t_Sbf, wtoks=[bps[yo]])
                j = self.rr("tmp", 2)
                kb.op("dve", lambda j=j, yo=yo, g=g: V.tensor_tensor(
                    out=self.tmp[:, j, :].rearrange("p (h d) -> p h d", h=8), in0=ps[yo].rearrange("p (h d) -> p h d", h=8),
                    in1=bc_last(eacs[:, 8 * g:8 * g + 8], 64), op=ALU.mult), reads=[bps[yo]] + t_sm, writes=[self.b_tmp[j]])
                kb.op("dve", lambda j=j, gs=gs: V.tensor_tensor(out=self.y[:, gs], in0=self.y[:, gs], in1=self.tmp[:, j, :], op=ALU.add),
                      reads=t_y + [self.b_tmp[j]], writes=t_y)
                kb.op("dve", lambda gs=gs: V.tensor_tensor(out=self.y[:, gs], in0=self.y[:, gs], in1=ps[7], op=ALU.add),
                      reads=t_y + [bps[7]], writes=t_y)
                self.mm(ps[2], [(self.Btok[:, g, :], self.xdtd[:, gs])], reads=t_sm + t_xdtd, wtoks=[bps[2]])
                kb.op("dve", lambda g=g, gs=gs: V.tensor_tensor(
                    out=self.S[:, gs].rearrange("p (h d) -> p h d", h=8), in0=self.S[:, gs].rearrange("p (h d) -> p h d", h=8),
                    in1=bc_last(cdec[:, 8 * g:8 * g + 8], 64), op=ALU.mult), reads=[self.b_S[g]] + t_sm, writes=[self.b_S[g]])
                kb.op("dve", lambda gs=gs: V.tensor_tensor(out=self.S[:, gs], in0=self.S[:, gs], in1=ps[2], op=ALU.add),
                      reads=[self.b_S[g], bps[2]], writes=[self.b_S[g]])
                kb.op("act", lambda gs=gs: S_.activation(out=self.Sbf[:, gs], in_=self.S[:, gs], func=AF.Copy), reads=[self.b_S[g]], writes=t_Sbf)
            kb.op("dve", lambda b=b: V.tensor_tensor(out=self.y[:], in0=self.y[:], in1=self.zs[:, b, :], op=ALU.mult), reads=t_y + t_zs, writes=t_y)
            ssq, rg = sv[:, 8, 0:4], sv[:, 9, 0:4]
            kb.op("dve", lambda: V.memset(ssq, 0.0), writes=t_sm)
            for g in range(4):
                j = self.rr("sg", 2)
                kb.op("act", lambda g=g, j=j: S_.activation(out=self.sg[:, j, :], in_=self.y[:, g * 512:(g + 1) * 512], func=AF.Square,
                                                            accum_out=ssq[:, g:g + 1]), reads=t_y + t_sm, writes=[self.b_sg[j]] + t_sm)
            kb.op("act", lambda: S_.activation(out=rg, in_=ssq, func=AF.Sqrt, scale=1.0 / 512, bias=self.epsc[:, 0:1]), reads=t_sm + [self.b_const], writes=t_sm)
            kb.op("dve", lambda: V.reciprocal(out=rg, in_=rg), reads=t_sm, writes=t_sm)
            for g in range(4):
                gs = slice(g * 512, (g + 1) * 512)
                kb.op("dve", lambda g=g, gs=gs: V.scalar_tensor_tensor(out=self.ynb[:, gs], in0=self.y[:, gs], scalar=rg[:, g:g + 1], in1=self.snorm[:, gs],
                                                                       op0=ALU.mult, op1=ALU.mult), reads=t_y + t_sm + [bl], writes=t_ynb)
            psY = self.psbf(5, 2)

            def tr_y():
                ins = None
                for c in range(C):
                    ins = nc.tensor.transpose(out=psY[:, c * 128:(c + 1) * 128], in_=self.ynb[:, c * 128:(c + 1) * 128], identity=self.ident_bf)
                return ins
            kb.op("pe", tr_y, reads=t_ynb + [self.b_const], writes=[bps[5], bps[6]])
            kb.op("act", lambda blk=blk: S_.activation(out=self.ynT[:, 0:8, blk], in_=psY[:, 0:1024].rearrange("p (c n) -> p c n", c=8), func=AF.Copy),
                  reads=[bps[5]], writes=t_ynT)
            kb.op("dve", lambda blk=blk: V.tensor_copy(out=self.ynT[:, 8:16, blk], in_=psY[:, 1024:2048].rearrange("p (c n) -> p c n", c=8)),
                  reads=[bps[6]], writes=t_ynT)
        self.stage("ssd")
        dist = self.acon[:, 0:256]
        for b in range(4):
            blk = slice(b * 128, (b + 1) * 128)
            mask = self.acon[:, 512:768] if (t == 0 and b == 0) else self.acon[:, 256:512]
            for g in range(4):
                k2 = self.rr("pS", 2)
                psS = self.psall[:, 2 * k2:2 * k2 + 2, :].rearrange("p a (h k) -> p (a h) k", h=2)
                bS = [bps[2 * k2], bps[2 * k2 + 1]]

                def sc(b=b, g=g, psS=psS):
                    ins = None
                    for r in range(4):
                        hf = slice((r % 2) * 64, (r % 2) * 64 + 64)
                        ch = 2 * g + r // 2
                        ins = nc.tensor.matmul(psS[:, (r % 2) * 2 + r // 2, :], lhsT=self.qT[hf, ch, b * 128:(b + 1) * 128],
                                               rhs=self.kT[hf, g, b * 128:b * 128 + 256], start=True, stop=True)
                    return ins
                kb.op("pe", sc, reads=t_qT + t_kT, writes=bS)
                i = self.rr("ssb", 2)
                ssb = self.ssb[i]
                for r in range(4):
                    kb.op("dve", lambda r=r, ssb=ssb, psS=psS, g=g: V.scalar_tensor_tensor(
                        out=ssb[:, r, 0:256], in0=dist, scalar=-SLOPES[4 * g + r], in1=psS[:, (r % 2) * 2 + r // 2, :], op0=ALU.mult, op1=ALU.add),
                        reads=bS + [self.b_const], writes=t_at)
                kb.op("dve", lambda ssb=ssb, mask=mask: V.tensor_tensor(out=ssb[:, :, 0:256], in0=ssb[:, :, 0:256], in1=bc_mid(mask, 4), op=ALU.add),
                      reads=t_at + [self.b_const], writes=t_at)
                kb.op("dve", lambda ssb=ssb, g=g: V.tensor_copy(out=ssb[:, :, 256:257], in_=self.sinkc[:, 4 * g:4 * g + 4].unsqueeze(2)),
                      reads=[bl], writes=t_at)
                mx, sm, rs = self.ast[:, 0, 0:4], self.ast[:, 0, 4:8], self.ast[:, 0, 8:12]
                kb.op("dve", lambda ssb=ssb: V.tensor_reduce(out=mx, in_=ssb[:, :, 0:257], axis=AX.X, op=ALU.max), reads=t_at, writes=t_at)
                kb.op("dve", lambda ssb=ssb: V.tensor_tensor(out=ssb[:, :, 0:257], in0=ssb[:, :, 0:257], in1=bc_last(mx, 257), op=ALU.subtract),
                      reads=t_at, writes=t_at)
                kb.op("act", lambda ssb=ssb: S_.activation(out=ssb[:, :, 0:257], in_=ssb[:, :, 0:257], func=AF.Exp), reads=t_at, writes=t_at)
                kb.op("dve", lambda ssb=ssb: V.tensor_reduce(out=sm, in_=ssb[:, :, 0:257], axis=AX.X, op=ALU.add), reads=t_at, writes=t_at)
                kb.op("dve", lambda: V.reciprocal(out=rs, in_=sm), reads=t_at, writes=t_at)
                pp = self.pp[i]
                kb.op("dve", lambda ssb=ssb, pp=pp: V.tensor_tensor(out=pp[:], in0=ssb[:, :, 0:256], in1=bc_last(rs, 256), op=ALU.mult),
                      reads=t_at, writes=t_at)
                pbT = 4 + self.rr("pT", 2)
                psT = self.psbf(pbT)

                def trp(pp=pp, psT=psT):
                    ins = None
                    for r in range(4):
                        for k2_ in range(2):
                            m = r * 2 + k2_
                            ins = nc.tensor.transpose(out=psT[:, m * 128:(m + 1) * 128], in_=pp[:, r, k2_ * 128:(k2_ + 1) * 128], identity=self.ident_bf)
                    return ins
                kb.op("pe", trp, reads=t_at + [self.b_const], writes=[bps[pbT]])
                pT = self.pT[i]
                kb.op("act", lambda pT=pT, psT=psT: S_.activation(out=pT[:].rearrange("p m n -> p (m n)"), in_=psT[:, 0:1024], func=AF.Copy),
                      reads=[bps[pbT]], writes=t_at)
                pbO = 6 + self.rr("pO", 2)

                def pv(b=b, g=g, pT=pT, pbO=pbO):
                    ins = None
                    for r in range(4):
                        nc.tensor.matmul(ps[pbO][:, r * 128:(r + 1) * 128], lhsT=self.vd[:, b, g, :], rhs=pT[:, 2 * r, :], start=True, stop=False)
                        ins = nc.tensor.matmul(ps[pbO][:, r * 128:(r + 1) * 128], lhsT=self.vd[:, b + 1, g, :], rhs=pT[:, 2 * r + 1, :], start=False, stop=True)
                    return ins
                kb.op("pe", pv, reads=t_at + t_vd, writes=[bps[pbO]])
                pO = ps[pbO].rearrange("p (r n) -> p r n", r=4)
                kb.op("act", lambda g=g, blk=blk, pO=pO: S_.activation(out=self.qT[0:64, 2 * g:2 * g + 2, blk], in_=pO[0:64, 0::2, :], func=AF.Copy),
                      reads=[bps[pbO]], writes=t_qT)
                kb.op("dve", lambda g=g, blk=blk, pO=pO: V.tensor_copy(out=self.qT[64:128, 2 * g:2 * g + 2, blk], in_=pO[64:128, 1::2, :]),
                      reads=[bps[pbO]], writes=t_qT)
        self.stage("attn")
        kb.op("act", lambda: S_.activation(out=self.kT[:, :, 0:128], in_=self.kT[:, :, 512:640], func=AF.Copy), reads=t_kT, writes=t_kT)
        kb.op("act", lambda: S_.activation(out=self.vd[:, 0].rearrange("p k n -> p (k n)"), in_=self.vd[:, 4].rearrange("p k n -> p (k n)"), func=AF.Copy),
              reads=t_vd, writes=t_vd)
        oT = self.qT
        mg = self.xsT
        for ph in range(2):
            for sl in range(4):
                if ph == 0:
                    i1 = self.load_slab(self.w_ao[l], 8, sl * 512, 512)
                    i2 = self.load_slab(win, C, 6688 + sl * 512, 512)
                    kcs, act, t_act = 8, oT, t_qT
                else:
                    i1 = self.load_slab(self.w_so[l], C, sl * 512, 512)
                    i2 = self.load_slab(win, C, 8736 + sl * 512, 512)
                    kcs, act, t_act = C, self.ynT, t_ynT
                w1, w1t = self.wA[i1], self.tok(self.o_wA[i1], 16 * K)
                w2, w2t = self.wA[i2], self.tok(self.o_wA[i2], 16 * K)
                for cc in range(4):
                    c = sl * 4 + cc
                    pa = self.rr("pa", 2)
                    pg = 2 + self.rr("pg", 2)
                    self.mm(ps[pa], [(w1[:, kc, cc * 128:(cc + 1) * 128], act[:, kc, :]) for kc in range(kcs)], reads=w1t + t_act, wtoks=[bps[pa]])
                    self.mm(ps[pg], [(w2[:, kc, cc * 128:(cc + 1) * 128], nT[:, kc, :]) for kc in range(C)], reads=w2t + t_nT, wtoks=[bps[pg]])
                    j = self.rr("sg", 2)
                    kb.op("act", lambda j=j, pg=pg: S_.activation(out=self.sg[:, j, :], in_=ps[pg], func=AF.Sigmoid), reads=[bps[pg]], writes=[self.b_sg[j]])
                    mc, mct = self.mrg(c)
                    if ph == 0:
                        kb.op("dve", lambda j=j, pa=pa, mc=mc: V.tensor_tensor(out=mc, in0=self.sg[:, j, :], in1=ps[pa], op=ALU.mult),
                              reads=[self.b_sg[j], bps[pa]], writes=mct)
                    else:
                        kb.op("dve", lambda j=j, pa=pa: V.tensor_tensor(out=self.sg[:, j, :], in0=self.sg[:, j, :], in1=ps[pa], op=ALU.mult),
                              reads=[self.b_sg[j], bps[pa]], writes=[self.b_sg[j]])
                        kb.op("dve", lambda j=j, mc=mc, c=c: V.tensor_tensor(out=mg[:, c, :], in0=self.sg[:, j, :], in1=mc, op=ALU.add),
                              reads=[self.b_sg[j]] + mct, writes=t_xsT)
        self.stage("merge")
        tT = self.carve(self.o_zs, F32, [128, C, NT])
        pend = [self.load_slab(self.w_o[l], C, 0, 512)]
        for sl in range(4):
            if sl + 1 < 4:
                pend.append(self.load_slab(self.w_o[l], C, (sl + 1) * 512, 512))
            wb, wt = next_slab()
            for cc in range(4):
                c = sl * 4 + cc
                pf = 4 + self.rr("pf", 2)
                self.mm(ps[pf], [(wb[:, kc, cc * 128:(cc + 1) * 128], mg[:, kc, :]) for kc in range(C)], reads=wt + t_xsT, wtoks=[bps[pf]])
                kb.op("dve", lambda c=c, pf=pf: V.tensor_copy(out=tT[:, c, :], in_=ps[pf]), reads=[bps[pf]], writes=self.tok(self.o_zs + c * 2048, 2048))
                i = self.rr("sqf", 2)
                kb.op("act", lambda i=i, pf=pf: S_.activation(out=self.sqf[:, i, :], in_=ps[pf], func=AF.Square), reads=[bps[pf]], writes=[self.b_sqf[i]])
                kb.op("pe", lambda i=i, c=c: nc.tensor.matmul(ps[7], lhsT=self.ones_bf, rhs=self.sqf[:, i, :], start=(c == 0), stop=(c == C - 1)),
                      reads=[self.b_sqf[i], self.b_const], writes=[bps[7]])
        self.emit_postnorm_store(1, t, self.o_zs, 7)


def _fm(v):
    v = np.asarray(v, np.float32)
    k = v.shape[-1] // 128
    return np.ascontiguousarray(v.reshape(k, 128).T)


def _consts():
    f32 = np.float32
    tri = np.triu(np.ones((128, 128), f32))
    cm = np.concatenate([tri, np.eye(128, dtype=f32), np.ones((128, 128), f32), -np.ones((128, 128), f32)], 1)
    q = np.arange(128)[:, None]
    k = np.arange(256)[None, :]
    dist = (128 + q - k).astype(f32)
    valid = (dist >= 0) & (dist < 128)
    mask = np.where(valid, 0.0, NEG).astype(f32)
    mask0 = np.where(valid & (k >= 128), 0.0, NEG).astype(f32)
    acon = np.concatenate([dist, mask, mask0], 1)
    return np.ascontiguousarray(cm), np.ascontiguousarray(acon)


def make_in_map(x_b, c_b, w, L):
    f32 = np.float32
    cm, acon = _consts()
    bc = lambda v: np.ascontiguousarray(np.broadcast_to(np.asarray(v, f32)[None, :], (128, v.shape[-1])))
    m = dict(
        xT=np.ascontiguousarray(np.asarray(x_b, f32).T), cT=_fm(c_b),
        w_mod=w["w_mod"], b_modT=np.stack([_fm(w["b_mod"][l]) for l in range(L)]),
        npreT=np.stack([_fm(w["norm_pre"][l].reshape(-1)) for l in range(L)]),
        npostT=np.stack([_fm(w["norm_post"][l].reshape(-1)) for l in range(L)]),
        w_gate=w["w_ffn_gate"], w_up=w["w_ffn_up"], w_down=w["w_ffn_down"], w_in=w["w_in"],
        w_ao=w["w_attn_out"], w_so=w["w_ssm_out"], w_o=w["w_out"],
        acon=acon, cmats=cm,
        sinkc=np.stack([bc(w["attn_sinks"][l]) for l in range(L)]),
        convwT=np.stack([np.ascontiguousarray(w["conv_w"][l].reshape(4, 24, 128).transpose(2, 1, 0).reshape(128, 96)) for l in range(L)]),
        convbT=np.stack([_fm(w["conv_b"][l]) for l in range(L)]),
        hvec=np.stack([bc(np.concatenate([w["dt_bias"][l], w["a_log"][l], w["d_skip"][l]])) for l in range(L)]),
        snorm=np.stack([bc(w["ssm_norm"][l]) for l in range(L)]),
    )
    return {k: np.ascontiguousarray(v, dtype=f32) for k, v in m.items()}


_PROG = {}


def kernel(**inputs):
    x = np.asarray(inputs["x"], np.float32)
    B, T, _ = x.shape
    L = inputs["w_mod"].shape[0]
    w = {k: np.asarray(v, np.float32) for k, v in inputs.items() if k not in ("x", "c")}
    key = (T, L)
    if key not in _PROG:
        _PROG[key] = Prog(T, L)
    p = _PROG[key]
    in_maps = [make_in_map(x[b], inputs["c"][b], w, L) for b in range(B)]
    res = run_bass_kernel_spmd(p.nc, in_maps, core_ids=list(range(B)))
    out = np.stack([np.ascontiguousarray(res.results[b]["yT"].T) for b in range(B)])
    return out.astype(np.float32)
```

```python
import numpy as np
import concourse.bass as bass
import concourse.mybir as mybir
from concourse.bass_utils import run_bass_kernel_spmd

F32 = mybir.dt.float32
BF16 = mybir.dt.bfloat16
AF = mybir.ActivationFunctionType
ALU = mybir.AluOpType
AX = mybir.AxisListType

D = 2048
C = 16
DFF = 5632
JF = 44
NT = 512
P_IN = 10784
EPS = 1e-6
NEG = -30000.0


class Buf:
    __slots__ = ("name", "w", "r", "dsem", "dcnt", "excl")

    def __init__(self, name):
        self.name = name
        self.excl = False
        self.w = None
        self.r = []
        self.dsem = None
        self.dcnt = 0


class KB:
    def __init__(self):
        nc = bass.Bass("TRN2", target_bir_lowering=False)
        self.nc = nc
        self.E = dict(pe=nc.tensor, act=nc.scalar, dve=nc.vector, pool=nc.gpsimd, sp=nc.sync)
        self.esem = {e: nc.alloc_semaphore("es_" + e) for e in ("pe", "act", "dve", "pool")}
        self.ecnt = {e: 0 for e in self.esem}
        self.known = {e: {} for e in self.E}
        self.out_events = []
        self.nbuf = 0

    def buf(self, name=None):
        self.nbuf += 1
        return Buf(name or f"b{self.nbuf}")

    def bufs(self, n, name="b"):
        return [self.buf(f"{name}{i}") for i in range(n)]

    def _waits(self, e, reads, writes):
        deps = {}

        def add(ev):
            if ev is None:
                return
            s, v = ev
            k = id(s)
            if k not in deps or deps[k][1] < v:
                deps[k] = (s, v)

        for b in reads:
            add(b.w)
            if b.excl:
                for ev in b.r:
                    add(ev)
        for b in writes:
            add(b.w)
            for ev in b.r:
                add(ev)
        kn = self.known[e]
        pes = self.esem["pe"]
        for k, (s, v) in deps.items():
            if e == "pe" and s is pes:
                continue
            if kn.get(k, 0) >= v:
                continue
            self.E[e].wait_ge(s, v)
            kn[k] = v

    def _commit(self, ev, reads, writes):
        for b in writes:
            b.w = ev
            b.r = []
        for b in reads:
            if b not in writes:
                b.r.append(ev)
                if len(b.r) > 12:
                    m = {}
                    for s, v in b.r:
                        if id(s) not in m or m[id(s)][1] < v:
                            m[id(s)] = (s, v)
                    b.r = list(m.values())

    def op(self, e, fn, reads=(), writes=()):
        self._waits(e, reads, writes)
        ins = fn()
        ins.then_inc(self.esem[e], 1)
        self.ecnt[e] += 1
        ev = (self.esem[e], self.ecnt[e])
        self._commit(ev, reads, writes)
        return ev

    def dma(self, q, out, in_, sembuf, reads=(), writes=(), is_output=False):
        self._waits(q, reads, writes)
        if sembuf.dsem is None:
            sembuf.dsem = self.nc.alloc_semaphore("ds_" + sembuf.name)
        self.E[q].dma_start(out=out, in_=in_).then_inc(sembuf.dsem, 16)
        sembuf.dcnt += 1
        ev = (sembuf.dsem, 16 * sembuf.dcnt)
        self._commit(ev, reads, writes)
        if is_output:
            self.out_events.append(ev)
        return ev

    def finish(self):
        m = {}
        for s, v in self.out_events:
            if id(s) not in m or m[id(s)][1] < v:
                m[id(s)] = (s, v)
        for s, v in m.values():
            self.E["sp"].wait_ge(s, v)


def bc_mid(ap, n):
    p, f = ap.shape
    return ap.unsqueeze(1).broadcast_to([p, n, f])


def bc_last(ap, n):
    p, g = ap.shape
    return ap.unsqueeze(2).broadcast_to([p, g, n])


SLOPES = [2.0 ** (-8.0 * (h + 1) / 16.0) for h in range(16)]
KB1 = 1024


class _Stop(Exception):
    pass


class Prog:
    def __init__(self, T, depth, do_ffn=True, do_mix=True, dbg=None):
        self.dbg = dbg
        self.T = T
        self.depth = depth
        self.do_ffn = do_ffn
        self.do_mix = do_mix
        self.NTILES = T // NT
        kb = KB()
        self.kb = kb
        nc = kb.nc
        self.nc = nc
        dt = nc.dram_tensor
        L = depth
        self.xT = dt("xT", [D, T], F32, kind="ExternalInput").ap()
        self.yT = dt("yT", [D, T], F32, kind="ExternalOutput").ap()
        self.cT = dt("cT", [128, C], F32, kind="ExternalInput").ap()
        self.w_mod = dt("w_mod", [L, D, 9 * D], F32, kind="ExternalInput").ap()
        self.b_modT = dt("b_modT", [L, 128, 9 * C], F32, kind="ExternalInput").ap()
        self.npreT = dt("npreT", [L, 128, 3 * C], F32, kind="ExternalInput").ap()
        self.npostT = dt("npostT", [L, 128, 3 * C], F32, kind="ExternalInput").ap()
        self.w_gate = dt("w_gate", [L, 2, D, DFF], F32, kind="ExternalInput").ap()
        self.w_up = dt("w_up", [L, 2, D, DFF], F32, kind="ExternalInput").ap()
        self.w_down = dt("w_down", [L, 2, DFF, D], F32, kind="ExternalInput").ap()
        self.w_in = dt("w_in", [L, D, P_IN], F32, kind="ExternalInput").ap()
        self.w_ao = dt("w_ao", [L, 1024, D], F32, kind="ExternalInput").ap()
        self.w_so = dt("w_so", [L, D, D], F32, kind="ExternalInput").ap()
        self.w_o = dt("w_o", [L, D, D], F32, kind="ExternalInput").ap()
        self.acon_d = dt("acon", [128, 3 * 256], F32, kind="ExternalInput").ap()
        self.sink_d = dt("sinkc", [L, 128, 16], F32, kind="ExternalInput").ap()
        self.convw_d = dt("convwT", [L, 128, 24 * 4], F32, kind="ExternalInput").ap()
        self.convb_d = dt("convbT", [L, 128, 24], F32, kind="ExternalInput").ap()
        self.hvec_d = dt("hvec", [L, 128, 3 * 32], F32, kind="ExternalInput").ap()
        self.snorm_d = dt("snorm", [L, 128, D], F32, kind="ExternalInput").ap()
        self.cmats_d = dt("cmats", [128, 4 * 128], F32, kind="ExternalInput").ap()
        self.wgb = dt("wgb", [L, 2, D, DFF], BF16).ap()
        self.wub = dt("wub", [L, 2, D, DFF], BF16).ap()
        self.wdb = dt("wdb", [L, 2, DFF, D], BF16).ap()
        self.b_cast = [[kb.bufs(3, f"cast{l}_{f}_") for f in range(2)] for l in range(L)]
        self.winb = dt("winb", [L, D, P_IN], BF16).ap()
        self.waob = dt("waob", [L, 1024, D], BF16).ap()
        self.wsob = dt("wsob", [L, D, D], BF16).ap()
        self.wob = dt("wob", [L, D, D], BF16).ap()
        self.b_castm = [kb.bufs(4, f"castm{l}_") for l in range(L)]
        self._slab_q = "pool"
        self._slab_rd = []
        self.cur = self.xT
        self.b_y = [kb.bufs(C, f"y{t}_") for t in range(self.NTILES)]
        self._alloc()
        self._emit()
        kb.finish()

    def carve(self, off, dtype, shape):
        esz = 4 if dtype == F32 else 2
        n = 1
        for d in shape[1:]:
            n *= d
        assert off % 4 == 0 and off + n * esz <= self.ARENA, (off, n * esz, self.ARENA)
        v = self.arena[:, off:off + n * esz].bitcast(dtype)
        if len(shape) == 3:
            v = v.rearrange("p (a b) -> p a b", a=shape[1])
        elif len(shape) == 4:
            v = v.rearrange("p (a b c) -> p a b c", a=shape[1], b=shape[2])
        return v

    def tok(self, off, nbytes):
        return self.b_ar[off // 2048:(off + nbytes + 2047) // 2048]

    def _alloc(self):
        nc, kb = self.nc, self.kb
        A = nc.alloc_sbuf_tensor
        K = KB1
        self.cm = A("cm", [128, 4 * 128], F32)
        self.cm_bf = A("cm_bf", [128, 4 * 128], BF16)
        self.tri = self.cm[:, 0:128]
        self.ident = self.cm[:, 128:256]
        self.ones = self.cm[:, 256:384]
        self.nones = self.cm[:, 384:512]
        self.ident_bf = self.cm_bf[:, 128:256]
        self.ones_bf = self.cm_bf[:, 256:384]
        self.epsc = A("epsc", [128, 1], F32)
        self.cact = A("cact", [128, C], BF16)
        self.ctmp = A("ctmp", [128, C], F32)
        self.mod = A("mod", [128, 9 * C], F32)
        self.bmod = A("bmod", [128, 9 * C], F32)
        self.npre = A("npre", [128, 3 * C], F32)
        self.npost = A("npost", [128, 3 * C], F32)
        self.gsc = A("gsc", [128, 3 * C], F32)
        self.gp = A("gp", [128, 3 * C], F32)
        self.b_const = kb.buf("const")
        self.b_mod = kb.buf("mod")
        self.b_cact = kb.buf("cact")
        self.rstd = A("rstd", [128, NT], F32)
        self.b_rstd = kb.buf("rstd")
        self.rstd2 = A("rstd2", [128, NT], F32)
        self.b_rstd2 = kb.buf("rstd2")
        self.sq = A("sq", [128, 2, 2 * NT], BF16)
        self.b_sq = kb.bufs(2, "sq")
        self.tmp = A("tmp", [128, 2, NT], F32)
        self.b_tmp = kb.bufs(2, "tmp")
        self.xr = A("xr", [128, 2, NT], F32)
        self.b_xr = kb.bufs(2, "xr")
        self.sg = A("sg", [128, 2, NT], F32)
        self.b_sg = kb.bufs(2, "sg")
        self.sqf = A("sqf", [128, 2, NT], BF16)
        self.b_sqf = kb.bufs(2, "sqf")
        self.acon = A("acon_sb", [128, 3 * 256], F32)
        self.sinkc = A("sink_sb", [128, 16], F32)
        self.convw = A("convw_sb", [128, 24 * 4], F32)
        self.convb = A("convb_sb", [128, 24], F32)
        self.hv = A("hv_sb", [128, 96], F32)
        self.na = A("na_sb", [128, 32], F32)
        self.snorm = A("snorm_sb", [128, D], F32)
        self.S = A("S_sb", [128, D], F32)
        self.halo = A("halo_sb", [128, 24, 4], F32)
        self.b_lay = kb.buf("layer_consts")
        self.b_S = kb.bufs(4, "S")
        self.b_halo = kb.bufs(24, "halo")
        self.psall = nc.alloc_psum_tensor("psall", [128, 8, 512], F32)
        self.ps = [self.psall[:, i, :] for i in range(8)]
        self.b_ps = kb.bufs(8, "ps")
        for b in self.b_ps:
            b.excl = True
        self.ARENA = (nc.sbuf_bytes_remaining // 1024 - 1) * 1024
        self.arena = A("arena", [128, self.ARENA], mybir.dt.uint8)
        self.b_ar = kb.bufs(self.ARENA // 2048 + 1, "ar")
        self.o_hT = 0
        self.hT = self.carve(0, BF16, [128, JF, NT])
        self.o_fT = 44 * K
        self.fT = self.carve(self.o_fT, F32, [128, C, NT])
        self.o_wA = [76 * K, 92 * K]
        self.wA = [self.carve(o, BF16, [128, C, 512]) for o in self.o_wA]
        self.o_wB = [108 * K, 130 * K]
        self.wB = [self.carve(o, BF16, [128, JF, 256]) for o in self.o_wB]
        assert 152 * K <= self.ARENA, self.ARENA
        self.o_qT = 16 * K
        self.qT = self.carve(self.o_qT, BF16, [128, 8, NT])
        self.o_kT = 24 * K
        self.kT = self.carve(self.o_kT, BF16, [128, 4, 640])
        self.o_vd = 30 * K
        self.vd = self.carve(self.o_vd, BF16, [128, 5, 4, 128])
        self.o_BT = 36 * K
        self.BT = self.carve(self.o_BT, BF16, [128, 4, NT])
        self.o_CT = 40 * K
        self.CT = self.carve(self.o_CT, BF16, [128, 4, NT])
        self.o_dt = 44 * K
        self.dtv = self.carve(self.o_dt, F32, [128, 4, 32])
        self.o_xsT = 46 * K
        self.xsT = self.carve(self.o_xsT, BF16, [128, C, NT])
        self.o_zs = 108 * K
        self.zs = self.carve(self.o_zs, BF16, [128, 4, D])
        self.o_ynT = 124 * K
        self.ynT = self.carve(self.o_ynT, BF16, [128, C, NT])
        self.o_xdt = 140 * K
        self.xdt = self.carve(self.o_xdt, BF16, [128, D])
        self.o_xdtd = 144 * K
        self.xdtd = self.carve(self.o_xdtd, BF16, [128, D])
        self.o_y = 148 * K
        self.y = self.carve(self.o_y, F32, [128, D])
        self.o_ynb = 140 * K
        self.ynb = self.carve(self.o_ynb, BF16, [128, D])
        assert 159 * K <= self.ARENA, self.ARENA
        self.o_Sbf = 62 * K
        self.Sbf = self.carve(self.o_Sbf, BF16, [128, D])
        self.o_Btok = 66 * K
        self.Btok = self.carve(self.o_Btok, BF16, [128, 4, 128])
        self.o_MT = 67 * K
        self.MT = self.carve(self.o_MT, BF16, [128, 4, 128])
        self.o_dat = 68 * K
        self.dat = self.carve(self.o_dat, F32, [128, 4, 128])
        self.o_Dm = 70 * K
        self.Dm = self.carve(self.o_Dm, F32, [128, 4, 128])
        self.o_CBm = 72 * K
        self.CBm = self.carve(self.o_CBm, F32, [128, 4, 128])
        self.o_sv = 74 * K
        self.sv = self.carve(self.o_sv, F32, [128, 16, 32])
        self.o_ssb = [140 * K, 140 * K + 4160]
        self.ssb = [self.carve(o, F32, [128, 4, 260]) for o in self.o_ssb]
        self.o_p = [150 * K, 152 * K]
        self.pp = [self.carve(o, BF16, [128, 4, 256]) for o in self.o_p]
        self.o_pT = [154 * K, 156 * K]
        self.pT = [self.carve(o, BF16, [128, 8, 128]) for o in self.o_pT]
        self.o_ast = 158 * K
        self.ast = self.carve(self.o_ast, F32, [128, 2, 16])
        self.ring = {}

    def rr(self, key, n):
        i = self.ring.get(key, 0)
        self.ring[key] = i + 1
        return i % n

    def mrg(self, c):
        if c < 7:
            o = 62 * KB1 + c * 2048
        else:
            o = 140 * KB1 + (c - 7) * 2048
        return self.carve(o, F32, [128, NT]), self.tok(o, 2048)

    def mm(self, out, pairs, reads, wtoks, f32=False):
        nc = self.nc
        n = len(pairs)

        def f_():
            ins = None
            for i, (l, r) in enumerate(pairs):
                ins = nc.tensor.matmul(out, lhsT=l, rhs=r, start=(i == 0), stop=(i == n - 1))
            return ins
        return self.kb.op("pe", f_, reads=reads, writes=wtoks)

    def rsqrt_mean(self, out, in_, n, rtoks, wtoks):
        nc = self.nc
        self.kb.op("act", lambda: nc.scalar.activation(out=out, in_=in_, func=AF.Sqrt, scale=1.0 / n, bias=self.epsc[:, 0:1]),
                   reads=rtoks + [self.b_const], writes=wtoks)
        self.kb.op("dve", lambda: nc.vector.reciprocal(out=out, in_=out), reads=wtoks, writes=wtoks)

    def load_slab(self, src, kcs, c0, ncols, dst_off=0):
        i = self.rr("wA", 2)
        self._slab_into(i, src, kcs, c0, ncols, dst_off)
        return i

    def _slab_into(self, i, src, kcs, c0, ncols, dst_off=0):
        wb = self.wA[i]
        toks = self.tok(self.o_wA[i], 16 * KB1)
        sv = src.rearrange("(kc p) n -> p kc n", p=128)
        step = 4 if ncols >= 512 else 8
        for k0 in range(0, kcs, step):
            k1 = min(kcs, k0 + step)
            self.kb.dma(self._slab_q, wb[:, k0:k1, dst_off:dst_off + ncols], sv[:, k0:k1, c0:c0 + ncols], toks[0],
                        reads=self._slab_rd, writes=toks)

    def _emit(self):
        kb, nc = self.kb, self.nc
        kb.dma("sp", self.cm[:], self.cmats_d[:, :], self.b_const, writes=[self.b_const])
        kb.op("dve", lambda: nc.vector.tensor_copy(out=self.cm_bf[:], in_=self.cm[:]),
              reads=[self.b_const], writes=[self.b_const])
        kb.dma("sp", self.acon[:], self.acon_d[:, :], self.b_const, writes=[self.b_const])
        kb.op("dve", lambda: nc.vector.memset(self.epsc[:], EPS), writes=[self.b_const])
        kb.dma("sp", self.ctmp[:], self.cT[:, :], self.b_cact, writes=[self.b_cact])
        kb.op("act", lambda: nc.scalar.activation(out=self.cact[:], in_=self.ctmp[:], func=AF.Silu),
              reads=[self.b_cact], writes=[self.b_cact])
        try:
            self._emit_layers()
        except _Stop:
            kb.op("dve", lambda: nc.vector.memset(self.rstd[:], 1.0), writes=[self.b_rstd])
            kb.dma("sp", self.yT[0:128, 0:NT], self.rstd[:], self.b_rstd, reads=[self.b_rstd], is_output=True)

    def stage(self, name):
        if self.dbg == name:
            raise _Stop()

    def _emit_layers(self):
        kb, nc = self.kb, self.nc
        for l in range(self.depth):
            self.emit_mod(l)
            if self.dbg == "mod":
                kb.dma("sp", self.yT[0:128, 0:144], self.mod[:], self.b_mod, reads=[self.b_mod], is_output=True)
                return
            if self.do_ffn:
                if l == 0:
                    self.emit_cast(0)
                    if self.do_mix:
                        self.emit_cast_mix(0)
                self.emit_ffn(l, 0)
            if self.do_mix:
                if not self.do_ffn:
                    self.emit_cast_mix(l)
                self.emit_mixer(l)
            if self.do_ffn:
                if l + 1 < self.depth:
                    self.emit_cast(l + 1)
                    if self.do_mix:
                        self.emit_cast_mix(l + 1)
                self.emit_ffn(l, 1)

    def emit_cast(self, l):
        kb = self.kb
        for f in range(2):
            for m, (src, dst, rows) in enumerate(((self.w_gate, self.wgb, D), (self.w_up, self.wub, D), (self.w_down, self.wdb, DFF))):
                b = self.b_cast[l][f][m]
                for r0 in range(0, rows, 256):
                    kb.dma("pool", dst[l, f, r0:r0 + 256, :], src[l, f, r0:r0 + 256, :], b, writes=[b])

    def emit_cast_mix(self, l):
        kb = self.kb
        for m, (src, dst, rows) in enumerate(((self.w_in, self.winb, D), (self.w_ao, self.waob, 1024),
                                              (self.w_so, self.wsob, D), (self.w_o, self.wob, D))):
            b = self.b_castm[l][m]
            for r0 in range(0, rows, 256):
                kb.dma("pool", dst[l, r0:r0 + 256, :], src[l, r0:r0 + 256, :], b, writes=[b])

    def emit_mod(self, l):
        kb, nc = self.kb, self.nc
        V = nc.vector
        kb.dma("sp", self.bmod[:], self.b_modT[l], self.b_mod, writes=[self.b_mod])
        kb.dma("sp", self.npre[:], self.npreT[l], self.b_mod, writes=[self.b_mod])
        kb.dma("sp", self.npost[:], self.npostT[l], self.b_mod, writes=[self.b_mod])
        pb = 7
        nslab = 9 * D // 512
        pend = [self.load_slab(self.w_mod[l], C, 0, 512)]
        for s in range(nslab):
            if s + 1 < nslab:
                pend.append(self.load_slab(self.w_mod[l], C, (s + 1) * 512, 512))
            i = pend.pop(0)
            wb = self.wA[i]
            toks = self.tok(self.o_wA[i], 16 * KB1)
            for cc in range(4):
                m = s * 4 + cc
                self.mm(self.ps[pb][:, m:m + 1],
                        [(wb[:, kc, cc * 128:(cc + 1) * 128], self.cact[:, kc:kc + 1]) for kc in range(C)],
                        reads=toks + [self.b_cact], wtoks=[self.b_ps[pb]])
        kb.op("dve", lambda: V.tensor_tensor(out=self.mod[:], in0=self.ps[pb][:, 0:9 * C], in1=self.bmod[:], op=ALU.add),
              reads=[self.b_ps[pb], self.b_mod], writes=[self.b_mod])
        for s in range(3):
            sc = self.mod[:, (3 * s + 1) * C:(3 * s + 2) * C]
            g = self.mod[:, (3 * s + 2) * C:(3 * s + 3) * C]
            kb.op("dve", lambda s=s, sc=sc: V.scalar_tensor_tensor(
                out=self.gsc[:, s * C:(s + 1) * C], in0=sc, scalar=1.0, in1=self.npre[:, s * C:(s + 1) * C],
                op0=ALU.add, op1=ALU.mult), reads=[self.b_mod], writes=[self.b_mod])
            kb.op("dve", lambda s=s, g=g: V.scalar_tensor_tensor(
                out=self.gp[:, s * C:(s + 1) * C], in0=g, scalar=(1.0 if s == 1 else 0.5),
                in1=self.npost[:, s * C:(s + 1) * C], op0=ALU.mult, op1=ALU.mult),
                reads=[self.b_mod], writes=[self.b_mod])

    def emit_prenorm(self, s, t, o_xt, o_nT):
        kb, nc = self.kb, self.nc
        V, S_ = nc.vector, nc.scalar
        xt = self.carve(o_xt, F32, [128, C, NT])
        nT = self.carve(o_nT, BF16, [128, C, NT])
        src = self.cur.rearrange("(c p) n -> p c n", p=128)
        for h in range(4):
            tk = self.tok(o_xt + 4 * h * 2048, 4 * 2048)
            kb.dma("sp", xt[:, 4 * h:4 * h + 4, :], src[:, 4 * h:4 * h + 4, t * NT:(t + 1) * NT],
                   tk[0], reads=self.b_y[t][4 * h:4 * h + 4], writes=tk)
        pb = 6
        for h in range(8):
            i = self.rr("sq", 2)
            tk = self.tok(o_xt + 2 * h * 2048, 2 * 2048)
            kb.op("act", lambda h=h, i=i: S_.activation(
                out=self.sq[:, i, :], in_=xt[:, 2 * h:2 * h + 2, :].rearrange("p c n -> p (c n)"), func=AF.Square),
                reads=tk, writes=[self.b_sq[i]])

            def mm(h=h, i=i):
                ins = None
                for cc in range(2):
                    ins = nc.tensor.matmul(self.ps[pb], lhsT=self.ones_bf, rhs=self.sq[:, i, cc * NT:(cc + 1) * NT],
                                           start=(h == 0 and cc == 0), stop=(h == 7 and cc == 1))
                return ins
            kb.op("pe", mm, reads=[self.b_sq[i], self.b_const], writes=[self.b_ps[pb]])
        self.rsqrt_mean(self.rstd[:], self.ps[pb], D, [self.b_ps[pb]], [self.b_rstd])
        for c in range(C):
            i = self.rr("tmp", 2)
            kb.op("dve", lambda c=c, i=i: V.tensor_tensor(out=self.tmp[:, i, :], in0=xt[:, c, :], in1=self.rstd[:], op=ALU.mult),
                  reads=self.tok(o_xt + c * 2048, 2048) + [self.b_rstd], writes=[self.b_tmp[i]])
            kb.op("act", lambda c=c, i=i: S_.activation(
                out=nT[:, c, :], in_=self.tmp[:, i, :], func=AF.Identity,
                scale=self.gsc[:, s * C + c:s * C + c + 1], bias=self.mod[:, 3 * s * C + c:3 * s * C + c + 1]),
                reads=[self.b_tmp[i], self.b_mod], writes=self.tok(o_nT + c * 1024, 1024))

    def emit_postnorm_store(self, s, t, o_fT, pb2):
        kb, nc = self.kb, self.nc
        V = nc.vector
        fT = self.carve(o_fT, F32, [128, C, NT])
        self.rsqrt_mean(self.rstd2[:], self.ps[pb2], D, [self.b_ps[pb2]], [self.b_rstd2])
        src = self.cur.rearrange("(c p) n -> p c n", p=128)
        dst = self.yT.rearrange("(c p) n -> p c n", p=128)
        for c in range(C):
            j = self.rr("xr", 2)
            kb.dma("sp", self.xr[:, j, :], src[:, c, t * NT:(t + 1) * NT], self.b_xr[j],
                   reads=[self.b_y[t][c]], writes=[self.b_xr[j]])
            i = self.rr("tmp", 2)
            kb.op("dve", lambda c=c, i=i: V.tensor_tensor(out=self.tmp[:, i, :], in0=fT[:, c, :], in1=self.rstd2[:], op=ALU.mult),
                  reads=self.tok(o_fT + c * 2048, 2048) + [self.b_rstd2], writes=[self.b_tmp[i]])
            kb.op("dve", lambda c=c, i=i, j=j: V.scalar_tensor_tensor(
                out=self.xr[:, j, :], in0=self.tmp[:, i, :], scalar=self.gp[:, s * C + c:s * C + c + 1], in1=self.xr[:, j, :],
                op0=ALU.mult, op1=ALU.add), reads=[self.b_tmp[i], self.b_mod, self.b_xr[j]], writes=[self.b_xr[j]])
            kb.dma("sp", dst[:, c, t * NT:(t + 1) * NT], self.xr[:, j, :], self.b_xr[j],
                   reads=[self.b_xr[j]], writes=[self.b_y[t][c]], is_output=True)

    def emit_ffn(self, l, f):
        kb, nc = self.kb, self.nc
        V, S_ = nc.vector, nc.scalar
        s = 0 if f == 0 else 2
        wd = self.w_down[l, f].rearrange("(j p) n -> p j n", p=128)
        NS1 = DFF // 256
        NS2 = D // 256
        o_nT = self.o_fT
        nT = self.carve(o_nT, BF16, [128, C, NT])
        nT_toks = self.tok(o_nT, 16 * KB1)
        hT_toks = self.tok(self.o_hT, 44 * KB1)

        bc = self.b_cast[l][f]
        wgv = self.wgb[l, f].rearrange("(kc p) n -> p kc n", p=128)
        wuv = self.wub[l, f].rearrange("(kc p) n -> p kc n", p=128)
        wdv = self.wdb[l, f].rearrange("(j p) n -> p j n", p=128)

        def load_gu(sl):
            i = self.rr("wA", 2)
            wb = self.wA[i]
            toks = self.tok(self.o_wA[i], 16 * KB1)
            for h in range(2):
                kb.dma("sp", wb[:, 8 * h:8 * h + 8, 0:256], wgv[:, 8 * h:8 * h + 8, sl * 256:(sl + 1) * 256], toks[0], reads=[bc[0]], writes=toks)
            for h in range(2):
                kb.dma("sp", wb[:, 8 * h:8 * h + 8, 256:512], wuv[:, 8 * h:8 * h + 8, sl * 256:(sl + 1) * 256], toks[0], reads=[bc[1]], writes=toks)
            return i

        def load_d(sl):
            i = self.rr("wB", 2)
            wb = self.wB[i]
            toks = self.tok(self.o_wB[i], 22 * KB1)
            for h in range(4):
                kb.dma("sp", wb[:, 11 * h:11 * h + 11, :], wdv[:, 11 * h:11 * h + 11, sl * 256:(sl + 1) * 256], toks[0], reads=[bc[2]], writes=toks)
            return i

        for t in range(self.NTILES):
            pend_gu = [load_gu(0), load_gu(1)]
            self.emit_prenorm(s, t, self.o_hT, o_nT)
            pend_d = []
            for sl in range(NS1):
                wi = pend_gu.pop(0)
                wb = self.wA[wi]
                wtoks = self.tok(self.o_wA[wi], 16 * KB1)
                for jj in range(2):
                    j = sl * 2 + jj
                    pg = self.rr("pgu", 2) * 2
                    pu = pg + 1
                    self.mm(self.ps[pg], [(wb[:, kc, jj * 128:(jj + 1) * 128], nT[:, kc, :]) for kc in range(C)],
                            reads=wtoks + nT_toks, wtoks=[self.b_ps[pg]])
                    self.mm(self.ps[pu], [(wb[:, kc, 256 + jj * 128:256 + (jj + 1) * 128], nT[:, kc, :]) for kc in range(C)],
                            reads=wtoks + nT_toks, wtoks=[self.b_ps[pu]])
                    i = self.rr("sg", 2)
                    kb.op("act", lambda i=i, pg=pg: S_.activation(out=self.sg[:, i, :], in_=self.ps[pg], func=AF.Silu),
                          reads=[self.b_ps[pg]], writes=[self.b_sg[i]])
                    kb.op("dve", lambda i=i, pu=pu, j=j: V.tensor_tensor(out=self.hT[:, j, :], in0=self.sg[:, i, :], in1=self.ps[pu], op=ALU.mult),
                          reads=[self.b_sg[i], self.b_ps[pu]], writes=self.tok(self.o_hT + j * 1024, 1024))
                if sl + 2 < NS1:
                    pend_gu.append(load_gu(sl + 2))
                else:
                    pend_d.append(load_d(len(pend_d)))
            pb2 = 7
            for sl in range(NS2):
                wi = pend_d.pop(0)
                wb = self.wB[wi]
                wtoks = self.tok(self.o_wB[wi], 22 * KB1)
                for ii in range(2):
                    c = sl * 2 + ii
                    pf = 4 + self.rr("pf", 2)
                    self.mm(self.ps[pf], [(wb[:, j, ii * 128:(ii + 1) * 128], self.hT[:, j, :]) for j in range(JF)],
                            reads=wtoks + hT_toks, wtoks=[self.b_ps[pf]])
                    kb.op("dve", lambda c=c, pf=pf: V.tensor_copy(out=self.fT[:, c, :], in_=self.ps[pf]),
                          reads=[self.b_ps[pf]], writes=self.tok(self.o_fT + c * 2048, 2048))
                    i = self.rr("sqf", 2)
                    kb.op("act", lambda i=i, pf=pf: S_.activation(out=self.sqf[:, i, :], in_=self.ps[pf], func=AF.Square),
                          reads=[self.b_ps[pf]], writes=[self.b_sqf[i]])
                    self.mm(self.ps[pb2], [(self.ones_bf, self.sqf[:, i, :])], reads=[self.b_sqf[i], self.b_const], wtoks=[self.b_ps[pb2]]) \
                        if False else kb.op("pe", lambda i=i, c=c: nc.tensor.matmul(
                            self.ps[pb2], lhsT=self.ones_bf, rhs=self.sqf[:, i, :], start=(c == 0), stop=(c == C - 1)),
                            reads=[self.b_sqf[i], self.b_const], writes=[self.b_ps[pb2]])
                if sl + 2 < NS2:
                    pend_d.append(load_d(sl + 2))
            self.emit_postnorm_store(s, t, self.o_fT, pb2)
        self.cur = self.yT

    def psbf(self, b0, nb=1):
        return self.psall[:, b0:b0 + nb, :].rearrange("p a b -> p (a b)").bitcast(BF16)

    def emit_mixer(self, l):
        kb, nc = self.kb, self.nc
        V, S_ = nc.vector, nc.scalar
        K = KB1
        bl = self.b_lay
        for dst, src in ((self.sinkc, self.sink_d), (self.convw, self.convw_d), (self.convb, self.convb_d),
                         (self.hv, self.hvec_d), (self.snorm, self.snorm_d)):
            kb.dma("sp", dst[:], src[l], bl, writes=[bl])
        kb.op("act", lambda: S_.activation(out=self.na[:], in_=self.hv[:, 32:64], func=AF.Exp), reads=[bl], writes=[bl])
        kb.op("dve", lambda: V.tensor_scalar(out=self.na[:], in0=self.na[:], scalar1=-1.0, scalar2=0.0, op0=ALU.mult, op1=ALU.add),
              reads=[bl], writes=[bl])
        kb.op("dve", lambda: V.memset(self.S[:], 0.0), writes=self.b_S)
        kb.op("dve", lambda: V.memset(self.halo[:], 0.0), writes=self.b_halo)
        t_kT = self.tok(self.o_kT, 5 * K)
        t_vd = self.tok(self.o_vd, 5 * K)
        kb.op("dve", lambda: V.memset(self.kT[:, :, 0:128], 0.0), writes=t_kT)
        kb.op("dve", lambda: V.memset(self.vd[:, 0], 0.0), writes=t_vd)
        self._slab_q, self._slab_rd = "sp", list(self.b_castm[l])
        for t in range(self.NTILES):
            self.mixer_tile(l, t)
        self._slab_q, self._slab_rd = "pool", []
        self.cur = self.yT

    def mixer_tile(self, l, t):
        kb, nc = self.kb, self.nc
        V, S_ = nc.vector, nc.scalar
        K = KB1
        bl = self.b_lay
        nT = self.carve(0, BF16, [128, C, NT])
        t_nT = self.tok(0, 16 * K)
        t_qT = self.tok(self.o_qT, 8 * K)
        t_kT = self.tok(self.o_kT, 5 * K)
        t_vd = self.tok(self.o_vd, 5 * K)
        t_BT = self.tok(self.o_BT, 4 * K)
        t_CT = self.tok(self.o_CT, 4 * K)
        t_dt = self.tok(self.o_dt, 2 * K)
        t_xsT = self.tok(self.o_xsT, 16 * K)
        t_zs = self.tok(self.o_zs, 16 * K)
        t_ynT = self.tok(self.o_ynT, 16 * K)
        t_xdt = self.tok(self.o_xdt, 4 * K)
        t_xdtd = self.tok(self.o_xdtd, 4 * K)
        t_y = self.tok(self.o_y, 8 * K)
        t_ynb = self.tok(self.o_ynb, 4 * K)
        t_Sbf = self.tok(self.o_Sbf, 4 * K)
        t_sm = self.tok(self.o_Btok, 10 * K)
        t_at = self.tok(140 * K, 19 * K)
        win = self.winb[l]
        ps, bps = self.ps, self.b_ps

        pend = [self.load_slab(win, C, 0, 512), self.load_slab(win, C, 512, 512)]
        self.emit_prenorm(1, t, self.o_zs, 0)

        def next_slab():
            i = pend.pop(0)
            return self.wA[i], self.tok(self.o_wA[i], 16 * K)

        def fm_chunk(wb, wt, cc, pb):
            self.mm(ps[pb], [(wb[:, kc, cc * 128:(cc + 1) * 128], nT[:, kc, :]) for kc in range(C)],
                    reads=wt + t_nT, wtoks=[bps[pb]])

        for sl in range(2):
            wb, wt = next_slab()
            for cc in range(4):
                pb = self.rr("p4", 4)
                fm_chunk(wb, wt, cc, pb)
                c = sl * 4 + cc
                kb.op("act", lambda c=c, pb=pb: S_.activation(out=self.qT[:, c, :], in_=ps[pb], func=AF.Copy, scale=0.125),
                      reads=[bps[pb]], writes=t_qT)
            pend.append(self.load_slab(win, C, 1024 if sl == 0 else 1536, 512))
        self.stage("q")
        wb, wt = next_slab()
        for i in range(2):
            pb = self.rr("p4", 4)
            fm_chunk(wb, wt, i, pb)
            kb.op("act", lambda i=i, pb=pb: S_.activation(out=self.kT[0:64, 2 * i, 128:640], in_=ps[pb][0:64, :], func=AF.Copy),
                  reads=[bps[pb]], writes=t_kT)
            kb.op("act", lambda i=i, pb=pb: S_.activation(out=self.kT[64:128, 2 * i + 1, 128:640], in_=ps[pb][64:128, :], func=AF.Copy),
                  reads=[bps[pb]], writes=t_kT)
        self.stage("kcopy")
        for i in range(2):
            kb.dma("sp", self.kT[64:128, 2 * i, 128:640], self.kT[0:64, 2 * i, 128:640], t_kT[0], reads=t_kT, writes=t_kT)
            kb.dma("sp", self.kT[0:64, 2 * i + 1, 128:640], self.kT[64:128, 2 * i + 1, 128:640], t_kT[0], reads=t_kT, writes=t_kT)
        self.stage("kdma")
        for b in range(4):
            pb = self.rr("p4", 4)
            self.mm(ps[pb][:, 0:256], [(nT[:, kc, b * 128:(b + 1) * 128], wb[:, kc, 256:512]) for kc in range(C)],
                    reads=wt + t_nT, wtoks=[bps[pb]])
            kb.op("dve", lambda b=b, pb=pb: V.tensor_copy(
                out=self.vd[:, 1 + b].rearrange("p k (r d) -> p k r d", r=2),
                in_=ps[pb][:, 0:256].rearrange("p (k d) -> p k d", k=4).unsqueeze(2).broadcast_to([128, 4, 2, 64])),
                reads=[bps[pb]], writes=t_vd)
        pend.append(self.load_slab(win, C, 2048, 512))
        self.stage("v")
        for i in range(4):
            wb, wt = next_slab()
            for b in range(4):
                pb = self.rr("p4", 4)
                self.mm(ps[pb], [(nT[:, kc, b * 128:(b + 1) * 128], wb[:, kc, :]) for kc in range(C)],
                        reads=wt + t_nT, wtoks=[bps[pb]])
                kb.op("act", lambda i=i, b=b, pb=pb: S_.activation(out=self.zs[:, b, i * 512:(i + 1) * 512], in_=ps[pb], func=AF.Silu),
                      reads=[bps[pb]], writes=t_zs)
            pend.append(self.load_slab(win, C, [2560, 3072, 3584, 4096][i], 512))
        self.stage("z")
        for i in range(6):
            wb, wt = next_slab()
            for cc in range(4):
                fc = i * 4 + cc
                pb = self.rr("p4", 4)
                fm_chunk(wb, wt, cc, pb)
                u = ps[pb]
                j = self.rr("tmp", 2)
                acc = self.tmp[:, j, :]
                bt = self.b_tmp[j]
                w = lambda k, fc=fc: self.convw[:, fc * 4 + k:fc * 4 + k + 1]
                hl = self.halo[:, fc, :]
                bh = self.b_halo[fc]
                kb.op("act", lambda acc=acc, u=u, fc=fc: S_.activation(out=acc, in_=u, func=AF.Identity, scale=self.convw[:, fc * 4 + 3:fc * 4 + 4],
                                                                       bias=self.convb[:, fc:fc + 1]), reads=[bps[pb], bl], writes=[bt])
                for k, sh in ((2, 1), (1, 2), (0, 3)):
                    kb.op("dve", lambda acc=acc, u=u, k=k, sh=sh, w=w: V.scalar_tensor_tensor(
                        out=acc[:, sh:NT], in0=u[:, 0:NT - sh], scalar=w(k), in1=acc[:, sh:NT], op0=ALU.mult, op1=ALU.add),
                        reads=[bps[pb], bl, bt], writes=[bt])
                    kb.op("dve", lambda acc=acc, hl=hl, k=k, sh=sh, w=w: V.scalar_tensor_tensor(
                        out=acc[:, 0:sh], in0=hl[:, 3 - sh:3], scalar=w(k), in1=acc[:, 0:sh], op0=ALU.mult, op1=ALU.add),
                        reads=[bh, bl, bt], writes=[bt])
                kb.op("dve", lambda hl=hl, u=u: V.tensor_copy(out=hl[:, 0:3], in_=u[:, NT - 3:NT]), reads=[bps[pb]], writes=[bh])
                if fc < 16:
                    dst, dtk = self.xsT[:, fc, :], t_xsT
                elif fc < 20:
                    dst, dtk = self.BT[:, fc - 16, :], t_BT
                else:
                    dst, dtk = self.CT[:, fc - 20, :], t_CT
                kb.op("act", lambda dst=dst, acc=acc: S_.activation(out=dst, in_=acc, func=AF.Silu), reads=[bt], writes=dtk)
            if i < 4:
                pend.append(self.load_slab(win, C, 3584 + (i + 2) * 512, 512))
            elif i == 4:
                pend.append(self.load_slab(win, C, 6688 - 512, 512))
        self.stage("conv")
        wb, wt = next_slab()
        sv = self.sv
        for b in range(4):
            pb = self.rr("p4", 4)
            self.mm(ps[pb][:, 0:32], [(nT[:, kc, b * 128:(b + 1) * 128], wb[:, kc, 480:512]) for kc in range(C)],
                    reads=wt + t_nT, wtoks=[bps[pb]])
            kb.op("dve", lambda pb=pb: V.tensor_tensor(out=sv[:, 0, :], in0=ps[pb][:, 0:32], in1=self.hv[:, 0:32], op=ALU.add),
                  reads=[bps[pb], bl], writes=t_sm)
            kb.op("act", lambda: S_.activation(out=sv[:, 1, :], in_=sv[:, 0, :], func=AF.Exp), reads=t_sm, writes=t_sm)
            kb.op("act", lambda b=b: S_.activation(out=self.dtv[:, b, :], in_=sv[:, 1, :], func=AF.Ln, bias=self.ones[:, 0:1], scale=1.0),
                  reads=t_sm + [self.b_const], writes=t_dt)
        self.stage("inproj")
        kb.op("dve", lambda: V.tensor_copy(out=self.Sbf[:], in_=self.S[:]), reads=self.b_S, writes=t_Sbf)
        psx = self.psbf(0, 2)
        psB = self.psbf(2)
        tri, ones, nones = self.tri, self.ones, self.nones
        for b in range(4):
            blk = slice(b * 128, (b + 1) * 128)
            dt_ = self.dtv[:, b, :]
            dA, acs, eacs, dd, dtd, cdec = sv[:, 2, :], sv[:, 3, :], sv[:, 4, :], sv[:, 5, :], sv[:, 6, :], sv[:, 7, :]
            kb.op("dve", lambda dt_=dt_: V.tensor_tensor(out=dA, in0=dt_, in1=self.na[:], op=ALU.mult), reads=t_dt + [bl], writes=t_sm)
            self.mm(ps[3][:, 0:32], [(tri, dA)], reads=t_sm + [self.b_const], wtoks=[bps[3]])
            self.mm(ps[3][:, 32:64], [(ones, dA)], reads=t_sm + [self.b_const], wtoks=[bps[3]])
            kb.op("dve", lambda: V.tensor_copy(out=acs, in_=ps[3][:, 0:32]), reads=[bps[3]], writes=t_sm)
            kb.op("act", lambda: S_.activation(out=eacs, in_=acs, func=AF.Exp), reads=t_sm, writes=t_sm)
            kb.op("dve", lambda: V.tensor_tensor(out=dd, in0=ps[3][:, 32:64], in1=acs, op=ALU.subtract), reads=[bps[3]] + t_sm, writes=t_sm)
            kb.op("act", lambda: S_.activation(out=dd, in_=dd, func=AF.Exp), reads=t_sm, writes=t_sm)
            kb.op("dve", lambda dt_=dt_: V.tensor_tensor(out=dtd, in0=dt_, in1=dd, op=ALU.mult), reads=t_dt + t_sm, writes=t_sm)
            kb.op("act", lambda: S_.activation(out=cdec, in_=ps[3][:, 32:64], func=AF.Exp), reads=[bps[3]], writes=t_sm)

            def tr_x(b=b):
                ins = None
                for c in range(C):
                    ins = nc.tensor.transpose(out=psx[:, c * 128:(c + 1) * 128], in_=self.xsT[:, c, b * 128:(b + 1) * 128], identity=self.ident_bf)
                return ins
            kb.op("pe", tr_x, reads=t_xsT + [self.b_const], writes=[bps[0], bps[1]])
            px3 = psx.rearrange("p (h d) -> p h d", h=32)
            kb.op("dve", lambda dt_=dt_: V.tensor_tensor(out=self.xdt[:].rearrange("p (h d) -> p h d", h=32), in0=px3, in1=bc_last(dt_, 64), op=ALU.mult),
                  reads=[bps[0], bps[1]] + t_dt, writes=t_xdt)
            kb.op("dve", lambda: V.tensor_tensor(out=self.xdtd[:].rearrange("p (h d) -> p h d", h=32), in0=px3, in1=bc_last(dtd, 64), op=ALU.mult),
                  reads=[bps[0], bps[1]] + t_sm, writes=t_xdtd)
            kb.op("dve", lambda: V.tensor_tensor(out=self.y[:].rearrange("p (h d) -> p h d", h=32), in0=px3, in1=bc_last(self.hv[:, 64:96], 64), op=ALU.mult),
                  reads=[bps[0], bps[1], bl], writes=t_y)

            def tr_b(b=b):
                ins = None
                for g in range(4):
                    ins = nc.tensor.transpose(out=psB[:, g * 128:(g + 1) * 128], in_=self.BT[:, g, b * 128:(b + 1) * 128], identity=self.ident_bf)
                return ins
            kb.op("pe", tr_b, reads=t_BT + [self.b_const], writes=[bps[2]])
            kb.op("act", lambda: S_.activation(out=self.Btok[:].rearrange("p g n -> p (g n)"), in_=psB[:, 0:512], func=AF.Copy),
                  reads=[bps[2]], writes=t_sm)
            for g in range(4):
                self.mm(ps[4][:, g * 128:(g + 1) * 128], [(self.BT[:, g, blk], self.CT[:, g, blk])], reads=t_BT + t_CT, wtoks=[bps[4]])
            kb.op("dve", lambda: V.tensor_tensor(out=self.CBm[:], in0=ps[4].rearrange("p (g n) -> p g n", g=4), in1=bc_mid(tri, 4), op=ALU.mult),
                  reads=[bps[4], self.b_const], writes=t_sm)
            for g in range(4):
                gs = slice(g * 512, (g + 1) * 512)
                for r in range(8):
                    h = 8 * g + r
                    i = self.rr("hd", 4)
                    q = self.rr("pD", 8)
                    pD = 5 + q // 4
                    dcol = slice((q % 4) * 128, (q % 4 + 1) * 128)
                    kb.op("dve", lambda i=i, h=h: V.tensor_scalar(out=self.dat[:, i, :], in0=tri, scalar1=dA[:, h:h + 1], scalar2=0.0,
                                                                  op0=ALU.mult, op1=ALU.add), reads=t_sm + [self.b_const], writes=t_sm)
                    self.mm(ps[pD][:, dcol], [(ones, self.dat[:, i, :]), (self.dat[:, i, :], nones)], reads=t_sm + [self.b_const], wtoks=[bps[pD]])
                    kb.op("dve", lambda i=i, pD=pD, dcol=dcol: V.tensor_scalar(out=self.Dm[:, i, :], in0=ps[pD][:, dcol], scalar1=0.0, scalar2=0.0,
                                                                              op0=ALU.min, op1=ALU.add), reads=[bps[pD]], writes=t_sm)
                    kb.op("act", lambda i=i: S_.activation(out=self.Dm[:, i, :], in_=self.Dm[:, i, :], func=AF.Exp), reads=t_sm, writes=t_sm)
                    kb.op("dve", lambda i=i, g=g: V.tensor_tensor(out=self.MT[:, i, :], in0=self.Dm[:, i, :], in1=self.CBm[:, g, :], op=ALU.mult),
                          reads=t_sm, writes=t_sm)
                    self.mm(ps[7][:, r * 64:(r + 1) * 64], [(self.MT[:, i, :], self.xdt[:, h * 64:(h + 1) * 64])], reads=t_sm + t_xdt, wtoks=[bps[7]])
                yo = g % 2
                self.mm(ps[yo], [(self.CT[:, g, blk], self.Sbf[:, gs])], reads=t_CT + t_Sbf, wtoks=[bps[yo]])
                j = self.rr("tmp", 2)
                kb.op("dve", lambda j=j, yo=yo, g=g: V.tensor_tensor(
                    out=self.tmp[:, j, :].rearrange("p (h d) -> p h d", h=8), in0=ps[yo].rearrange("p (h d) -> p h d", h=8),
                    in1=bc_last(eacs[:, 8 * g:8 * g + 8], 64), op=ALU.mult), reads=[bps[yo]] + t_sm, writes=[self.b_tmp[j]])
                kb.op("dve", lambda j=j, gs=gs: V.tensor_tensor(out=self.y[:, gs], in0=self.y[:, gs], in1=self.tmp[:, j, :], op=ALU.add),
                      reads=t_y + [self.b_tmp[j]], writes=t_y)
                kb.op("dve", lambda gs=gs: V.tensor_tensor(out=self.y[:, gs], in0=self.y[:, gs], in1=ps[7], op=ALU.add),
                      reads=t_y + [bps[7]], writes=t_y)
                self.mm(ps[2], [(self.Btok[:, g, :], self.xdtd[:, gs])], reads=t_sm + t_xdtd, wtoks=[bps[2]])
                kb.op("dve", lambda g=g, gs=gs: V.tensor_tensor(
                    out=self.S[:, gs].rearrange("p (h d) -> p h d", h=8), in0=self.S[:, gs].rearrange("p (h d) -> p h d", h=8),
                    in1=bc_last(cdec[:, 8 * g:8 * g + 8], 64), op=ALU.mult), reads=[self.b_S[g]] + t_sm, writes=[self.b_S[g]])
                kb.op("dve", lambda gs=gs: V.tensor_tensor(out=self.S[:, gs], in0=self.S[:, gs], in1=ps[2], op=ALU.add),
                      reads=[self.b_S[g], bps[2]], writes=[self.b_S[g]])
                kb.op("act", lambda gs=gs: S_.activation(out=self.Sbf[:, gs], in_=self.S[:, gs], func=AF.Copy), reads=[self.b_S[g]], writes=t_Sbf)
            kb.op("dve", lambda b=b: V.tensor_tensor(out=self.y[:], in0=self.y[:], in1=self.zs[:, b, :], op=ALU.mult), reads=t_y + t_zs, writes=t_y)
            ssq, rg = sv[:, 8, 0:4], sv[:, 9, 0:4]
            kb.op("dve", lambda: V.memset(ssq, 0.0), writes=t_sm)
            for g in range(4):
                j = self.rr("sg", 2)
                kb.op("act", lambda g=g, j=j: S_.activation(out=self.sg[:, j, :], in_=self.y[:, g * 512:(g + 1) * 512], func=AF.Square,
                                                            accum_out=ssq[:, g:g + 1]), reads=t_y + t_sm, writes=[self.b_sg[j]] + t_sm)
            kb.op("act", lambda: S_.activation(out=rg, in_=ssq, func=AF.Sqrt, scale=1.0 / 512, bias=self.epsc[:, 0:1]), reads=t_sm + [self.b_const], writes=t_sm)
            kb.op("dve", lambda: V.reciprocal(out=rg, in_=rg), reads=t_sm, writes=t_sm)
            for g in range(4):
                gs = slice(g * 512, (g + 1) * 512)
                kb.op("dve", lambda g=g, gs=gs: V.scalar_tensor_tensor(out=self.ynb[:, gs], in0=self.y[:, gs], scalar=rg[:, g:g + 1], in1=self.snorm[:, gs],
                                                                       op0=ALU.mult, op1=ALU.mult), reads=t_y + t_sm + [bl], writes=t_ynb)
            psY = self.psbf(5, 2)

            def tr_y():
                ins = None
                for c in range(C):
                    ins = nc.tensor.transpose(out=psY[:, c * 128:(c + 1) * 128], in_=self.ynb[:, c * 128:(c + 1) * 128], identity=self.ident_bf)
                return ins
            kb.op("pe", tr_y, reads=t_ynb + [self.b_const], writes=[bps[5], bps[6]])
            kb.op("act", lambda blk=blk: S_.activation(out=self.ynT[:, 0:8, blk], in_=psY[:, 0:1024].rearrange("p (c n) -> p c n", c=8), func=AF.Copy),
                  reads=[bps[5]], writes=t_ynT)
            kb.op("dve", lambda blk=blk: V.tensor_copy(out=self.ynT[:, 8:16, blk], in_=psY[:, 1024:2048].rearrange("p (c n) -> p c n", c=8)),
                  reads=[bps[6]], writes=t_ynT)
        self.stage("ssd")
        dist = self.acon[:, 0:256]
        for b in range(4):
            blk = slice(b * 128, (b + 1) * 128)
            mask = self.acon[:, 512:768] if (t == 0 and b == 0) else self.acon[:, 256:512]
            for g in range(4):
                k2 = self.rr("pS", 2)
                psS = self.psall[:, 2 * k2:2 * k2 + 2, :].rearrange("p a (h k) -> p (a h) k", h=2)
                bS = [bps[2 * k2], bps[2 * k2 + 1]]

                def sc(b=b, g=g, psS=psS):
                    ins = None
                    for r in range(4):
                        hf = slice((r % 2) * 64, (r % 2) * 64 + 64)
                        ch = 2 * g + r // 2
                        ins = nc.tensor.matmul(psS[:, (r % 2) * 2 + r // 2, :], lhsT=self.qT[hf, ch, b * 128:(b + 1) * 128],
                                               rhs=self.kT[hf, g, b * 128:b * 128 + 256], start=True, stop=True)
                    return ins
                kb.op("pe", sc, reads=t_qT + t_kT, writes=bS)
                i = self.rr("ssb", 2)
                ssb = self.ssb[i]
                for r in range(4):
                    kb.op("dve", lambda r=r, ssb=ssb, psS=psS, g=g: V.scalar_tensor_tensor(
                        out=ssb[:, r, 0:256], in0=dist, scalar=-SLOPES[4 * g + r], in1=psS[:, (r % 2) * 2 + r // 2, :], op0=ALU.mult, op1=ALU.add),
                        reads=bS + [self.b_const], writes=t_at)
                kb.op("dve", lambda ssb=ssb, mask=mask: V.tensor_tensor(out=ssb[:, :, 0:256], in0=ssb[:, :, 0:256], in1=bc_mid(mask, 4), op=ALU.add),
                      reads=t_at + [self.b_const], writes=t_at)
                kb.op("dve", lambda ssb=ssb, g=g: V.tensor_copy(out=ssb[:, :, 256:257], in_=self.sinkc[:, 4 * g:4 * g + 4].unsqueeze(2)),
                      reads=[bl], writes=t_at)
                mx, sm, rs = self.ast[:, 0, 0:4], self.ast[:, 0, 4:8], self.ast[:, 0, 8:12]
                kb.op("dve", lambda ssb=ssb: V.tensor_reduce(out=mx, in_=ssb[:, :, 0:257], axis=AX.X, op=ALU.max), reads=t_at, writes=t_at)
                kb.op("dve", lambda ssb=ssb: V.tensor_tensor(out=ssb[:, :, 0:257], in0=ssb[:, :, 0:257], in1=bc_last(mx, 257), op=ALU.subtract),
                      reads=t_at, writes=t_at)
                kb.op("act", lambda ssb=ssb: S_.activation(out=ssb[:, :, 0:257], in_=ssb[:, :, 0:257], func=AF.Exp), reads=t_at, writes=t_at)
                kb.op("dve", lambda ssb=ssb: V.tensor_reduce(out=sm, in_=ssb[:, :, 0:257], axis=AX.X, op=ALU.add), reads=t_at, writes=t_at)
                kb.op("dve", lambda: V.reciprocal(out=rs, in_=sm), reads=t_at, writes=t_at)
                pp = self.pp[i]
                kb.op("dve", lambda ssb=ssb, pp=pp: V.tensor_tensor(out=pp[:], in0=ssb[:, :, 0:256], in1=bc_last(rs, 256), op=ALU.mult),
                      reads=t_at, writes=t_at)
                pbT = 4 + self.rr("pT", 2)
                psT = self.psbf(pbT)

                def trp(pp=pp, psT=psT):
                    ins = None
                    for r in range(4):
                        for k2_ in range(2):
                            m = r * 2 + k2_
                            ins = nc.tensor.transpose(out=psT[:, m * 128:(m + 1) * 128], in_=pp[:, r, k2_ * 128:(k2_ + 1) * 128], identity=self.ident_bf)
                    return ins
                kb.op("pe", trp, reads=t_at + [self.b_const], writes=[bps[pbT]])
                pT = self.pT[i]
                kb.op("act", lambda pT=pT, psT=psT: S_.activation(out=pT[:].rearrange("p m n -> p (m n)"), in_=psT[:, 0:1024], func=AF.Copy),
                      reads=[bps[pbT]], writes=t_at)
                pbO = 6 + self.rr("pO", 2)

                def pv(b=b, g=g, pT=pT, pbO=pbO):
                    ins = None
                    for r in range(4):
                        nc.tensor.matmul(ps[pbO][:, r * 128:(r + 1) * 128], lhsT=self.vd[:, b, g, :], rhs=pT[:, 2 * r, :], start=True, stop=False)
                        ins = nc.tensor.matmul(ps[pbO][:, r * 128:(r + 1) * 128], lhsT=self.vd[:, b + 1, g, :], rhs=pT[:, 2 * r + 1, :], start=False, stop=True)
                    return ins
                kb.op("pe", pv, reads=t_at + t_vd, writes=[bps[pbO]])
                pO = ps[pbO].rearrange("p (r n) -> p r n", r=4)
                kb.op("act", lambda g=g, blk=blk, pO=pO: S_.activation(out=self.qT[0:64, 2 * g:2 * g + 2, blk], in_=pO[0:64, 0::2, :], func=AF.Copy),
                      reads=[bps[pbO]], writes=t_qT)
                kb.op("dve", lambda g=g, blk=blk, pO=pO: V.tensor_copy(out=self.qT[64:128, 2 * g:2 * g + 2, blk], in_=pO[64:128, 1::2, :]),
                      reads=[bps[pbO]], writes=t_qT)
        self.stage("attn")
        kb.op("act", lambda: S_.activation(out=self.kT[:, :, 0:128], in_=self.kT[:, :, 512:640], func=AF.Copy), reads=t_kT, writes=t_kT)
        kb.op("act", lambda: S_.activation(out=self.vd[:, 0].rearrange("p k n -> p (k n)"), in_=self.vd[:, 4].rearrange("p k n -> p (k n)"), func=AF.Copy),
              reads=t_vd, writes=t_vd)
        oT = self.qT
        mg = self.xsT
        for ph in range(2):
            for sl in range(4):
                if ph == 0:
                    i1 = self.load_slab(self.waob[l], 8, sl * 512, 512)
                    i2 = self.load_slab(win, C, 6688 + sl * 512, 512)
                    kcs, act, t_act = 8, oT, t_qT
                else:
                    i1 = self.load_slab(self.wsob[l], C, sl * 512, 512)
                    i2 = self.load_slab(win, C, 8736 + sl * 512, 512)
                    kcs, act, t_act = C, self.ynT, t_ynT
                w1, w1t = self.wA[i1], self.tok(self.o_wA[i1], 16 * K)
                w2, w2t = self.wA[i2], self.tok(self.o_wA[i2], 16 * K)
                for cc in range(4):
                    c = sl * 4 + cc
                    pa = self.rr("pa", 2)
                    pg = 2 + self.rr("pg", 2)
                    self.mm(ps[pa], [(w1[:, kc, cc * 128:(cc + 1) * 128], act[:, kc, :]) for kc in range(kcs)], reads=w1t + t_act, wtoks=[bps[pa]])
                    self.mm(ps[pg], [(w2[:, kc, cc * 128:(cc + 1) * 128], nT[:, kc, :]) for kc in range(C)], reads=w2t + t_nT, wtoks=[bps[pg]])
                    j = self.rr("sg", 2)
                    kb.op("act", lambda j=j, pg=pg: S_.activation(out=self.sg[:, j, :], in_=ps[pg], func=AF.Sigmoid), reads=[bps[pg]], writes=[self.b_sg[j]])
                    mc, mct = self.mrg(c)
                    if ph == 0:
                        kb.op("dve", lambda j=j, pa=pa, mc=mc: V.tensor_tensor(out=mc, in0=self.sg[:, j, :], in1=ps[pa], op=ALU.mult),
                              reads=[self.b_sg[j], bps[pa]], writes=mct)
                    else:
                        kb.op("dve", lambda j=j, pa=pa: V.tensor_tensor(out=self.sg[:, j, :], in0=self.sg[:, j, :], in1=ps[pa], op=ALU.mult),
                              reads=[self.b_sg[j], bps[pa]], writes=[self.b_sg[j]])
                        kb.op("dve", lambda j=j, mc=mc, c=c: V.tensor_tensor(out=mg[:, c, :], in0=self.sg[:, j, :], in1=mc, op=ALU.add),
                              reads=[self.b_sg[j]] + mct, writes=t_xsT)
        self.stage("merge")
        tT = self.carve(self.o_zs, F32, [128, C, NT])
        pend = [self.load_slab(self.wob[l], C, 0, 512)]
        for sl in range(4):
            if sl + 1 < 4:
                pend.append(self.load_slab(self.wob[l], C, (sl + 1) * 512, 512))
            wb, wt = next_slab()
            for cc in range(4):
                c = sl * 4 + cc
                pf = 4 + self.rr("pf", 2)
                self.mm(ps[pf], [(wb[:, kc, cc * 128:(cc + 1) * 128], mg[:, kc, :]) for kc in range(C)], reads=wt + t_xsT, wtoks=[bps[pf]])
                kb.op("dve", lambda c=c, pf=pf: V.tensor_copy(out=tT[:, c, :], in_=ps[pf]), reads=[bps[pf]], writes=self.tok(self.o_zs + c * 2048, 2048))
                i = self.rr("sqf", 2)
                kb.op("act", lambda i=i, pf=pf: S_.activation(out=self.sqf[:, i, :], in_=ps[pf], func=AF.Square), reads=[bps[pf]], writes=[self.b_sqf[i]])
                kb.op("pe", lambda i=i, c=c: nc.tensor.matmul(ps[7], lhsT=self.ones_bf, rhs=self.sqf[:, i, :], start=(c == 0), stop=(c == C - 1)),
                      reads=[self.b_sqf[i], self.b_const], writes=[bps[7]])
        self.emit_postnorm_store(1, t, self.o_zs, 7)


def _fm(v):
    v = np.asarray(v, np.float32)
    k = v.shape[-1] // 128
    return np.ascontiguousarray(v.reshape(k, 128).T)


def _consts():
    f32 = np.float32
    tri = np.triu(np.ones((128, 128), f32))
    cm = np.concatenate([tri, np.eye(128, dtype=f32), np.ones((128, 128), f32), -np.ones((128, 128), f32)], 1)
    q = np.arange(128)[:, None]
    k = np.arange(256)[None, :]
    dist = (128 + q - k).astype(f32)
    valid = (dist >= 0) & (dist < 128)
    mask = np.where(valid, 0.0, NEG).astype(f32)
    mask0 = np.where(valid & (k >= 128), 0.0, NEG).astype(f32)
    acon = np.concatenate([dist, mask, mask0], 1)
    return np.ascontiguousarray(cm), np.ascontiguousarray(acon)


def make_in_map(x_b, c_b, w, L):
    f32 = np.float32
    cm, acon = _consts()
    bc = lambda v: np.ascontiguousarray(np.broadcast_to(np.asarray(v, f32)[None, :], (128, v.shape[-1])))
    m = dict(
        xT=np.ascontiguousarray(np.asarray(x_b, f32).T), cT=_fm(c_b),
        w_mod=w["w_mod"], b_modT=np.stack([_fm(w["b_mod"][l]) for l in range(L)]),
        npreT=np.stack([_fm(w["norm_pre"][l].reshape(-1)) for l in range(L)]),
        npostT=np.stack([_fm(w["norm_post"][l].reshape(-1)) for l in range(L)]),
        w_gate=w["w_ffn_gate"], w_up=w["w_ffn_up"], w_down=w["w_ffn_down"], w_in=w["w_in"],
        w_ao=w["w_attn_out"], w_so=w["w_ssm_out"], w_o=w["w_out"],
        acon=acon, cmats=cm,
        sinkc=np.stack([bc(w["attn_sinks"][l]) for l in range(L)]),
        convwT=np.stack([np.ascontiguousarray(w["conv_w"][l].reshape(4, 24, 128).transpose(2, 1, 0).reshape(128, 96)) for l in range(L)]),
        convbT=np.stack([_fm(w["conv_b"][l]) for l in range(L)]),
        hvec=np.stack([bc(np.concatenate([w["dt_bias"][l], w["a_log"][l], w["d_skip"][l]])) for l in range(L)]),
        snorm=np.stack([bc(w["ssm_norm"][l]) for l in range(L)]),
    )
    return {k: np.ascontiguousarray(v, dtype=f32) for k, v in m.items()}


_PROG = {}


def kernel(**inputs):
    x = np.asarray(inputs["x"], np.float32)
    B, T, _ = x.shape
    L = inputs["w_mod"].shape[0]
    w = {k: np.asarray(v, np.float32) for k, v in inputs.items() if k not in ("x", "c")}
    key = (T, L)
    if key not in _PROG:
        _PROG[key] = Prog(T, L)
    p = _PROG[key]
    in_maps = [make_in_map(x[b], inputs["c"][b], w, L) for b in range(B)]
    res = run_bass_kernel_spmd(p.nc, in_maps, core_ids=list(range(B)))
    out = np.stack([np.ascontiguousarray(res.results[b]["yT"].T) for b in range(B)])
    return out.astype(np.float32)
```
